# Optimizing a Trainium2 kernel written in Bass

```python
import jax, jax.numpy as jnp
from jax import lax
import numpy as np

D_MODEL = 4096
BATCH = 4
SEQ = 4096
DEPTH = 2

N_MIXERS = 2
N_ATTN_LAYERS = (DEPTH + 1) // 2
N_CONV_LAYERS = DEPTH // 2
HEAD_DIM = 128
N_Q_HEADS = D_MODEL // HEAD_DIM
N_KV_HEADS = N_Q_HEADS // 4
GQA_GROUP = N_Q_HEADS // N_KV_HEADS
QKV_DIM = (N_Q_HEADS + 2 * N_KV_HEADS) * HEAD_DIM
Q_BLOCK = 128
ROPE_THETA = 10000.0
GRID_W = 64
AXIS_DIM = HEAD_DIM // 2
AXIS_FREQS = AXIS_DIM // 2
CONV_WIDTH = 3
CONV_DIM = D_MODEL
PEER_HEADS = 8
N_KEYS = 128
N_EXPERTS = N_KEYS * N_KEYS
PEER_QUERY_DIM = 256
PEER_HALF = PEER_QUERY_DIM // 2
PEER_TOPK = 16
PEER_SLOTS = PEER_HEADS * PEER_TOPK
PEER_TOKEN_BLOCK = 128
EPS = 1e-6

kernel_name = "hybrid_gqa_shortconv_peer_encoder"


def rms_norm(x, g):
    xf = x.astype(jnp.float32)
    y = xf * lax.rsqrt(jnp.mean(xf * xf, axis=-1, keepdims=True) + EPS)
    return (y * g.astype(jnp.float32)).astype(x.dtype)


def axial_rope_tables(seq_len):
    rows = seq_len // GRID_W
    row_idx = jnp.broadcast_to(jnp.arange(rows, dtype=jnp.float32)[:, None], (rows, GRID_W)).reshape(seq_len)
    col_idx = jnp.broadcast_to(jnp.arange(GRID_W, dtype=jnp.float32)[None, :], (rows, GRID_W)).reshape(seq_len)
    inv_freq = ROPE_THETA ** (-jnp.arange(0, AXIS_DIM, 2, dtype=jnp.float32) / AXIS_DIM)
    ang = jnp.stack([row_idx[:, None] * inv_freq, col_idx[:, None] * inv_freq], axis=1)
    return jnp.cos(ang), jnp.sin(ang)


def apply_axial_rope(x, cos, sin):
    lead = x.shape[:-1]
    xr = x.reshape(lead + (2, 2, AXIS_FREQS))
    x1, x2 = xr[..., 0, :], xr[..., 1, :]
    bshape = (cos.shape[0],) + (1,) * (x.ndim - 3) + (2, AXIS_FREQS)
    c = cos.reshape(bshape).astype(x.dtype)
    s = sin.reshape(bshape).astype(x.dtype)
    out = jnp.stack([x1 * c - x2 * s, x2 * c + x1 * s], axis=-2)
    return out.reshape(x.shape)


def attention_mixer(h, w_qkv, w_o, q_gain, k_gain):
    B, S, _ = h.shape
    qkv = h @ w_qkv
    q, k, v = jnp.split(qkv, [N_Q_HEADS * HEAD_DIM, (N_Q_HEADS + N_KV_HEADS) * HEAD_DIM], axis=-1)
    q = q.reshape(B, S, N_KV_HEADS, GQA_GROUP, HEAD_DIM)
    k = k.reshape(B, S, N_KV_HEADS, HEAD_DIM)
    v = v.reshape(B, S, N_KV_HEADS, HEAD_DIM)
    q = rms_norm(q, q_gain)
    k = rms_norm(k, k_gain)
    cos, sin = axial_rope_tables(S)
    q = apply_axial_rope(q, cos, sin)
    k = apply_axial_rope(k, cos, sin)
    scale = HEAD_DIM ** -0.5
    n_blk = S // Q_BLOCK
    qb = jnp.moveaxis(q.reshape(B, n_blk, Q_BLOCK, N_KV_HEADS, GQA_GROUP, HEAD_DIM), 1, 0)

    def one_block(qblk):
        s = jnp.einsum('bqkgd,bskd->bkgqs', qblk, k).astype(jnp.float32) * scale
        p = jax.nn.softmax(s, axis=-1).astype(v.dtype)
        return jnp.einsum('bkgqs,bskd->bqkgd', p, v)

    o = lax.map(one_block, qb)
    o = jnp.moveaxis(o, 0, 1).reshape(B, S, N_Q_HEADS * HEAD_DIM)
    return o @ w_o


def short_conv_mixer(h, w_in, conv_w, conv_b, w_out):
    S = h.shape[1]
    bcx = h @ w_in
    b_gate, c_gate, xin = jnp.split(bcx, 3, axis=-1)
    u = c_gate * xin
    up = jnp.pad(u, ((0, 0), (1, 1), (0, 0)))
    conv = (up[:, 0:S] * conv_w[0] + up[:, 1:S + 1] * conv_w[1]
            + up[:, 2:S + 2] * conv_w[2] + conv_b)
    return (b_gate * conv) @ w_out


def peer_ffn(h, w_query, sub_keys, expert_u, expert_v):
    B, S, D = h.shape
    T = B * S
    xt = h.reshape(T, D)
    q = (xt @ w_query).reshape(T, PEER_HEADS, 2, PEER_HALF)
    scores = jnp.einsum('thpd,hpnd->thpn', q, sub_keys).astype(jnp.float32)
    top_s, top_i = lax.top_k(scores, PEER_TOPK)
    cand_s = top_s[:, :, 0, :, None] + top_s[:, :, 1, None, :]
    cand_i = top_i[:, :, 0, :, None] * N_KEYS + top_i[:, :, 1, None, :]
    cand_s = cand_s.reshape(T, PEER_HEADS, PEER_TOPK * PEER_TOPK)
    cand_i = cand_i.reshape(T, PEER_HEADS, PEER_TOPK * PEER_TOPK)
    best_s, pos = lax.top_k(cand_s, PEER_TOPK)
    expert_idx = jnp.take_along_axis(cand_i, pos, axis=-1).reshape(T, PEER_SLOTS)
    gates = jax.nn.softmax(best_s, axis=-1).astype(h.dtype).reshape(T, PEER_SLOTS)
    n_blk = T // PEER_TOKEN_BLOCK

    def one_block(args):
        xb, idx, g = args
        u = expert_u[idx]
        act = jax.nn.gelu(jnp.einsum('td,tkd->tk', xb, u), approximate=False)
        v = expert_v[idx]
        return jnp.einsum('tk,tkd->td', g * act, v)

    out = lax.map(one_block, (xt.reshape(n_blk, PEER_TOKEN_BLOCK, D),
                              expert_idx.reshape(n_blk, PEER_TOKEN_BLOCK, PEER_SLOTS),
                              gates.reshape(n_blk, PEER_TOKEN_BLOCK, PEER_SLOTS)))
    return out.reshape(B, S, D)


def setup_inputs(seed: int = 0) -> dict:
    key = jax.random.key(seed)
    ks = jax.random.split(key, 16)
    f32 = jnp.float32
    D = D_MODEL
    nrm = lambda k, shape, s: jax.random.normal(k, shape, f32) * s
    return {
        "x": nrm(ks[0], (BATCH, SEQ, D), 1.0),
        "mixer_norm_g": 1.0 + nrm(ks[1], (DEPTH, D), 0.01),
        "ffn_norm_g": 1.0 + nrm(ks[2], (DEPTH, D), 0.01),
        "attn_w_qkv": nrm(ks[3], (N_ATTN_LAYERS, D, QKV_DIM), D ** -0.5),
        "attn_w_o": nrm(ks[4], (N_ATTN_LAYERS, N_Q_HEADS * HEAD_DIM, D), (N_Q_HEADS * HEAD_DIM) ** -0.5),
        "attn_q_gain": 1.0 + nrm(ks[5], (N_ATTN_LAYERS, HEAD_DIM), 0.01),
        "attn_k_gain": 1.0 + nrm(ks[6], (N_ATTN_LAYERS, HEAD_DIM), 0.01),
        "conv_w_in": nrm(ks[7], (N_CONV_LAYERS, D, 3 * CONV_DIM), D ** -0.5),
        "conv_w": nrm(ks[8], (N_CONV_LAYERS, CONV_WIDTH, CONV_DIM), CONV_WIDTH ** -0.5),
        "conv_b": nrm(ks[9], (N_CONV_LAYERS, CONV_DIM), 0.01),
        "conv_w_out": nrm(ks[10], (N_CONV_LAYERS, CONV_DIM, D), CONV_DIM ** -0.5),
        "peer_w_query": nrm(ks[11], (DEPTH, D, PEER_HEADS * PEER_QUERY_DIM), D ** -0.5),
        "peer_sub_keys": nrm(ks[12], (DEPTH, PEER_HEADS, 2, N_KEYS, PEER_HALF), PEER_HALF ** -0.5),
        "peer_u": nrm(ks[13], (DEPTH, N_EXPERTS, D), D ** -0.5),
        "peer_v": nrm(ks[14], (DEPTH, N_EXPERTS, D), PEER_SLOTS ** -0.5),
    }


def reference(x, mixer_norm_g, ffn_norm_g, attn_w_qkv, attn_w_o, attn_q_gain, attn_k_gain,
              conv_w_in, conv_w, conv_b, conv_w_out, peer_w_query, peer_sub_keys, peer_u, peer_v):
    h = x
    for layer in range(DEPTH):
        hn = rms_norm(h, mixer_norm_g[layer])
        j = layer // N_MIXERS
        if layer % N_MIXERS == 0:
            mix = attention_mixer(hn, attn_w_qkv[j], attn_w_o[j], attn_q_gain[j], attn_k_gain[j])
        else:
            mix = short_conv_mixer(hn, conv_w_in[j], conv_w[j], conv_b[j], conv_w_out[j])
        h = h + mix
        h = h + peer_ffn(rms_norm(h, ffn_norm_g[layer]), peer_w_query[layer], peer_sub_keys[layer],
                         peer_u[layer], peer_v[layer])
    return h
```

```python
from contextlib import ExitStack
import numpy as np
import concourse.bass as bass
import concourse.mybir as mybir
from concourse.bass_utils import run_bass_kernel_spmd

F32 = mybir.dt.float32
BF16 = mybir.dt.bfloat16
ALU = mybir.AluOpType
AF = mybir.ActivationFunctionType
AX = mybir.AxisListType

ENGS = ["pe", "act", "dve", "pool", "sp"]
DMA_RING = 8

D = 4096
T = 4096
NT = T // 128
TB = 512
NB = T // TB
NE = 16384
EPS = 1e-6
NCORES = 4


class Buf:
    __slots__ = ("name", "last_w", "readers")

    def __init__(self, name=""):
        self.name = name
        self.last_w = None
        self.readers = {}


class Prog:
    def __init__(self):
        self.ops = {e: [] for e in ENGS}
        self.cnt = {e: 0 for e in ENGS}
        self.seen = {e: {} for e in ENGS}
        self.dma_n = {e: 0 for e in ENGS}
        self.last = {}
        self.semkeys = set()
        self.pending = {}

    def _deps(self, reads, writes):
        deps = {}

        def add(k, v):
            if deps.get(k, 0) < v:
                deps[k] = v
        for b in reads:
            if b.last_w is not None:
                add(*b.last_w)
        for b in writes:
            if b.last_w is not None:
                add(*b.last_w)
            for k, v in b.readers.items():
                add(k, v)
        return deps

    def _commit(self, ev, reads, writes):
        k, v = ev
        self.last[k] = v
        for b in reads:
            if b.readers.get(k, 0) < v:
                b.readers[k] = v
        for b in writes:
            b.last_w = ev
            b.readers = {}

    def _waits(self, eng, deps):
        waits = []
        seen = self.seen[eng]
        for k, v in deps.items():
            if k == eng and eng == "pe":
                continue
            if seen.get(k, 0) >= v:
                continue
            seen[k] = v
            waits.append((k, v))
        return waits

    def op(self, eng, fn, reads=(), writes=(), sig=True):
        deps = self._deps(reads, writes)
        waits = self._waits(eng, deps)
        if not sig:
            self.ops[eng].append((waits, fn, None, 0))
            self.pending.setdefault(eng, []).append((tuple(reads), tuple(writes)))
            return
        self.cnt[eng] += 1
        ev = (eng, self.cnt[eng])
        self.semkeys.add(eng)
        self.ops[eng].append((waits, fn, eng, 1))
        for r_, w_ in self.pending.pop(eng, []):
            self._commit(ev, r_, w_)
        self._commit(ev, reads, writes)

    def dma(self, q, fn, reads=(), writes=()):
        deps = self._deps(reads, writes)
        n = self.dma_n[q]
        self.dma_n[q] += 1
        slot, rnd = n % DMA_RING, n // DMA_RING
        key = ("dma", q, slot)
        self.semkeys.add(key)
        if rnd > 0 and deps.get(key, 0) < 16 * rnd:
            deps[key] = 16 * rnd
        waits = self._waits(q, deps)
        ev = (key, 16 * (rnd + 1))
        self.ops[q].append((waits, fn, key, 16))
        self._commit(ev, reads, writes)

    def barrier(self):
        assert not any(self.pending.values()), "silent op without a following signalling op"
        for e in ENGS:
            waits = self._waits(e, dict(self.last))
            if waits:
                self.ops[e].append((waits, None, None, 0))

    def emit(self, nc, stack):
        sems = {}
        for k in sorted(self.semkeys, key=str):
            nm = "s_" + (k if isinstance(k, str) else "_".join(str(x) for x in k))
            sems[k] = stack.enter_context(nc.semaphore(nm))
        block = stack.enter_context(nc.Block())

        def run(name):
            def body(eng):
                for waits, fn, key, inc in self.ops[name]:
                    for k, v in waits:
                        eng.wait_ge(sems[k], v)
                    if fn is not None:
                        ins = fn(eng)
                        if inc:
                            ins.then_inc(sems[key], inc)
            return body
        m = {"pe": block.tensor, "act": block.scalar, "dve": block.vector,
             "pool": block.gpsimd, "sp": block.sync}
        for e in ENGS:
            if self.ops[e]:
                m[e](run(e))


class Arena:
    def __init__(self, nc, st, nbytes):
        self.cap = nbytes
        self.t = st.enter_context(nc.sbuf_tensor("arena", [128, nbytes // 4], F32))
        self.off = 0

    def reset(self):
        self.off = 0

    def _take(self, nbytes):
        nbytes = (nbytes + 63) // 64 * 64
        o = self.off
        self.off += nbytes
        assert self.off <= self.cap, (self.off, self.cap)
        return o

    def f32(self, n):
        o = self._take(n * 4)
        return self.t[:, o // 4:o // 4 + n]

    def bf16(self, n):
        assert n % 2 == 0
        o = self._take(n * 2)
        return self.t[:, o // 4:o // 4 + n // 2].bitcast(BF16)


def bc_ap(ap, dims):
    return bass.AP(tensor=ap.tensor, offset=ap.offset, ap=[list(ap.ap[0])] + [list(d) for d in dims])


def row_bcast(dram_ap_1d, n):
    return bass.AP(tensor=dram_ap_1d.tensor, offset=dram_ap_1d.offset, ap=[[0, 128], [1, n]])


def build(stages=("attn", "peer0", "conv", "peer1"), out_stage=None, dbg=False):
    nc = bass.Bass("TRN2", target_bir_lowering=False)
    P = Prog()
    st = ExitStack()

    def din(name, shape, dt=F32):
        return nc.dram_tensor(name, list(shape), dt, kind="ExternalInput").ap()

    out_stage = out_stage or stages[-1]

    def dscr(name, shape, dt=F32, stage=None):
        kind = "ExternalOutput" if (stage is not None and stage == out_stage) else "Internal"
        nm = "y" if kind == "ExternalOutput" else name
        return nc.dram_tensor(nm, list(shape), dt, kind=kind).ap()

    x = din("x", [T, D])
    mixer_g = din("mixer_norm_g", [2, D])
    ffn_g = din("ffn_norm_g", [2, D])
    consts = din("consts", [128, 5 * 128])
    rope = din("rope_cs", [2, 128, T])
    gains = din("qk_gain", [128, 2])
    if "attn" in stages:
        w_qkv = din("attn_w_qkv", [D, 6144])
        w_o = din("attn_w_o", [D, D])
    if "conv" in stages:
        w_in = din("conv_w_in", [D, 3 * D])
        conv_wb = din("conv_wb", [32, 128, 4])
        w_out = din("conv_w_out", [D, D])
    if "peer0" in stages or "peer1" in stages:
        w_query = din("peer_w_query", [2, D, 2048])
        sub_keys = din("peer_sub_keys", [2, 16, 128, 128])
        peer_u = din("peer_u", [2, NE, D])
        peer_v = din("peer_v", [2, NE, D])

    h1 = dscr("h1", [T, D], stage="attn")
    h2 = dscr("h2", [T, D], stage="peer0")
    h3 = dscr("h3", [T, D], stage="conv")
    h4 = dscr("h4", [T, D], stage="peer1")

    arena = Arena(nc, st, 206 * 1024)
    ps = [st.enter_context(nc.psum_tensor(f"ps{i}", [128, 512], F32)) for i in range(8)]
    bps = [Buf(f"ps{i}") for i in range(8)]

    c_f32 = arena.f32(5 * 128)
    ident_f = c_f32[:, 0:128]
    rot_f = c_f32[:, 256:384]
    ident_b = arena.bf16(128)
    ones_b = arena.bf16(128)
    gain_t = arena.f32(2)
    bconst = Buf("const")
    P.dma("sp", lambda e: e.dma_start(out=c_f32, in_=consts), writes=[bconst])
    P.dma("sp", lambda e: e.dma_start(out=gain_t, in_=gains), writes=[bconst])
    P.op("dve", lambda e: e.tensor_copy(out=ident_b, in_=c_f32[:, 0:128]), reads=[bconst], writes=[bconst])
    P.op("dve", lambda e: e.tensor_copy(out=ones_b, in_=c_f32[:, 128:256]), reads=[bconst], writes=[bconst])
    base_off = arena.off

    dumps = {}

    def dump(name, ap, buf, dt):
        if not dbg or name in dumps:
            return
        shp = list(ap.shape)
        d_ = nc.dram_tensor("dbg_" + name, shp, dt, kind="ExternalOutput").ap()
        dumps[name] = d_
        P.dma("sp", lambda e: e.dma_start(out=d_, in_=ap), reads=[buf], writes=[Buf()])

    def phase_start():
        P.barrier()
        arena.off = base_off

    def norm_block(src, g_b, bg, tb, xnT, bxnT, xt_bufs, keep_x=None):
        for ti in range(4):
            row0 = tb * TB + ti * 128
            if keep_x is not None:
                xt, bxt = keep_x[ti]
            else:
                xt, bxt = xt_bufs[ti % len(xt_bufs)]
            P.dma("sp", lambda e, xt=xt, row0=row0: e.dma_start(out=xt, in_=src[row0:row0 + 128, :]), writes=[bxt])
            junk, bjunk = norm_block.junk
            ssq, bssq = norm_block.ssq
            P.op("act", lambda e, xt=xt: e.activation(out=junk, in_=xt, func=AF.Square),
                 reads=[bxt], writes=[bjunk])
            P.op("dve", lambda e: e.reduce_sum(out=ssq[:, 0:1], in_=junk, axis=AX.X), reads=[bjunk], writes=[bssq])
            P.op("dve", lambda e: e.tensor_scalar(out=ssq[:, 1:2], in0=ssq[:, 0:1], scalar1=1.0 / D, scalar2=EPS,
                                                  op0=ALU.mult, op1=ALU.add), reads=[bssq], writes=[bssq])
            P.op("act", lambda e: e.activation(out=ssq[:, 3:4], in_=ssq[:, 1:2], func=AF.Sqrt), reads=[bssq], writes=[bssq])
            P.op("dve", lambda e: e.reciprocal(out=ssq[:, 2:3], in_=ssq[:, 3:4]), reads=[bssq], writes=[bssq])
            xn, bxn = norm_block.xn
            P.op("dve", lambda e, xt=xt: e.scalar_tensor_tensor(out=xn, in0=xt, scalar=ssq[:, 2:3], in1=g_b,
                                                                op0=ALU.mult, op1=ALU.mult),
                 reads=[bxt, bssq, bg], writes=[bxn])
            dump("xt", xt, bxt, F32); dump("ssq", ssq, bssq, F32); dump("xn", xn, bxn, BF16)
            for k8 in range(4):
                pi = 6 + (k8 % 2)
                pv = ps[pi][:].bitcast(BF16)
                for kk in range(8):
                    k = k8 * 8 + kk
                    P.op("pe", lambda e, k=k, kk=kk, pv=pv: e.transpose(pv[:, kk * 128:(kk + 1) * 128],
                                                                      xn[:, k * 128:(k + 1) * 128], ident_b),
                         reads=[bxn, bconst], writes=[bps[pi]], sig=(kk == 7))
                eng = "act" if k8 % 2 == 0 else "dve"
                dst = xnT[:, k8 * 8:(k8 + 1) * 8, ti * 128:(ti + 1) * 128]
                srcv = pv.rearrange("p (k t) -> p k t", k=8)
                if eng == "act":
                    P.op("act", lambda e, dst=dst, srcv=srcv: e.copy(out=dst, in_=srcv), reads=[bps[pi]], writes=[bxnT])
                else:
                    P.op("dve", lambda e, dst=dst, srcv=srcv: e.tensor_copy(out=dst, in_=srcv), reads=[bps[pi]], writes=[bxnT])

    def norm_setup():
        norm_block.junk = (arena.bf16(D), Buf("junk"))
        norm_block.ssq = (arena.f32(4), Buf("ssq"))
        norm_block.xn = (arena.bf16(D), Buf("xn"))

    def load_gain_row(g_dram_row):
        g_b = arena.f32(D)
        bg = Buf("g")
        P.dma("sp", lambda e: e.dma_start(out=g_b, in_=row_bcast(g_dram_row, D)), writes=[bg])
        return g_b, bg

    def wstream(loads, compute):
        n = len(loads)
        loads[0](0)
        for i in range(n):
            if i + 1 < n:
                loads[i + 1]((i + 1) % 2)
            compute(i, i % 2)

    def proj_residual(srcT, bsrcT, w_dram, res_src, dst):
        phase_start()
        ot = arena.bf16(32 * TB).rearrange("p (h t) -> p h t", h=32); bot = Buf("ot")
        wC = [arena.bf16(32 * 512).rearrange("p (k c) -> p k c", k=32) for _ in range(2)]
        bwC = [Buf("wo0"), Buf("wo1")]
        xts = [(arena.f32(D), Buf(f"xr{i}")) for i in range(4)]
        wo_v = w_dram.rearrange("(h p) c -> p h c", p=128)
        cnt = [0]
        for tb in range(NB):
            P.dma("sp", lambda e, tb=tb: e.dma_start(out=ot, in_=srcT[:, :, tb * TB:(tb + 1) * TB].rearrange("h p t -> p h t")),
                  reads=[bsrcT], writes=[bot])
            for ti in range(4):
                r0 = tb * TB + ti * 128
                P.dma("sp", lambda e, ti=ti, r0=r0: e.dma_start(out=xts[ti][0], in_=res_src[r0:r0 + 128, :]), writes=[xts[ti][1]])

            def ld(cb):
                def f(slot):
                    P.dma("pool", lambda e: e.dma_start(out=wC[slot], in_=wo_v[:, :, cb * 512:(cb + 1) * 512]), writes=[bwC[slot]])
                return f

            def comp(cb, slot):
                for ti in range(4):
                    pq = cnt[0] % 4
                    cnt[0] += 1
                    for h in range(32):
                        P.op("pe", lambda e, h=h, ti=ti, pq=pq: e.matmul(ps[pq][:], ot[:, h, ti * 128:(ti + 1) * 128], wC[slot][:, h, :],
                                                                         start=(h == 0), stop=(h == 31)),
                                 sig=(h == 31),
                             reads=[bot, bwC[slot]], writes=[bps[pq]])
                    xs = xts[ti][0][:, cb * 512:(cb + 1) * 512]
                    P.op("dve", lambda e, xs=xs, pq=pq: e.tensor_tensor(out=xs, in0=xs, in1=ps[pq][:], op=ALU.add),
                         reads=[bps[pq]], writes=[xts[ti][1]])
            wstream([ld(cb) for cb in range(8)], comp)
            for ti in range(4):
                r0 = tb * TB + ti * 128
                P.dma("sp", lambda e, ti=ti, r0=r0: e.dma_start(out=dst[r0:r0 + 128, :], in_=xts[ti][0]), reads=[xts[ti][1]], writes=[Buf()])

    if "attn" in stages:
        QT = nc.dram_tensor("QT", [32, 128, T], BF16, kind=("ExternalOutput" if dbg else "Internal")).ap()
        KT = nc.dram_tensor("KT", [8, 128, T], BF16, kind=("ExternalOutput" if dbg else "Internal")).ap()
        VS = nc.dram_tensor("VS", [T, 1024], BF16, kind=("ExternalOutput" if dbg else "Internal")).ap()
        OT = nc.dram_tensor("OT", [32, 128, T], BF16, kind=("ExternalOutput" if dbg else "Internal")).ap()
        bQT, bKT, bVS, bOT = Buf("QT"), Buf("KT"), Buf("VS"), Buf("OT")

        phase_start()
        g_b, bg = load_gain_row(mixer_g[0])
        norm_setup()
        cs = arena.f32(2 * T).rearrange("p (c t) -> p c t", c=2)
        bcs = Buf("cs")
        P.dma("sp", lambda e: e.dma_start(out=cs, in_=rope.rearrange("c p t -> p c t")), writes=[bcs])
        xnT = arena.bf16(32 * TB).rearrange("p (k t) -> p k t", k=32)
        bxnT = Buf("xnT")
        xt_bufs = [(arena.f32(D), Buf("xt0"))]
        wt = [arena.bf16(32 * 512).rearrange("p (k c) -> p k c", k=32) for _ in range(2)]
        bwt = [Buf("wt0"), Buf("wt1")]
        sq = arena.bf16(512); bsq = Buf("sq")
        rstd = arena.f32(512); brstd = Buf("rstd")
        qn = arena.f32(512); bqn = Buf("qn")
        t1 = arena.f32(512); bt1 = Buf("t1")
        t2 = arena.f32(512); bt2 = Buf("t2")
        qo = [arena.bf16(512) for _ in range(2)]; bqo = [Buf("qo0"), Buf("qo1")]
        vo = [arena.bf16(512) for _ in range(2)]; bvo = [Buf("vo0"), Buf("vo1")]
        wq_v = w_qkv.rearrange("(k p) c -> p k c", p=128)
        cnt = {"q": 0, "v": 0}
        for tb in range(1 if dbg == 2 else NB):
            norm_block(x, g_b, bg, tb, xnT, bxnT, xt_bufs)

            dump("xnT", xnT, bxnT, BF16)

            def ld(cb):
                def f(slot):
                    P.dma("pool", lambda e: e.dma_start(out=wt[slot], in_=wq_v[:, :, cb * 512:(cb + 1) * 512]),
                          writes=[bwt[slot]])
                    dump("wt", wt[slot], bwt[slot], BF16)
                return f

            def comp(cb, slot, tb=tb):
                if cb < 10:
                    for j in range(4):
                        hb = cb * 4 + j
                        pq = cnt["q"] % 2
                        cnt["q"] += 1
                        for k in range(32):
                            P.op("pe", lambda e, k=k, j=j, pq=pq: e.matmul(ps[pq][:], wt[slot][:, k, j * 128:(j + 1) * 128],
                                                                           xnT[:, k, :], start=(k == 0), stop=(k == 31)),
                                 sig=(k == 31),
                                 reads=[bwt[slot], bxnT], writes=[bps[pq]])
                        gcol = gain_t[:, 0:1] if hb < 32 else gain_t[:, 1:2]
                        P.op("act", lambda e, pq=pq: e.activation(out=sq, in_=ps[pq][:], func=AF.Square),
                             reads=[bps[pq]], writes=[bsq])
                        P.op("pe", lambda e: e.matmul(ps[2][:], ones_b, sq, start=True, stop=True),
                             reads=[bsq, bconst], writes=[bps[2]])
                        P.op("dve", lambda e: e.tensor_scalar(out=rstd, in0=ps[2][:], scalar1=1.0 / 128, scalar2=EPS,
                                                              op0=ALU.mult, op1=ALU.add), reads=[bps[2]], writes=[brstd])
                        P.op("act", lambda e: e.activation(out=rstd, in_=rstd, func=AF.Sqrt), reads=[brstd], writes=[brstd])
                        P.op("dve", lambda e: e.reciprocal(out=rstd, in_=rstd), reads=[brstd], writes=[brstd])
                        P.op("dve", lambda e, pq=pq, gcol=gcol: e.scalar_tensor_tensor(out=qn, in0=ps[pq][:], scalar=gcol, in1=rstd,
                                                                                       op0=ALU.mult, op1=ALU.mult),
                             reads=[bps[pq], brstd, bconst], writes=[bqn])
                        dump("rstd", rstd, brstd, F32); dump("qn", qn, bqn, F32); dump("sq", sq, bsq, BF16)
                        P.op("pe", lambda e: e.matmul(ps[3][:], rot_f, qn, start=True, stop=True),
                             reads=[bqn, bconst], writes=[bps[3]])
                        tsl = slice(tb * TB, (tb + 1) * TB)
                        P.op("pool", lambda e, tsl=tsl: e.tensor_tensor(out=t1, in0=qn, in1=cs[:, 0, tsl], op=ALU.mult),
                             reads=[bqn, bcs], writes=[bt1])
                        P.op("dve", lambda e, tsl=tsl: e.tensor_tensor(out=t2, in0=ps[3][:], in1=cs[:, 1, tsl], op=ALU.mult),
                             reads=[bps[3], bcs], writes=[bt2])
                        qs = hb % 2
                        P.op("dve", lambda e, qs=qs: e.tensor_tensor(out=qo[qs], in0=t1, in1=t2, op=ALU.add),
                             reads=[bt1, bt2], writes=[bqo[qs]])
                        dump("t1", t1, bt1, F32); dump("t2", t2, bt2, F32); dump("qo", qo[qs], bqo[qs], BF16)
                        if hb < 32:
                            dst, bd = QT[hb][:, tb * TB:(tb + 1) * TB], bQT
                        else:
                            dst, bd = KT[hb - 32][:, tb * TB:(tb + 1) * TB], bKT
                        P.dma("sp", lambda e, qs=qs, dst=dst: e.dma_start(out=dst, in_=qo[qs]), reads=[bqo[qs]], writes=[bd])
                else:
                    for ti in range(4):
                        pq = 4 + cnt["v"] % 2
                        vs = cnt["v"] % 2
                        cnt["v"] += 1
                        for k in range(32):
                            P.op("pe", lambda e, k=k, ti=ti, pq=pq: e.matmul(ps[pq][:], xnT[:, k, ti * 128:(ti + 1) * 128],
                                                                             wt[slot][:, k, :], start=(k == 0), stop=(k == 31)),
                                 sig=(k == 31),
                                 reads=[bwt[slot], bxnT], writes=[bps[pq]])
                        P.op("act", lambda e, pq=pq, vs=vs: e.copy(out=vo[vs], in_=ps[pq][:]), reads=[bps[pq]], writes=[bvo[vs]])
                        dump("vo", vo[vs], bvo[vs], BF16)
                        r0 = tb * TB + ti * 128
                        c0 = (cb - 10) * 512
                        P.dma("sp", lambda e, vs=vs, r0=r0, c0=c0: e.dma_start(out=VS[r0:r0 + 128, c0:c0 + 512], in_=vo[vs]),
                              reads=[bvo[vs]], writes=[bVS])
            wstream([ld(cb) for cb in range(12)], comp)

        if dbg in (2, 3):
            P.barrier(); P.emit(nc, st); st.close(); return nc
        phase_start()
        kt = arena.bf16(T); bkt = Buf("kt")
        vt = arena.bf16(32 * 130).rearrange("p (c d) -> p c d", c=32); bvt = Buf("vt")
        qt = arena.bf16(4 * T).rearrange("p (h t) -> p h t", h=4); bqt = Buf("qt")
        pT = [arena.bf16(512) for _ in range(3)]; bpT = [Buf(f"pT{i}") for i in range(3)]
        rden = arena.f32(4); brden = Buf("rden")
        on = arena.bf16(128 * 4).rearrange("p (h d) -> p h d", h=4); bon = Buf("on")
        oT = [arena.bf16(4 * 512).rearrange("p (h t) -> p h t", h=4) for _ in range(2)]; boT = [Buf("oT0"), Buf("oT1")]
        scale = 128.0 ** -0.5
        for g in range(8):
            P.dma("sp", lambda e, g=g: e.dma_start(out=kt, in_=KT[g]), writes=[bkt], reads=[bKT])
            P.dma("sp", lambda e, g=g: e.dma_start(out=vt[:, :, 0:128],
                                                   in_=VS[:, g * 128:(g + 1) * 128].rearrange("(c p) d -> p c d", p=128)),
                  writes=[bvt], reads=[bVS])
            P.op("pool", lambda e: e.memset(vt[:, :, 128:129], 1.0), writes=[bvt])
            P.dma("sp", lambda e, g=g: e.dma_start(out=qt, in_=QT[4 * g:4 * g + 4].rearrange("h p t -> p h t")),
                  writes=[bqt], reads=[bQT])
            n_s = 0
            for qi in range(NT):
                for c in range(32):
                    sp_ = n_s % 2
                    pslot = n_s % 3
                    n_s += 1
                    P.op("pe", lambda e, c=c, qi=qi, sp_=sp_: e.matmul(ps[sp_][:], kt[:, c * 128:(c + 1) * 128],
                                                                      qt[:, :, qi * 128:(qi + 1) * 128], start=True, stop=True),
                         reads=[bkt, bqt], writes=[bps[sp_]])
                    P.op("act", lambda e, sp_=sp_, pslot=pslot: e.activation(out=pT[pslot], in_=ps[sp_][:], func=AF.Exp, scale=scale),
                         reads=[bps[sp_]], writes=[bpT[pslot]])
                    for j in range(4):
                        P.op("pe", lambda e, j=j, c=c, pslot=pslot: e.matmul(ps[2 + j][:, 0:129], pT[pslot][:, j * 128:(j + 1) * 128],
                                                                            vt[:, c, 0:129], start=(c == 0), stop=(c == 31)),
                             reads=[bpT[pslot], bvt], writes=[bps[2 + j]], sig=(j == 3))
                os_ = (qi // 4) % 2
                for j in range(4):
                    P.op("dve", lambda e, j=j: e.reciprocal(out=rden[:, j:j + 1], in_=ps[2 + j][:, 128:129]),
                         reads=[bps[2 + j]], writes=[brden])
                    P.op("dve", lambda e, j=j: e.tensor_scalar(out=on[:, j, :], in0=ps[2 + j][:, 0:128], scalar1=rden[:, j:j + 1],
                                                               scalar2=None, op0=ALU.mult),
                         reads=[bps[2 + j], brden], writes=[bon])
                pv = ps[6 + qi % 2][:].bitcast(BF16)
                for j in range(4):
                    P.op("pe", lambda e, j=j, pv=pv: e.transpose(pv[:, j * 128:(j + 1) * 128], on[:, j, :], ident_b),
                         reads=[bon, bconst], writes=[bps[6 + qi % 2]])
                P.op("dve", lambda e, pv=pv, os_=os_, qi=qi: e.tensor_copy(out=oT[os_][:, :, (qi % 4) * 128:(qi % 4 + 1) * 128],
                                                                         in_=pv[:, 0:512].rearrange("p (h t) -> p h t", h=4)),
                     reads=[bps[6 + qi % 2]], writes=[boT[os_]])
                if qi % 4 == 3:
                    t0 = (qi // 4) * 512
                    P.dma("sp", lambda e, os_=os_, t0=t0, g=g: e.dma_start(out=OT[4 * g:4 * g + 4][:, :, t0:t0 + 512].rearrange("h p t -> p h t"),
                                                                         in_=oT[os_]),
                          reads=[boT[os_]], writes=[bOT])

        if dbg == 4:
            P.barrier(); P.emit(nc, st); st.close(); return nc
        proj_residual(OT, bOT, w_o, x, h1)

    def conv_layer(hin, hout):
        UTs = nc.dram_tensor("cUT", [32, 128, T], F32, kind="Internal").ap()
        BTs = nc.dram_tensor("cBT", [32, 128, T], F32, kind="Internal").ap()
        YT = nc.dram_tensor("cYT", [32, 128, T], BF16, kind="Internal").ap()
        bUTs, bBTs, bYT = Buf("cUT"), Buf("cBT"), Buf("cYT")
        phase_start()
        g_b, bg = load_gain_row(mixer_g[1])
        norm_setup()
        xnT = arena.bf16(32 * TB).rearrange("p (k t) -> p k t", k=32); bxnT = Buf("xnT")
        xt_bufs = [(arena.f32(D), Buf("xt0"))]
        wt = [arena.bf16(32 * 512).rearrange("p (k c) -> p k c", k=32) for _ in range(2)]
        bwt = [Buf("wi0"), Buf("wi1")]
        cbuf = [arena.f32(512) for _ in range(4)]; bcbuf = [Buf(f"cb{i}") for i in range(4)]
        obuf = [arena.f32(512) for _ in range(2)]; bobuf = [Buf("ob0"), Buf("ob1")]
        wi_v = w_in.rearrange("(k p) c -> p k c", p=128)
        cnt = [0, 0]
        for tb in range(NB):
            norm_block(hin, g_b, bg, tb, xnT, bxnT, xt_bufs)
            items = []
            for cb4 in range(8):
                for kind in ("c", "x", "b"):
                    items.append((cb4, kind))

            def ld(it):
                cb4, kind = it
                col0 = {"b": 0, "c": D, "x": 2 * D}[kind] + cb4 * 512

                def f(slot):
                    P.dma("pool", lambda e: e.dma_start(out=wt[slot], in_=wi_v[:, :, col0:col0 + 512]), writes=[bwt[slot]])
                return f

            def comp(i, slot, tb=tb, items=items):
                cb4, kind = items[i]
                for j in range(4):
                    pq = cnt[0] % 4
                    cnt[0] += 1
                    for k in range(32):
                        P.op("pe", lambda e, k=k, j=j, pq=pq: e.matmul(ps[pq][:], wt[slot][:, k, j * 128:(j + 1) * 128], xnT[:, k, :],
                                                                       start=(k == 0), stop=(k == 31)),
                                 sig=(k == 31),
                             reads=[bwt[slot], bxnT], writes=[bps[pq]])
                    chunk = cb4 * 4 + j
                    if kind == "c":
                        P.op("act", lambda e, j=j, pq=pq: e.copy(out=cbuf[j], in_=ps[pq][:]), reads=[bps[pq]], writes=[bcbuf[j]])
                    else:
                        os_ = cnt[1] % 2
                        cnt[1] += 1
                        if kind == "x":
                            P.op("dve", lambda e, j=j, pq=pq, os_=os_: e.tensor_tensor(out=obuf[os_], in0=cbuf[j], in1=ps[pq][:], op=ALU.mult),
                                 reads=[bps[pq], bcbuf[j]], writes=[bobuf[os_]])
                            dst_, bd = UTs[chunk][:, tb * TB:(tb + 1) * TB], bUTs
                        else:
                            P.op("act", lambda e, pq=pq, os_=os_: e.copy(out=obuf[os_], in_=ps[pq][:]), reads=[bps[pq]], writes=[bobuf[os_]])
                            dst_, bd = BTs[chunk][:, tb * TB:(tb + 1) * TB], bBTs
                        P.dma("sp", lambda e, os_=os_, dst_=dst_: e.dma_start(out=dst_, in_=obuf[os_]), reads=[bobuf[os_]], writes=[bd])
            wstream([ld(it) for it in items], comp)

        phase_start()
        ub = [arena.f32(T + 2) for _ in range(2)]; bub = [Buf("ub0"), Buf("ub1")]
        bt = [arena.f32(T) for _ in range(2)]; bbt = [Buf("bt0"), Buf("bt1")]
        cv = arena.f32(T); bcv = Buf("cv")
        yb = [arena.bf16(T) for _ in range(2)]; byb = [Buf("yb0"), Buf("yb1")]
        wb = [arena.f32(4) for _ in range(2)]; bwb = [Buf("wb0"), Buf("wb1")]
        for s_ in range(2):
            P.op("pool", lambda e, s_=s_: e.memset(ub[s_][:, 0:1], 0.0), writes=[bub[s_]])
            P.op("pool", lambda e, s_=s_: e.memset(ub[s_][:, T + 1:T + 2], 0.0), writes=[bub[s_]])
        for ch in range(32):
            s_ = ch % 2
            P.dma("sp", lambda e, ch=ch, s_=s_: e.dma_start(out=ub[s_][:, 1:T + 1], in_=UTs[ch]), reads=[bUTs], writes=[bub[s_]])
            P.dma("sp", lambda e, ch=ch, s_=s_: e.dma_start(out=bt[s_], in_=BTs[ch]), reads=[bBTs], writes=[bbt[s_]])
            P.dma("sp", lambda e, ch=ch, s_=s_: e.dma_start(out=wb[s_], in_=conv_wb[ch]), writes=[bwb[s_]])
            P.op("dve", lambda e, s_=s_: e.tensor_scalar(out=cv, in0=ub[s_][:, 0:T], scalar1=wb[s_][:, 0:1], scalar2=wb[s_][:, 3:4],
                                                         op0=ALU.mult, op1=ALU.add), reads=[bub[s_], bwb[s_]], writes=[bcv])
            P.op("dve", lambda e, s_=s_: e.scalar_tensor_tensor(out=cv, in0=ub[s_][:, 1:T + 1], scalar=wb[s_][:, 1:2], in1=cv,
                                                                op0=ALU.mult, op1=ALU.add), reads=[bub[s_], bwb[s_]], writes=[bcv])
            P.op("dve", lambda e, s_=s_: e.scalar_tensor_tensor(out=cv, in0=ub[s_][:, 2:T + 2], scalar=wb[s_][:, 2:3], in1=cv,
                                                                op0=ALU.mult, op1=ALU.add), reads=[bub[s_], bwb[s_]], writes=[bcv])
            P.op("pool", lambda e, s_=s_: e.tensor_tensor(out=yb[s_], in0=cv, in1=bt[s_], op=ALU.mult), reads=[bcv, bbt[s_]], writes=[byb[s_]])
            P.dma("sp", lambda e, ch=ch, s_=s_: e.dma_start(out=YT[ch], in_=yb[s_]), reads=[byb[s_]], writes=[bYT])
        proj_residual(YT, bYT, w_out, hin, hout)

    def peer_layer(L, hin, hout):
        UT = nc.dram_tensor(f"pUT{L}", [32, 128, 32 * 512], BF16, kind="Internal").ap()
        XT = nc.dram_tensor(f"pXT{L}", [NB, 128, 32 * TB], BF16, kind="Internal").ap()
        SS = nc.dram_tensor(f"pSS{L}", [T, 2048], F32, kind="Internal").ap()
        GS = nc.dram_tensor(f"pGS{L}", [T, NE], BF16, kind="Internal").ap()
        bUT, bXT, bSS, bGS = Buf("pUT"), Buf("pXT"), Buf("pSS"), Buf("pGS")

        phase_start()
        ub = [arena.bf16(4 * D).rearrange("p (c d) -> p c d", c=4) for _ in range(2)]; bub = [Buf("ub0"), Buf("ub1")]
        utb = [arena.bf16(32 * 512).rearrange("p (k e) -> p k e", k=32) for _ in range(2)]; butb = [Buf("utb0"), Buf("utb1")]
        ncp = [0]

        def ld0(eb2):
            def f(slot):
                P.dma("pool", lambda e: e.dma_start(out=ub[slot], in_=peer_u[L][eb2 * 512:(eb2 + 1) * 512, :].rearrange("(c p) d -> p c d", p=128)),
                      writes=[bub[slot]])
            return f

        def comp0(eb2, slot):
            for c in range(4):
                for k8 in range(4):
                    pi = ncp[0] % 4
                    ncp[0] += 1
                    pv = ps[pi][:].bitcast(BF16)
                    for kk in range(8):
                        k = k8 * 8 + kk
                        P.op("pe", lambda e, k=k, kk=kk, c=c, pv=pv: e.transpose(pv[:, kk * 128:(kk + 1) * 128], ub[slot][:, c, k * 128:(k + 1) * 128], ident_b),
                             reads=[bub[slot], bconst], writes=[bps[pi]], sig=(kk == 7))
                    dst_ = utb[slot][:, k8 * 8:(k8 + 1) * 8, c * 128:(c + 1) * 128]
                    srcv = pv.rearrange("p (k t) -> p k t", k=8)
                    if ncp[0] % 2 == 0:
                        P.op("act", lambda e, dst_=dst_, srcv=srcv: e.copy(out=dst_, in_=srcv), reads=[bps[pi]], writes=[butb[slot]])
                    else:
                        P.op("dve", lambda e, dst_=dst_, srcv=srcv: e.tensor_copy(out=dst_, in_=srcv), reads=[bps[pi]], writes=[butb[slot]])
            P.dma("sp", lambda e: e.dma_start(out=UT[eb2].rearrange("p (k e) -> p k e", k=32), in_=utb[slot]), reads=[butb[slot]], writes=[bUT])
        wstream([ld0(eb2) for eb2 in range(32)], comp0)

        phase_start()
        g_b, bg = load_gain_row(ffn_g[L])
        norm_setup()
        xnT = arena.bf16(32 * TB).rearrange("p (k t) -> p k t", k=32); bxnT = Buf("xnT")
        xt_bufs = [(arena.f32(D), Buf("xt0"))]
        wq = [arena.bf16(32 * 256).rearrange("p (k c) -> p k c", k=32) for _ in range(2)]; bwq = [Buf("wq0"), Buf("wq1")]
        qT = arena.f32(16 * TB).rearrange("p (h t) -> p h t", h=16); bqT = Buf("qT")
        sk = arena.f32(16 * 128).rearrange("p (h d) -> p h d", h=16); bsk = Buf("sk")
        skT = arena.f32(16 * 128).rearrange("p (h n) -> p h n", h=16); bskT = Buf("skT")
        s_t = [arena.f32(2048) for _ in range(2)]; bs_t = [Buf("st0"), Buf("st1")]
        P.dma("sp", lambda e: e.dma_start(out=sk, in_=sub_keys[L].rearrange("h n d -> n h d")), writes=[bsk])
        for hp in range(16):
            pi = 4 + (hp // 4) % 2
            P.op("pe", lambda e, hp=hp, pi=pi: e.transpose(ps[pi][:, (hp % 4) * 128:(hp % 4 + 1) * 128], sk[:, hp, :], ident_f),
                 reads=[bsk, bconst], writes=[bps[pi]])
            if hp % 4 == 3:
                P.op("dve", lambda e, hp=hp, pi=pi: e.tensor_copy(out=skT[:, hp - 3:hp + 1, :], in_=ps[pi][:].rearrange("p (h n) -> p h n", h=4)),
                     reads=[bps[pi]], writes=[bskT])
        wq_v = w_query[L].rearrange("(k p) c -> p k c", p=128)
        cn = [0, 0]
        for tb in range(NB):
            norm_block(hin, g_b, bg, tb, xnT, bxnT, xt_bufs)
            P.dma("sp", lambda e, tb=tb: e.dma_start(out=XT[tb].rearrange("p (k t) -> p k t", k=32), in_=xnT), reads=[bxnT], writes=[bXT])

            def ld(ct):
                def f(slot):
                    P.dma("pool", lambda e: e.dma_start(out=wq[slot], in_=wq_v[:, :, ct * 256:(ct + 1) * 256]), writes=[bwq[slot]])
                return f

            def comp(ct, slot):
                for j in range(2):
                    hp = ct * 2 + j
                    pq = cn[0] % 2
                    cn[0] += 1
                    for k in range(32):
                        P.op("pe", lambda e, k=k, j=j, pq=pq: e.matmul(ps[pq][:], wq[slot][:, k, j * 128:(j + 1) * 128], xnT[:, k, :],
                                                                       start=(k == 0), stop=(k == 31)),
                                 sig=(k == 31),
                             reads=[bwq[slot], bxnT], writes=[bps[pq]])
                    if hp % 2 == 0:
                        P.op("act", lambda e, hp=hp, pq=pq: e.copy(out=qT[:, hp, :], in_=ps[pq][:]), reads=[bps[pq]], writes=[bqT])
                    else:
                        P.op("dve", lambda e, hp=hp, pq=pq: e.tensor_copy(out=qT[:, hp, :], in_=ps[pq][:]), reads=[bps[pq]], writes=[bqT])
            wstream([ld(ct) for ct in range(8)], comp)
            for ti in range(4):
                ss = cn[1] % 2
                cn[1] += 1
                for hp4 in range(4):
                    pi = 2 + hp4 % 2
                    for j in range(4):
                        hp = hp4 * 4 + j
                        P.op("pe", lambda e, hp=hp, j=j, ti=ti, pi=pi: e.matmul(ps[pi][:, j * 128:(j + 1) * 128], qT[:, hp, ti * 128:(ti + 1) * 128],
                                                                              skT[:, hp, :], start=True, stop=True),
                             reads=[bqT, bskT], writes=[bps[pi]])
                    P.op("dve", lambda e, hp4=hp4, pi=pi, ss=ss: e.tensor_copy(out=s_t[ss][:, hp4 * 512:(hp4 + 1) * 512], in_=ps[pi][:]),
                         reads=[bps[pi]], writes=[bs_t[ss]])
                r0 = tb * TB + ti * 128
                P.dma("sp", lambda e, r0=r0, ss=ss: e.dma_start(out=SS[r0:r0 + 128, :], in_=s_t[ss]), reads=[bs_t[ss]], writes=[bSS])

        phase_start()
        NEG = -1.0e30
        sb = [arena.f32(2048) for _ in range(2)]; bsb = [Buf("sb0"), Buf("sb1")]
        top = arena.f32(256); btop = Buf("top")
        tmp = arena.f32(128); btmp = Buf("tmp")
        cand = arena.f32(8 * 256); bcand = Buf("cand")
        tmpc = arena.f32(256); btmpc = Buf("tmpc")
        best = arena.f32(8 * 24); bbest = Buf("best")
        sm = arena.f32(64); bsm = Buf("sm")
        ex = arena.f32(128); bex = Buf("ex")
        pen = arena.f32(128); bpen = Buf("pen")
        d1 = arena.f32(8 * 128); bd1 = Buf("d1")
        s2m = arena.f32(8 * 128); bs2m = Buf("s2m")
        Z3 = [arena.f32(2048) for _ in range(2)]; bZ3 = [Buf("Z30"), Buf("Z31")]
        E3 = [arena.f32(2048) for _ in range(2)]; bE3 = [Buf("E30"), Buf("E31")]
        M3 = [arena.f32(2048) for _ in range(2)]; bM3 = [Buf("M30"), Buf("M31")]
        Gc = [arena.f32(2048) for _ in range(2)]; bGc = [Buf("Gc0"), Buf("Gc1")]
        gout = [arena.bf16(NE) for _ in range(2)]; bgout = [Buf("go0"), Buf("go1")]
        nz = [0]
        for tile in range(NT):
            ss = tile % 2
            s = sb[ss]
            bs = bsb[ss]
            r0 = tile * 128
            P.dma("sp", lambda e, r0=r0, s=s: e.dma_start(out=s, in_=SS[r0:r0 + 128, :]), reads=[bSS], writes=[bs])
            for hp in range(16):
                sv = s[:, hp * 128:(hp + 1) * 128]
                P.op("dve", lambda e, hp=hp, sv=sv: e.max(out=top[:, hp * 16:hp * 16 + 8], in_=sv), reads=[bs], writes=[btop])
                P.op("dve", lambda e, hp=hp, sv=sv: e.match_replace(out=tmp, in_to_replace=top[:, hp * 16:hp * 16 + 8], in_values=sv, imm_value=NEG),
                     reads=[bs, btop], writes=[btmp])
                P.op("dve", lambda e, hp=hp: e.max(out=top[:, hp * 16 + 8:hp * 16 + 16], in_=tmp), reads=[btmp], writes=[btop])
            cand4 = cand.rearrange("p (h a b) -> p h a b", h=8, a=16)
            P.op("dve", lambda e, cand4=cand4: e.tensor_tensor(out=cand4, in0=bc_ap(top, [[32, 8], [1, 16], [0, 16]]),
                                                              in1=bc_ap(top[:, 16:], [[32, 8], [0, 16], [1, 16]]), op=ALU.add),
                 reads=[btop], writes=[bcand])
            for h in range(8):
                cvw = cand[:, h * 256:(h + 1) * 256]
                P.op("dve", lambda e, h=h, cvw=cvw: e.max(out=best[:, h * 24:h * 24 + 8], in_=cvw), reads=[bcand], writes=[bbest])
                P.op("dve", lambda e, h=h, cvw=cvw: e.match_replace(out=tmpc, in_to_replace=best[:, h * 24:h * 24 + 8], in_values=cvw, imm_value=NEG),
                     reads=[bcand, bbest], writes=[btmpc])
                P.op("dve", lambda e, h=h: e.max(out=best[:, h * 24 + 8:h * 24 + 16], in_=tmpc), reads=[btmpc], writes=[bbest])
                P.op("dve", lambda e, h=h: e.match_replace(out=tmpc, in_to_replace=best[:, h * 24 + 8:h * 24 + 16], in_values=tmpc, imm_value=NEG),
                     reads=[bbest, btmpc], writes=[btmpc])
                P.op("dve", lambda e, h=h: e.max(out=best[:, h * 24 + 16:h * 24 + 24], in_=tmpc), reads=[btmpc], writes=[bbest])
            b3 = best.rearrange("p (h k) -> p h k", h=8)
            negm, tau, tm, Zs, rZ, negtau = [sm[:, i * 8:(i + 1) * 8] for i in range(6)]
            P.op("dve", lambda e: e.tensor_scalar(out=negm, in0=b3[:, :, 0], scalar1=-1.0, scalar2=None, op0=ALU.mult), reads=[bbest], writes=[bsm])
            P.op("dve", lambda e: e.tensor_tensor(out=tau, in0=b3[:, :, 15], in1=b3[:, :, 16], op=ALU.add), reads=[bbest], writes=[bsm])
            P.op("dve", lambda e: e.tensor_scalar(out=tau, in0=tau, scalar1=0.5, scalar2=None, op0=ALU.mult), reads=[bsm], writes=[bsm])
            P.op("dve", lambda e: e.tensor_tensor(out=tm, in0=tau, in1=negm, op=ALU.add), reads=[bsm], writes=[bsm])
            P.op("dve", lambda e: e.tensor_scalar(out=negtau, in0=tau, scalar1=-1.0, scalar2=None, op0=ALU.mult), reads=[bsm], writes=[bsm])
            for h in range(8):
                P.op("act", lambda e, h=h: e.activation(out=ex[:, h * 16:(h + 1) * 16], in_=best[:, h * 24:h * 24 + 16], func=AF.Exp, bias=negm[:, h:h + 1]),
                     reads=[bbest, bsm], writes=[bex])
                P.op("dve", lambda e, h=h: e.reduce_sum(out=Zs[:, h:h + 1], in_=ex[:, h * 16:(h + 1) * 16], axis=AX.X), reads=[bex], writes=[bsm])
            P.op("dve", lambda e: e.reciprocal(out=rZ, in_=Zs), reads=[bsm], writes=[bsm])
            for h in range(8):
                for p_ in range(2):
                    hp = 2 * h + p_
                    sv = s[:, hp * 128:(hp + 1) * 128]
                    thr = top[:, hp * 16 + 15:hp * 16 + 16]
                    P.op("dve", lambda e, sv=sv, thr=thr: e.tensor_scalar(out=pen, in0=sv, scalar1=thr, scalar2=None, op0=ALU.is_ge),
                         reads=[bs, btop], writes=[bpen])
                    P.op("dve", lambda e: e.tensor_scalar(out=pen, in0=pen, scalar1=-1.0, scalar2=1.0e4, op0=ALU.add, op1=ALU.mult),
                         reads=[bpen], writes=[bpen])
                    if p_ == 0:
                        dv = d1[:, h * 128:(h + 1) * 128]
                        P.op("dve", lambda e, sv=sv, h=h, dv=dv: e.scalar_tensor_tensor(out=dv, in0=sv, scalar=negtau[:, h:h + 1], in1=pen,
                                                                                    op0=ALU.add, op1=ALU.add), reads=[bs, bsm, bpen], writes=[bd1])
                    else:
                        dv = s2m[:, h * 128:(h + 1) * 128]
                        P.op("dve", lambda e, sv=sv, dv=dv: e.tensor_tensor(out=dv, in0=sv, in1=pen, op=ALU.add), reads=[bs, bpen], writes=[bs2m])
            go = gout[tile % 2]
            bgo = bgout[tile % 2]
            for cc in range(8):
                gs = cc % 2
                for h in range(8):
                    zs = nz[0] % 2
                    nz[0] += 1
                    z3v = Z3[zs].rearrange("p (c i) -> p c i", c=16)
                    in0 = bc_ap(s2m[:, h * 128:(h + 1) * 128], [[0, 16], [1, 128]])
                    in1 = bc_ap(d1[:, h * 128 + cc * 16:h * 128 + cc * 16 + 16], [[1, 16], [0, 128]])
                    P.op("pool", lambda e, z3v=z3v, in0=in0, in1=in1: e.tensor_tensor(out=z3v, in0=in0, in1=in1, op=ALU.add),
                         reads=[bs2m, bd1], writes=[bZ3[zs]])
                    P.op("act", lambda e, zs=zs, h=h: e.activation(out=E3[zs], in_=Z3[zs], func=AF.Exp, bias=tm[:, h:h + 1]),
                         reads=[bZ3[zs], bsm], writes=[bE3[zs]])
                    P.op("dve", lambda e, zs=zs: e.scalar_tensor_tensor(out=M3[zs], in0=Z3[zs], scalar=0.0, in1=E3[zs], op0=ALU.is_ge, op1=ALU.mult),
                         reads=[bZ3[zs], bE3[zs]], writes=[bM3[zs]])
                    if h == 0:
                        P.op("dve", lambda e, zs=zs, gs=gs: e.tensor_scalar(out=Gc[gs], in0=M3[zs], scalar1=rZ[:, 0:1], scalar2=None, op0=ALU.mult),
                             reads=[bM3[zs], bsm], writes=[bGc[gs]])
                    else:
                        P.op("dve", lambda e, zs=zs, gs=gs, h=h: e.scalar_tensor_tensor(out=Gc[gs], in0=M3[zs], scalar=rZ[:, h:h + 1], in1=Gc[gs],
                                                                                      op0=ALU.mult, op1=ALU.add),
                             reads=[bM3[zs], bsm], writes=[bGc[gs]])
                P.op("act", lambda e, go=go, cc=cc, gs=gs: e.copy(out=go[:, cc * 2048:(cc + 1) * 2048], in_=Gc[gs]), reads=[bGc[gs]], writes=[bgo])
            P.dma("sp", lambda e, r0=r0, go=go: e.dma_start(out=GS[r0:r0 + 128, :], in_=go), reads=[bgo], writes=[bGS])

        phase_start()
        xnT2 = arena.bf16(32 * TB).rearrange("p (k t) -> p k t", k=32); bxn2 = Buf("xnT2")
        acc = arena.f32(4 * D).rearrange("p (i d) -> p i d", i=4); bacc = [Buf(f"acc{i}") for i in range(4)]
        ut = [arena.bf16(32 * 256).rearrange("p (k e) -> p k e", k=32) for _ in range(2)]; but = [Buf("ut0"), Buf("ut1")]
        vt = [arena.bf16(2 * D).rearrange("p (c d) -> p c d", c=2) for _ in range(2)]; bvt = [Buf("vt0"), Buf("vt1")]
        gt = [arena.bf16(4 * 256).rearrange("p (i e) -> p i e", i=4) for _ in range(2)]; bgt = [Buf("gt0"), Buf("gt1")]
        ga = [arena.f32(256) for _ in range(2)]; bga = [Buf("ga0"), Buf("ga1")]
        wv = [arena.bf16(256) for _ in range(2)]; bwv = [Buf("w0"), Buf("w1")]
        wT = [arena.bf16(256).rearrange("p (c t) -> p c t", c=2) for _ in range(2)]; bwT = [Buf("wT0"), Buf("wT1")]
        hres = arena.f32(D); bhres = Buf("hres")
        UTv = UT.rearrange("b p (k e) -> b p k e", k=32)
        GSv = GS.rearrange("(i p) e -> p i e", p=128)
        n2 = [0, 0, 0]
        for tb in range(NB):
            P.dma("sp", lambda e, tb=tb: e.dma_start(out=xnT2, in_=XT[tb].rearrange("p (k t) -> p k t", k=32)), reads=[bXT], writes=[bxn2])

            def ld(eb, tb=tb):
                def f(slot):
                    P.dma("sp", lambda e: e.dma_start(out=ut[slot], in_=UTv[eb // 2][:, :, (eb % 2) * 256:(eb % 2 + 1) * 256]), reads=[bUT], writes=[but[slot]])
                    P.dma("pool", lambda e: e.dma_start(out=vt[slot], in_=peer_v[L][eb * 256:(eb + 1) * 256, :].rearrange("(c p) d -> p c d", p=128)),
                          writes=[bvt[slot]])
                    P.dma("sp", lambda e: e.dma_start(out=gt[slot], in_=GSv[:, tb * 4:(tb + 1) * 4, eb * 256:(eb + 1) * 256]), reads=[bGS], writes=[bgt[slot]])
                return f

            def comp(eb, slot):
                for ti in range(4):
                    pa = n2[0] % 2
                    n2[0] += 1
                    for k in range(32):
                        P.op("pe", lambda e, k=k, ti=ti, pa=pa: e.matmul(ps[pa][:, 0:256], xnT2[:, k, ti * 128:(ti + 1) * 128], ut[slot][:, k, :],
                                                                         start=(k == 0), stop=(k == 31)),
                                 sig=(k == 31),
                             reads=[bxn2, but[slot]], writes=[bps[pa]])
                    P.op("act", lambda e, pa=pa: e.activation(out=ga[pa], in_=ps[pa][:, 0:256], func=AF.Gelu), reads=[bps[pa]], writes=[bga[pa]])
                    P.op("dve", lambda e, pa=pa, ti=ti: e.tensor_tensor(out=wv[pa], in0=ga[pa], in1=gt[slot][:, ti, :], op=ALU.mult),
                         reads=[bga[pa], bgt[slot]], writes=[bwv[pa]])
                    pvw = ps[2][:].bitcast(BF16)
                    for c in range(2):
                        P.op("pe", lambda e, c=c, pa=pa, pvw=pvw: e.transpose(pvw[:, c * 128:(c + 1) * 128], wv[pa][:, c * 128:(c + 1) * 128], ident_b),
                             reads=[bwv[pa], bconst], writes=[bps[2]], sig=(c == 1))
                    P.op("act", lambda e, pa=pa, pvw=pvw: e.copy(out=wT[pa], in_=pvw[:, 0:256].rearrange("p (c t) -> p c t", c=2)),
                         reads=[bps[2]], writes=[bwT[pa]])
                    for cb in range(8):
                        po = 3 + n2[1] % 5
                        n2[1] += 1
                        for c in range(2):
                            P.op("pe", lambda e, c=c, cb=cb, pa=pa, po=po: e.matmul(ps[po][:], wT[pa][:, c, :], vt[slot][:, c, cb * 512:(cb + 1) * 512],
                                                                                   start=(c == 0), stop=(c == 1)),
                                 reads=[bwT[pa], bvt[slot]], writes=[bps[po]], sig=(c == 1))
                        av = acc[:, ti, cb * 512:(cb + 1) * 512]
                        if eb == 0:
                            P.op("dve", lambda e, av=av, po=po: e.tensor_copy(out=av, in_=ps[po][:]), reads=[bps[po]], writes=[bacc[ti]])
                        else:
                            P.op("dve", lambda e, av=av, po=po: e.tensor_tensor(out=av, in0=av, in1=ps[po][:], op=ALU.add), reads=[bps[po]], writes=[bacc[ti]])
            wstream([ld(eb) for eb in range(64)], comp)
            for ti in range(4):
                r0 = tb * TB + ti * 128
                P.dma("sp", lambda e, r0=r0: e.dma_start(out=hres, in_=hin[r0:r0 + 128, :]), writes=[bhres])
                P.op("dve", lambda e, ti=ti: e.tensor_tensor(out=hres, in0=hres, in1=acc[:, ti, :], op=ALU.add), reads=[bacc[ti]], writes=[bhres])
                P.dma("sp", lambda e, r0=r0: e.dma_start(out=hout[r0:r0 + 128, :], in_=hres), reads=[bhres], writes=[Buf()])

    cur = x
    if "attn" in stages:
        cur = h1
    if "peer0" in stages:
        peer_layer(0, cur, h2)
        cur = h2
    if "conv" in stages:
        conv_layer(cur, h3)
        cur = h3
    if "peer1" in stages:
        peer_layer(1, cur, h4)
        cur = h4

    P.barrier()
    P.emit(nc, st)
    st.close()
    return nc


def host_consts():
    ident = np.eye(128, dtype=np.float32)
    ones = np.ones((128, 128), np.float32)
    rot = np.zeros((128, 128), np.float32)
    for m in range(128):
        half = (m % 64) // 32
        if half == 0:
            rot[m + 32, m] = -1.0
        else:
            rot[m - 32, m] = 1.0
    consts = np.concatenate([ident, ones, rot, np.zeros((128, 256), np.float32)], axis=1)
    pos = np.arange(T)
    row = (pos // 64).astype(np.float32)
    col = (pos % 64).astype(np.float32)
    inv_freq = (10000.0 ** (-np.arange(0, 64, 2, dtype=np.float32) / 64)).astype(np.float32)
    ang = np.zeros((128, T), np.float32)
    for d in range(128):
        axis = d // 64
        f = d % 32
        ang[d] = (row if axis == 0 else col) * inv_freq[f]
    rope = np.stack([np.cos(ang), np.sin(ang)]).astype(np.float32)
    return consts, rope


def make_in_maps(inputs, ncores, stages):
    consts, rope = host_consts()
    maps = []
    gains = np.stack([inputs["attn_q_gain"][0], inputs["attn_k_gain"][0]], axis=1).astype(np.float32)
    for c in range(ncores):
        m = {"x": np.ascontiguousarray(inputs["x"][c]), "mixer_norm_g": inputs["mixer_norm_g"],
             "ffn_norm_g": inputs["ffn_norm_g"], "consts": consts, "rope_cs": rope, "qk_gain": gains}
        if "attn" in stages:
            m["attn_w_qkv"] = inputs["attn_w_qkv"][0]
            m["attn_w_o"] = inputs["attn_w_o"][0]
        if "conv" in stages:
            m["conv_w_in"] = inputs["conv_w_in"][0]
            m["conv_w_out"] = inputs["conv_w_out"][0]
            wb = np.concatenate([inputs["conv_w"][0], inputs["conv_b"]], axis=0)
            m["conv_wb"] = np.ascontiguousarray(wb.T.reshape(32, 128, 4))
        if "peer0" in stages or "peer1" in stages:
            m["peer_w_query"] = inputs["peer_w_query"]
            m["peer_sub_keys"] = np.ascontiguousarray(inputs["peer_sub_keys"].reshape(2, 16, 128, 128))
            m["peer_u"] = inputs["peer_u"]
            m["peer_v"] = inputs["peer_v"]
        maps.append(m)
    return maps


def kernel(**inputs):
    inputs = {k: np.asarray(v) for k, v in inputs.items()}
    stages = ("attn", "peer0", "conv", "peer1")
    nc = build(stages)
    maps = make_in_maps(inputs, NCORES, stages)
    res = run_bass_kernel_spmd(nc, maps, core_ids=list(range(NCORES)))
    return np.stack([r["y"] for r in res.results], axis=0)
```

```python
from contextlib import ExitStack
import numpy as np
import concourse.bass as bass
import concourse.mybir as mybir
from concourse.bass_utils import run_bass_kernel_spmd

F32 = mybir.dt.float32
BF16 = mybir.dt.bfloat16
ALU = mybir.AluOpType
AF = mybir.ActivationFunctionType
AX = mybir.AxisListType

ENGS = ["pe", "act", "dve", "pool", "sp"]
DMA_RING = 8

D = 4096
T = 4096
NT = T // 128
TB = 512
NB = T // TB
NE = 16384
EPS = 1e-6
NCORES = 8
NBQ = 5
NBO = 4
TL = NBQ * TB
TOWN = NBO * TB


class Buf:
    __slots__ = ("name", "last_w", "readers")

    def __init__(self, name=""):
        self.name = name
        self.last_w = None
        self.readers = {}


class Prog:
    def __init__(self):
        self.ops = {e: [] for e in ENGS}
        self.cnt = {e: 0 for e in ENGS}
        self.seen = {e: {} for e in ENGS}
        self.dma_n = {e: 0 for e in ENGS}
        self.last = {}
        self.semkeys = set()
        self.pending = {}

    def _deps(self, reads, writes):
        deps = {}

        def add(k, v):
            if deps.get(k, 0) < v:
                deps[k] = v
        for b in reads:
            if b.last_w is not None:
                add(*b.last_w)
        for b in writes:
            if b.last_w is not None:
                add(*b.last_w)
            for k, v in b.readers.items():
                add(k, v)
        return deps

    def _commit(self, ev, reads, writes):
        k, v = ev
        self.last[k] = v
        for b in reads:
            if b.readers.get(k, 0) < v:
                b.readers[k] = v
        for b in writes:
            b.last_w = ev
            b.readers = {}

    def _waits(self, eng, deps):
        waits = []
        seen = self.seen[eng]
        for k, v in deps.items():
            if k == eng and eng == "pe":
                continue
            if seen.get(k, 0) >= v:
                continue
            seen[k] = v
            waits.append((k, v))
        return waits

    def op(self, eng, fn, reads=(), writes=(), sig=True):
        deps = self._deps(reads, writes)
        waits = self._waits(eng, deps)
        if not sig:
            self.ops[eng].append((waits, fn, None, 0))
            self.pending.setdefault(eng, []).append((tuple(reads), tuple(writes)))
            return
        self.cnt[eng] += 1
        ev = (eng, self.cnt[eng])
        self.semkeys.add(eng)
        self.ops[eng].append((waits, fn, eng, 1))
        for r_, w_ in self.pending.pop(eng, []):
            self._commit(ev, r_, w_)
        self._commit(ev, reads, writes)

    def dma(self, q, fn, reads=(), writes=()):
        deps = self._deps(reads, writes)
        n = self.dma_n[q]
        self.dma_n[q] += 1
        slot, rnd = n % DMA_RING, n // DMA_RING
        key = ("dma", q, slot)
        self.semkeys.add(key)
        if rnd > 0 and deps.get(key, 0) < 16 * rnd:
            deps[key] = 16 * rnd
        waits = self._waits(q, deps)
        ev = (key, 16 * (rnd + 1))
        self.ops[q].append((waits, fn, key, 16))
        self._commit(ev, reads, writes)

    def barrier(self):
        assert not any(self.pending.values()), "silent op without a following signalling op"
        for e in ENGS:
            waits = self._waits(e, dict(self.last))
            if waits:
                self.ops[e].append((waits, None, None, 0))

    def emit(self, nc, stack):
        sems = {}
        for k in sorted(self.semkeys, key=str):
            nm = "s_" + (k if isinstance(k, str) else "_".join(str(x) for x in k))
            sems[k] = stack.enter_context(nc.semaphore(nm))
        block = stack.enter_context(nc.Block())

        def run(name):
            def body(eng):
                for waits, fn, key, inc in self.ops[name]:
                    for k, v in waits:
                        eng.wait_ge(sems[k], v)
                    if fn is not None:
                        ins = fn(eng)
                        if inc:
                            ins.then_inc(sems[key], inc)
            return body
        m = {"pe": block.tensor, "act": block.scalar, "dve": block.vector,
             "pool": block.gpsimd, "sp": block.sync}
        for e in ENGS:
            if self.ops[e]:
                m[e](run(e))


class Arena:
    def __init__(self, nc, st, nbytes):
        self.cap = nbytes
        self.t = st.enter_context(nc.sbuf_tensor("arena", [128, nbytes // 4], F32))
        self.off = 0

    def reset(self):
        self.off = 0

    def _take(self, nbytes):
        nbytes = (nbytes + 63) // 64 * 64
        o = self.off
        self.off += nbytes
        assert self.off <= self.cap, (self.off, self.cap)
        return o

    def f32(self, n):
        o = self._take(n * 4)
        return self.t[:, o // 4:o // 4 + n]

    def bf16(self, n):
        assert n % 2 == 0
        o = self._take(n * 2)
        return self.t[:, o // 4:o // 4 + n // 2].bitcast(BF16)


def bc_ap(ap, dims):
    return bass.AP(tensor=ap.tensor, offset=ap.offset, ap=[list(ap.ap[0])] + [list(d) for d in dims])


def row_bcast(dram_ap_1d, n):
    return bass.AP(tensor=dram_ap_1d.tensor, offset=dram_ap_1d.offset, ap=[[0, 128], [1, n]])


def build(stages=("attn", "peer0", "conv", "peer1"), out_stage=None, dbg=False):
    nc = bass.Bass("TRN2", target_bir_lowering=False)
    P = Prog()
    st = ExitStack()

    def din(name, shape, dt=F32):
        return nc.dram_tensor(name, list(shape), dt, kind="ExternalInput").ap()

    out_stage = out_stage or stages[-1]

    def dscr(name, shape, dt=F32, stage=None):
        kind = "ExternalOutput" if (stage is not None and stage == out_stage) else "Internal"
        nm = "y" if kind == "ExternalOutput" else name
        return nc.dram_tensor(nm, list(shape), dt, kind=kind).ap()

    x = din("x", [T, D])
    mixer_g = din("mixer_norm_g", [2, D])
    ffn_g = din("ffn_norm_g", [2, D])
    consts = din("consts", [128, 5 * 128])
    rope = din("rope_cs", [2, 128, T])
    gains = din("qk_gain", [128, 2])
    if "attn" in stages:
        w_qkv = din("attn_w_qkv", [D, 6144])
        w_o = din("attn_w_o", [D, D])
    if "conv" in stages:
        w_in = din("conv_w_in", [D, 3 * D])
        conv_wb = din("conv_wb", [32, 128, 4])
        w_out = din("conv_w_out", [D, D])
    if "peer0" in stages or "peer1" in stages:
        w_query = din("peer_w_query", [2, D, 2048])
        sub_keys = din("peer_sub_keys", [2, 16, 128, 128])
        peer_u = din("peer_u", [2, NE, D])
        peer_v = din("peer_v", [2, NE, D])

    h1 = dscr("h1", [T, D], stage="attn")
    h2 = dscr("h2", [T, D], stage="peer0")
    h3 = dscr("h3", [TOWN, D], stage="conv")
    h4 = dscr("h4", [TOWN, D], stage="peer1")

    arena = Arena(nc, st, 206 * 1024)
    ps = [st.enter_context(nc.psum_tensor(f"ps{i}", [128, 512], F32)) for i in range(8)]
    bps = [Buf(f"ps{i}") for i in range(8)]

    c_f32 = arena.f32(5 * 128)
    ident_f = c_f32[:, 0:128]
    rot_f = c_f32[:, 256:384]
    ident_b = arena.bf16(128)
    ones_b = arena.bf16(128)
    gain_t = arena.f32(2)
    bconst = Buf("const")
    P.dma("sp", lambda e: e.dma_start(out=c_f32, in_=consts), writes=[bconst])
    P.dma("sp", lambda e: e.dma_start(out=gain_t, in_=gains), writes=[bconst])
    P.op("dve", lambda e: e.tensor_copy(out=ident_b, in_=c_f32[:, 0:128]), reads=[bconst], writes=[bconst])
    P.op("dve", lambda e: e.tensor_copy(out=ones_b, in_=c_f32[:, 128:256]), reads=[bconst], writes=[bconst])
    base_off = arena.off

    dumps = {}

    def dump(name, ap, buf, dt):
        if not dbg or name in dumps:
            return
        shp = list(ap.shape)
        d_ = nc.dram_tensor("dbg_" + name, shp, dt, kind="ExternalOutput").ap()
        dumps[name] = d_
        P.dma("sp", lambda e: e.dma_start(out=d_, in_=ap), reads=[buf], writes=[Buf()])

    def phase_start():
        P.barrier()
        arena.off = base_off

    def norm_block(src, g_b, bg, tb, xnT, bxnT, xt_bufs, keep_x=None):
        for ti in range(4):
            row0 = tb * TB + ti * 128
            if keep_x is not None:
                xt, bxt = keep_x[ti]
            else:
                xt, bxt = xt_bufs[ti % len(xt_bufs)]
            P.dma("sp", lambda e, xt=xt, row0=row0: e.dma_start(out=xt, in_=src[row0:row0 + 128, :]), writes=[bxt])
            junk, bjunk = norm_block.junk
            ssq, bssq = norm_block.ssq
            P.op("act", lambda e, xt=xt: e.activation(out=junk, in_=xt, func=AF.Square),
                 reads=[bxt], writes=[bjunk])
            P.op("dve", lambda e: e.reduce_sum(out=ssq[:, 0:1], in_=junk, axis=AX.X), reads=[bjunk], writes=[bssq])
            P.op("dve", lambda e: e.tensor_scalar(out=ssq[:, 1:2], in0=ssq[:, 0:1], scalar1=1.0 / D, scalar2=EPS,
                                                  op0=ALU.mult, op1=ALU.add), reads=[bssq], writes=[bssq])
            P.op("act", lambda e: e.activation(out=ssq[:, 3:4], in_=ssq[:, 1:2], func=AF.Sqrt), reads=[bssq], writes=[bssq])
            P.op("dve", lambda e: e.reciprocal(out=ssq[:, 2:3], in_=ssq[:, 3:4]), reads=[bssq], writes=[bssq])
            xn, bxn = norm_block.xn
            P.op("dve", lambda e, xt=xt: e.scalar_tensor_tensor(out=xn, in0=xt, scalar=ssq[:, 2:3], in1=g_b,
                                                                op0=ALU.mult, op1=ALU.mult),
                 reads=[bxt, bssq, bg], writes=[bxn])
            dump("xt", xt, bxt, F32); dump("ssq", ssq, bssq, F32); dump("xn", xn, bxn, BF16)
            for k8 in range(4):
                pi = 6 + (k8 % 2)
                pv = ps[pi][:].bitcast(BF16)
                for kk in range(8):
                    k = k8 * 8 + kk
                    P.op("pe", lambda e, k=k, kk=kk, pv=pv: e.transpose(pv[:, kk * 128:(kk + 1) * 128],
                                                                      xn[:, k * 128:(k + 1) * 128], ident_b),
                         reads=[bxn, bconst], writes=[bps[pi]], sig=(kk == 7))
                eng = "act" if k8 % 2 == 0 else "dve"
                dst = xnT[:, k8 * 8:(k8 + 1) * 8, ti * 128:(ti + 1) * 128]
                srcv = pv.rearrange("p (k t) -> p k t", k=8)
                if eng == "act":
                    P.op("act", lambda e, dst=dst, srcv=srcv: e.copy(out=dst, in_=srcv), reads=[bps[pi]], writes=[bxnT])
                else:
                    P.op("dve", lambda e, dst=dst, srcv=srcv: e.tensor_copy(out=dst, in_=srcv), reads=[bps[pi]], writes=[bxnT])

    def norm_setup():
        norm_block.junk = (arena.bf16(D), Buf("junk"))
        norm_block.ssq = (arena.f32(4), Buf("ssq"))
        norm_block.xn = (arena.bf16(D), Buf("xn"))

    def load_gain_row(g_dram_row):
        g_b = arena.f32(D)
        bg = Buf("g")
        P.dma("sp", lambda e: e.dma_start(out=g_b, in_=row_bcast(g_dram_row, D)), writes=[bg])
        return g_b, bg

    def wstream(loads, compute):
        n = len(loads)
        loads[0](0)
        for i in range(n):
            if i + 1 < n:
                loads[i + 1]((i + 1) % 2)
            compute(i, i % 2)

    def proj_residual(srcT, bsrcT, w_dram, res_src, dst, nblk):
        phase_start()
        ot = arena.bf16(32 * TB).rearrange("p (h t) -> p h t", h=32); bot = Buf("ot")
        wC = [arena.bf16(32 * 512).rearrange("p (k c) -> p k c", k=32) for _ in range(2)]
        bwC = [Buf("wo0"), Buf("wo1")]
        xts = [(arena.f32(D), Buf(f"xr{i}")) for i in range(4)]
        wo_v = w_dram.rearrange("(h p) c -> p h c", p=128)
        cnt = [0]
        for tb in range(nblk):
            P.dma("sp", lambda e, tb=tb: e.dma_start(out=ot, in_=srcT[:, :, tb * TB:(tb + 1) * TB].rearrange("h p t -> p h t")),
                  reads=[bsrcT], writes=[bot])
            for ti in range(4):
                r0 = tb * TB + ti * 128
                P.dma("sp", lambda e, ti=ti, r0=r0: e.dma_start(out=xts[ti][0], in_=res_src[r0:r0 + 128, :]), writes=[xts[ti][1]])

            def ld(cb):
                def f(slot):
                    P.dma("pool", lambda e: e.dma_start(out=wC[slot], in_=wo_v[:, :, cb * 512:(cb + 1) * 512]), writes=[bwC[slot]])
                return f

            def comp(cb, slot):
                for ti in range(4):
                    pq = cnt[0] % 4
                    cnt[0] += 1
                    for h in range(32):
                        P.op("pe", lambda e, h=h, ti=ti, pq=pq: e.matmul(ps[pq][:], ot[:, h, ti * 128:(ti + 1) * 128], wC[slot][:, h, :],
                                                                         start=(h == 0), stop=(h == 31)),
                                 sig=(h == 31),
                             reads=[bot, bwC[slot]], writes=[bps[pq]])
                    xs = xts[ti][0][:, cb * 512:(cb + 1) * 512]
                    P.op("dve", lambda e, xs=xs, pq=pq: e.tensor_tensor(out=xs, in0=xs, in1=ps[pq][:], op=ALU.add),
                         reads=[bps[pq]], writes=[xts[ti][1]])
            wstream([ld(cb) for cb in range(8)], comp)
            for ti in range(4):
                r0 = tb * TB + ti * 128
                P.dma("sp", lambda e, ti=ti, r0=r0: e.dma_start(out=dst[r0:r0 + 128, :], in_=xts[ti][0]), reads=[xts[ti][1]], writes=[Buf()])

    if "attn" in stages:
        QT = nc.dram_tensor("QT", [32, 128, T], BF16, kind=("ExternalOutput" if dbg else "Internal")).ap()
        KT = nc.dram_tensor("KT", [8, 128, T], BF16, kind=("ExternalOutput" if dbg else "Internal")).ap()
        VS = nc.dram_tensor("VS", [T, 1024], BF16, kind=("ExternalOutput" if dbg else "Internal")).ap()
        OT = nc.dram_tensor("OT", [32, 128, T], BF16, kind=("ExternalOutput" if dbg else "Internal")).ap()
        bQT, bKT, bVS, bOT = Buf("QT"), Buf("KT"), Buf("VS"), Buf("OT")

        phase_start()
        g_b, bg = load_gain_row(mixer_g[0])
        norm_setup()
        cs = arena.f32(2 * T).rearrange("p (c t) -> p c t", c=2)
        bcs = Buf("cs")
        P.dma("sp", lambda e: e.dma_start(out=cs, in_=rope.rearrange("c p t -> p c t")), writes=[bcs])
        xnT = arena.bf16(32 * TB).rearrange("p (k t) -> p k t", k=32)
        bxnT = Buf("xnT")
        xt_bufs = [(arena.f32(D), Buf("xt0"))]
        wt = [arena.bf16(32 * 512).rearrange("p (k c) -> p k c", k=32) for _ in range(2)]
        bwt = [Buf("wt0"), Buf("wt1")]
        sq = arena.bf16(512); bsq = Buf("sq")
        rstd = arena.f32(512); brstd = Buf("rstd")
        qn = arena.f32(512); bqn = Buf("qn")
        t1 = arena.f32(512); bt1 = Buf("t1")
        t2 = arena.f32(512); bt2 = Buf("t2")
        qo = [arena.bf16(512) for _ in range(2)]; bqo = [Buf("qo0"), Buf("qo1")]
        vo = [arena.bf16(512) for _ in range(2)]; bvo = [Buf("vo0"), Buf("vo1")]
        wq_v = w_qkv.rearrange("(k p) c -> p k c", p=128)
        cnt = {"q": 0, "v": 0}
        for tb in range(1 if dbg == 2 else NB):
            norm_block(x, g_b, bg, tb, xnT, bxnT, xt_bufs)

            dump("xnT", xnT, bxnT, BF16)

            def ld(cb):
                def f(slot):
                    P.dma("pool", lambda e: e.dma_start(out=wt[slot], in_=wq_v[:, :, cb * 512:(cb + 1) * 512]),
                          writes=[bwt[slot]])
                    dump("wt", wt[slot], bwt[slot], BF16)
                return f

            def comp(cb, slot, tb=tb):
                if cb < 10:
                    for j in range(4):
                        hb = cb * 4 + j
                        pq = cnt["q"] % 2
                        cnt["q"] += 1
                        for k in range(32):
                            P.op("pe", lambda e, k=k, j=j, pq=pq: e.matmul(ps[pq][:], wt[slot][:, k, j * 128:(j + 1) * 128],
                                                                           xnT[:, k, :], start=(k == 0), stop=(k == 31)),
                                 sig=(k == 31),
                                 reads=[bwt[slot], bxnT], writes=[bps[pq]])
                        gcol = gain_t[:, 0:1] if hb < 32 else gain_t[:, 1:2]
                        P.op("act", lambda e, pq=pq: e.activation(out=sq, in_=ps[pq][:], func=AF.Square),
                             reads=[bps[pq]], writes=[bsq])
                        P.op("pe", lambda e: e.matmul(ps[2][:], ones_b, sq, start=True, stop=True),
                             reads=[bsq, bconst], writes=[bps[2]])
                        P.op("dve", lambda e: e.tensor_scalar(out=rstd, in0=ps[2][:], scalar1=1.0 / 128, scalar2=EPS,
                                                              op0=ALU.mult, op1=ALU.add), reads=[bps[2]], writes=[brstd])
                        P.op("act", lambda e: e.activation(out=rstd, in_=rstd, func=AF.Sqrt), reads=[brstd], writes=[brstd])
                        P.op("dve", lambda e: e.reciprocal(out=rstd, in_=rstd), reads=[brstd], writes=[brstd])
                        P.op("dve", lambda e, pq=pq, gcol=gcol: e.scalar_tensor_tensor(out=qn, in0=ps[pq][:], scalar=gcol, in1=rstd,
                                                                                       op0=ALU.mult, op1=ALU.mult),
                             reads=[bps[pq], brstd, bconst], writes=[bqn])
                        dump("rstd", rstd, brstd, F32); dump("qn", qn, bqn, F32); dump("sq", sq, bsq, BF16)
                        P.op("pe", lambda e: e.matmul(ps[3][:], rot_f, qn, start=True, stop=True),
                             reads=[bqn, bconst], writes=[bps[3]])
                        tsl = slice(tb * TB, (tb + 1) * TB)
                        P.op("pool", lambda e, tsl=tsl: e.tensor_tensor(out=t1, in0=qn, in1=cs[:, 0, tsl], op=ALU.mult),
                             reads=[bqn, bcs], writes=[bt1])
                        P.op("dve", lambda e, tsl=tsl: e.tensor_tensor(out=t2, in0=ps[3][:], in1=cs[:, 1, tsl], op=ALU.mult),
                             reads=[bps[3], bcs], writes=[bt2])
                        qs = hb % 2
                        P.op("dve", lambda e, qs=qs: e.tensor_tensor(out=qo[qs], in0=t1, in1=t2, op=ALU.add),
                             reads=[bt1, bt2], writes=[bqo[qs]])
                        dump("t1", t1, bt1, F32); dump("t2", t2, bt2, F32); dump("qo", qo[qs], bqo[qs], BF16)
                        if hb < 32:
                            dst, bd = QT[hb][:, tb * TB:(tb + 1) * TB], bQT
                        else:
                            dst, bd = KT[hb - 32][:, tb * TB:(tb + 1) * TB], bKT
                        P.dma("sp", lambda e, qs=qs, dst=dst: e.dma_start(out=dst, in_=qo[qs]), reads=[bqo[qs]], writes=[bd])
                else:
                    for ti in range(4):
                        pq = 4 + cnt["v"] % 2
                        vs = cnt["v"] % 2
                        cnt["v"] += 1
                        for k in range(32):
                            P.op("pe", lambda e, k=k, ti=ti, pq=pq: e.matmul(ps[pq][:], xnT[:, k, ti * 128:(ti + 1) * 128],
                                                                             wt[slot][:, k, :], start=(k == 0), stop=(k == 31)),
                                 sig=(k == 31),
                                 reads=[bwt[slot], bxnT], writes=[bps[pq]])
                        P.op("act", lambda e, pq=pq, vs=vs: e.copy(out=vo[vs], in_=ps[pq][:]), reads=[bps[pq]], writes=[bvo[vs]])
                        dump("vo", vo[vs], bvo[vs], BF16)
                        r0 = tb * TB + ti * 128
                        c0 = (cb - 10) * 512
                        P.dma("sp", lambda e, vs=vs, r0=r0, c0=c0: e.dma_start(out=VS[r0:r0 + 128, c0:c0 + 512], in_=vo[vs]),
                              reads=[bvo[vs]], writes=[bVS])
            cbs = list(range(12)) if (tb < NBQ or dbg) else list(range(8, 12))
            wstream([ld(cb) for cb in cbs], lambda i, slot, cbs=cbs: comp(cbs[i], slot))

        if dbg in (2, 3):
            P.barrier(); P.emit(nc, st); st.close(); return nc
        phase_start()
        kt = arena.bf16(T); bkt = Buf("kt")
        vt = arena.bf16(32 * 130).rearrange("p (c d) -> p c d", c=32); bvt = Buf("vt")
        qt = arena.bf16(4 * T).rearrange("p (h t) -> p h t", h=4); bqt = Buf("qt")
        pT = [arena.bf16(512) for _ in range(3)]; bpT = [Buf(f"pT{i}") for i in range(3)]
        rden = arena.f32(4); brden = Buf("rden")
        on = arena.bf16(128 * 4).rearrange("p (h d) -> p h d", h=4); bon = Buf("on")
        oT = [arena.bf16(4 * 512).rearrange("p (h t) -> p h t", h=4) for _ in range(2)]; boT = [Buf("oT0"), Buf("oT1")]
        scale = 128.0 ** -0.5
        for g in range(8):
            P.dma("sp", lambda e, g=g: e.dma_start(out=kt, in_=KT[g]), writes=[bkt], reads=[bKT])
            P.dma("sp", lambda e, g=g: e.dma_start(out=vt[:, :, 0:128],
                                                   in_=VS[:, g * 128:(g + 1) * 128].rearrange("(c p) d -> p c d", p=128)),
                  writes=[bvt], reads=[bVS])
            P.op("pool", lambda e: e.memset(vt[:, :, 128:129], 1.0), writes=[bvt])
            P.dma("sp", lambda e, g=g: e.dma_start(out=qt, in_=QT[4 * g:4 * g + 4].rearrange("h p t -> p h t")),
                  writes=[bqt], reads=[bQT])
            NQ = NT if dbg else NBQ * 4
            seq = [(qi, c) for qi in range(NQ) for c in range(32)]

            def S_(idx):
                qi, c = seq[idx]
                sp_, pslot = idx % 2, idx % 3
                P.op("pe", lambda e: e.matmul(ps[sp_][:], kt[:, c * 128:(c + 1) * 128], qt[:, :, qi * 128:(qi + 1) * 128], start=True, stop=True),
                     reads=[bkt, bqt], writes=[bps[sp_]])
                P.op("act", lambda e: e.activation(out=pT[pslot], in_=ps[sp_][:], func=AF.Exp, scale=scale),
                     reads=[bps[sp_]], writes=[bpT[pslot]])

            def PV_(idx, g=g):
                qi, c = seq[idx]
                pslot = idx % 3
                for j in range(4):
                    P.op("pe", lambda e, j=j: e.matmul(ps[2 + j][:, 0:129], pT[pslot][:, j * 128:(j + 1) * 128], vt[:, c, 0:129],
                                                       start=(c == 0), stop=(c == 31)),
                         reads=[bpT[pslot], bvt], writes=[bps[2 + j]], sig=(j == 3))
                if c != 31:
                    return
                os_ = (qi // 4) % 2
                for j in range(4):
                    P.op("dve", lambda e, j=j: e.reciprocal(out=rden[:, j:j + 1], in_=ps[2 + j][:, 128:129]),
                         reads=[bps[2 + j]], writes=[brden])
                    P.op("dve", lambda e, j=j: e.tensor_scalar(out=on[:, j, :], in0=ps[2 + j][:, 0:128], scalar1=rden[:, j:j + 1],
                                                               scalar2=None, op0=ALU.mult),
                         reads=[bps[2 + j], brden], writes=[bon])
                pb = 6 + qi % 2
                pv = ps[pb][:].bitcast(BF16)
                for j in range(4):
                    P.op("pe", lambda e, j=j: e.transpose(pv[:, j * 128:(j + 1) * 128], on[:, j, :], ident_b),
                         reads=[bon, bconst], writes=[bps[pb]], sig=(j == 3))
                P.op("dve", lambda e: e.tensor_copy(out=oT[os_][:, :, (qi % 4) * 128:(qi % 4 + 1) * 128],
                                                    in_=pv[:, 0:512].rearrange("p (h t) -> p h t", h=4)),
                     reads=[bps[pb]], writes=[boT[os_]])
                if qi % 4 == 3:
                    t0 = (qi // 4) * 512
                    P.dma("sp", lambda e: e.dma_start(out=OT[4 * g:4 * g + 4][:, :, t0:t0 + 512].rearrange("h p t -> p h t"), in_=oT[os_]),
                          reads=[boT[os_]], writes=[bOT])
            S_(0)
            for idx in range(len(seq)):
                if idx + 1 < len(seq):
                    S_(idx + 1)
                PV_(idx)

        if dbg == 4:
            P.barrier(); P.emit(nc, st); st.close(); return nc
        proj_residual(OT, bOT, w_o, x, h1, NBQ)

    def conv_layer(hin, hout):
        UTs = nc.dram_tensor("cUT", [32, 128, T], F32, kind="Internal").ap()
        BTs = nc.dram_tensor("cBT", [32, 128, T], F32, kind="Internal").ap()
        YT = nc.dram_tensor("cYT", [32, 128, T], BF16, kind="Internal").ap()
        bUTs, bBTs, bYT = Buf("cUT"), Buf("cBT"), Buf("cYT")
        phase_start()
        g_b, bg = load_gain_row(mixer_g[1])
        norm_setup()
        xnT = arena.bf16(32 * TB).rearrange("p (k t) -> p k t", k=32); bxnT = Buf("xnT")
        xt_bufs = [(arena.f32(D), Buf("xt0"))]
        wt = [arena.bf16(32 * 512).rearrange("p (k c) -> p k c", k=32) for _ in range(2)]
        bwt = [Buf("wi0"), Buf("wi1")]
        cbuf = [arena.f32(512) for _ in range(4)]; bcbuf = [Buf(f"cb{i}") for i in range(4)]
        obuf = [arena.f32(512) for _ in range(2)]; bobuf = [Buf("ob0"), Buf("ob1")]
        wi_v = w_in.rearrange("(k p) c -> p k c", p=128)
        cnt = [0, 0]
        for tb in range(NBQ):
            norm_block(hin, g_b, bg, tb, xnT, bxnT, xt_bufs)
            items = []
            for cb4 in range(8):
                for kind in ("c", "x", "b"):
                    items.append((cb4, kind))

            def ld(it):
                cb4, kind = it
                col0 = {"b": 0, "c": D, "x": 2 * D}[kind] + cb4 * 512

                def f(slot):
                    P.dma("pool", lambda e: e.dma_start(out=wt[slot], in_=wi_v[:, :, col0:col0 + 512]), writes=[bwt[slot]])
                return f

            def comp(i, slot, tb=tb, items=items):
                cb4, kind = items[i]
                for j in range(4):
                    pq = cnt[0] % 4
                    cnt[0] += 1
                    for k in range(32):
                        P.op("pe", lambda e, k=k, j=j, pq=pq: e.matmul(ps[pq][:], wt[slot][:, k, j * 128:(j + 1) * 128], xnT[:, k, :],
                                                                       start=(k == 0), stop=(k == 31)),
                                 sig=(k == 31),
                             reads=[bwt[slot], bxnT], writes=[bps[pq]])
                    chunk = cb4 * 4 + j
                    if kind == "c":
                        P.op("act", lambda e, j=j, pq=pq: e.copy(out=cbuf[j], in_=ps[pq][:]), reads=[bps[pq]], writes=[bcbuf[j]])
                    else:
                        os_ = cnt[1] % 2
                        cnt[1] += 1
                        if kind == "x":
                            P.op("dve", lambda e, j=j, pq=pq, os_=os_: e.tensor_tensor(out=obuf[os_], in0=cbuf[j], in1=ps[pq][:], op=ALU.mult),
                                 reads=[bps[pq], bcbuf[j]], writes=[bobuf[os_]])
                            dst_, bd = UTs[chunk][:, tb * TB:(tb + 1) * TB], bUTs
                        else:
                            P.op("act", lambda e, pq=pq, os_=os_: e.copy(out=obuf[os_], in_=ps[pq][:]), reads=[bps[pq]], writes=[bobuf[os_]])
                            dst_, bd = BTs[chunk][:, tb * TB:(tb + 1) * TB], bBTs
                        P.dma("sp", lambda e, os_=os_, dst_=dst_: e.dma_start(out=dst_, in_=obuf[os_]), reads=[bobuf[os_]], writes=[bd])
            wstream([ld(it) for it in items], comp)

        phase_start()
        ub = [arena.f32(TL + 2) for _ in range(2)]; bub = [Buf("ub0"), Buf("ub1")]
        bt = [arena.f32(TL) for _ in range(2)]; bbt = [Buf("bt0"), Buf("bt1")]
        cv = arena.f32(TL); bcv = Buf("cv")
        yb = [arena.bf16(TL) for _ in range(2)]; byb = [Buf("yb0"), Buf("yb1")]
        wb = [arena.f32(4) for _ in range(2)]; bwb = [Buf("wb0"), Buf("wb1")]
        for s_ in range(2):
            P.op("pool", lambda e, s_=s_: e.memset(ub[s_][:, 0:1], 0.0), writes=[bub[s_]])
            P.op("pool", lambda e, s_=s_: e.memset(ub[s_][:, TL + 1:TL + 2], 0.0), writes=[bub[s_]])
        for ch in range(32):
            s_ = ch % 2
            P.dma("sp", lambda e, ch=ch, s_=s_: e.dma_start(out=ub[s_][:, 1:TL + 1], in_=UTs[ch][:, 0:TL]), reads=[bUTs], writes=[bub[s_]])
            P.dma("sp", lambda e, ch=ch, s_=s_: e.dma_start(out=bt[s_], in_=BTs[ch][:, 0:TL]), reads=[bBTs], writes=[bbt[s_]])
            P.dma("sp", lambda e, ch=ch, s_=s_: e.dma_start(out=wb[s_], in_=conv_wb[ch]), writes=[bwb[s_]])
            P.op("dve", lambda e, s_=s_: e.tensor_scalar(out=cv, in0=ub[s_][:, 0:TL], scalar1=wb[s_][:, 0:1], scalar2=wb[s_][:, 3:4],
                                                         op0=ALU.mult, op1=ALU.add), reads=[bub[s_], bwb[s_]], writes=[bcv])
            P.op("dve", lambda e, s_=s_: e.scalar_tensor_tensor(out=cv, in0=ub[s_][:, 1:TL + 1], scalar=wb[s_][:, 1:2], in1=cv,
                                                                op0=ALU.mult, op1=ALU.add), reads=[bub[s_], bwb[s_]], writes=[bcv])
            P.op("dve", lambda e, s_=s_: e.scalar_tensor_tensor(out=cv, in0=ub[s_][:, 2:TL + 2], scalar=wb[s_][:, 2:3], in1=cv,
                                                                op0=ALU.mult, op1=ALU.add), reads=[bub[s_], bwb[s_]], writes=[bcv])
            P.op("pool", lambda e, s_=s_: e.tensor_tensor(out=yb[s_], in0=cv, in1=bt[s_], op=ALU.mult), reads=[bcv, bbt[s_]], writes=[byb[s_]])
            P.dma("sp", lambda e, ch=ch, s_=s_: e.dma_start(out=YT[ch][:, 0:TL], in_=yb[s_]), reads=[byb[s_]], writes=[bYT])
        proj_residual(YT, bYT, w_out, hin, hout, NBO)

    def peer_layer(L, hin, hout, nblk):
        UT = nc.dram_tensor(f"pUT{L}", [32, 128, 32 * 512], BF16, kind="Internal").ap()
        XT = nc.dram_tensor(f"pXT{L}", [NB, 128, 32 * TB], BF16, kind="Internal").ap()
        SS = nc.dram_tensor(f"pSS{L}", [T, 2048], F32, kind="Internal").ap()
        GS = nc.dram_tensor(f"pGS{L}", [T, NE], BF16, kind="Internal").ap()
        bUT, bXT, bSS, bGS = Buf("pUT"), Buf("pXT"), Buf("pSS"), Buf("pGS")

        phase_start()
        ub = [arena.bf16(4 * D).rearrange("p (c d) -> p c d", c=4) for _ in range(2)]; bub = [Buf("ub0"), Buf("ub1")]
        utb = [arena.bf16(32 * 512).rearrange("p (k e) -> p k e", k=32) for _ in range(2)]; butb = [Buf("utb0"), Buf("utb1")]
        ncp = [0]

        def ld0(eb2):
            def f(slot):
                P.dma("pool", lambda e: e.dma_start(out=ub[slot], in_=peer_u[L][eb2 * 512:(eb2 + 1) * 512, :].rearrange("(c p) d -> p c d", p=128)),
                      writes=[bub[slot]])
            return f

        def comp0(eb2, slot):
            for c in range(4):
                for k8 in range(4):
                    pi = ncp[0] % 4
                    ncp[0] += 1
                    pv = ps[pi][:].bitcast(BF16)
                    for kk in range(8):
                        k = k8 * 8 + kk
                        P.op("pe", lambda e, k=k, kk=kk, c=c, pv=pv: e.transpose(pv[:, kk * 128:(kk + 1) * 128], ub[slot][:, c, k * 128:(k + 1) * 128], ident_b),
                             reads=[bub[slot], bconst], writes=[bps[pi]], sig=(kk == 7))
                    dst_ = utb[slot][:, k8 * 8:(k8 + 1) * 8, c * 128:(c + 1) * 128]
                    srcv = pv.rearrange("p (k t) -> p k t", k=8)
                    if ncp[0] % 2 == 0:
                        P.op("act", lambda e, dst_=dst_, srcv=srcv: e.copy(out=dst_, in_=srcv), reads=[bps[pi]], writes=[butb[slot]])
                    else:
                        P.op("dve", lambda e, dst_=dst_, srcv=srcv: e.tensor_copy(out=dst_, in_=srcv), reads=[bps[pi]], writes=[butb[slot]])
            P.dma("sp", lambda e: e.dma_start(out=UT[eb2].rearrange("p (k e) -> p k e", k=32), in_=utb[slot]), reads=[butb[slot]], writes=[bUT])
        wstream([ld0(eb2) for eb2 in range(32)], comp0)

        phase_start()
        g_b, bg = load_gain_row(ffn_g[L])
        norm_setup()
        xnT = arena.bf16(32 * TB).rearrange("p (k t) -> p k t", k=32); bxnT = Buf("xnT")
        xt_bufs = [(arena.f32(D), Buf("xt0"))]
        wq = [arena.bf16(32 * 256).rearrange("p (k c) -> p k c", k=32) for _ in range(2)]; bwq = [Buf("wq0"), Buf("wq1")]
        qT = arena.f32(16 * TB).rearrange("p (h t) -> p h t", h=16); bqT = Buf("qT")
        sk = arena.f32(16 * 128).rearrange("p (h d) -> p h d", h=16); bsk = Buf("sk")
        skT = arena.f32(16 * 128).rearrange("p (h n) -> p h n", h=16); bskT = Buf("skT")
        s_t = [arena.f32(2048) for _ in range(2)]; bs_t = [Buf("st0"), Buf("st1")]
        P.dma("sp", lambda e: e.dma_start(out=sk, in_=sub_keys[L].rearrange("h n d -> n h d")), writes=[bsk])
        for hp in range(16):
            pi = 4 + (hp // 4) % 2
            P.op("pe", lambda e, hp=hp, pi=pi: e.transpose(ps[pi][:, (hp % 4) * 128:(hp % 4 + 1) * 128], sk[:, hp, :], ident_f),
                 reads=[bsk, bconst], writes=[bps[pi]])
            if hp % 4 == 3:
                P.op("dve", lambda e, hp=hp, pi=pi: e.tensor_copy(out=skT[:, hp - 3:hp + 1, :], in_=ps[pi][:].rearrange("p (h n) -> p h n", h=4)),
                     reads=[bps[pi]], writes=[bskT])
        wq_v = w_query[L].rearrange("(k p) c -> p k c", p=128)
        cn = [0, 0]
        for tb in range(nblk):
            norm_block(hin, g_b, bg, tb, xnT, bxnT, xt_bufs)
            P.dma("sp", lambda e, tb=tb: e.dma_start(out=XT[tb].rearrange("p (k t) -> p k t", k=32), in_=xnT), reads=[bxnT], writes=[bXT])

            def ld(ct):
                def f(slot):
                    P.dma("pool", lambda e: e.dma_start(out=wq[slot], in_=wq_v[:, :, ct * 256:(ct + 1) * 256]), writes=[bwq[slot]])
                return f

            def comp(ct, slot):
                for j in range(2):
                    hp = ct * 2 + j
                    pq = cn[0] % 2
                    cn[0] += 1
                    for k in range(32):
                        P.op("pe", lambda e, k=k, j=j, pq=pq: e.matmul(ps[pq][:], wq[slot][:, k, j * 128:(j + 1) * 128], xnT[:, k, :],
                                                                       start=(k == 0), stop=(k == 31)),
                                 sig=(k == 31),
                             reads=[bwq[slot], bxnT], writes=[bps[pq]])
                    if hp % 2 == 0:
                        P.op("act", lambda e, hp=hp, pq=pq: e.copy(out=qT[:, hp, :], in_=ps[pq][:]), reads=[bps[pq]], writes=[bqT])
                    else:
                        P.op("dve", lambda e, hp=hp, pq=pq: e.tensor_copy(out=qT[:, hp, :], in_=ps[pq][:]), reads=[bps[pq]], writes=[bqT])
            wstream([ld(ct) for ct in range(8)], comp)
            for ti in range(4):
                ss = cn[1] % 2
                cn[1] += 1
                for hp4 in range(4):
                    pi = 2 + hp4 % 2
                    for j in range(4):
                        hp = hp4 * 4 + j
                        P.op("pe", lambda e, hp=hp, j=j, ti=ti, pi=pi: e.matmul(ps[pi][:, j * 128:(j + 1) * 128], qT[:, hp, ti * 128:(ti + 1) * 128],
                                                                              skT[:, hp, :], start=True, stop=True),
                             reads=[bqT, bskT], writes=[bps[pi]])
                    P.op("dve", lambda e, hp4=hp4, pi=pi, ss=ss: e.tensor_copy(out=s_t[ss][:, hp4 * 512:(hp4 + 1) * 512], in_=ps[pi][:]),
                         reads=[bps[pi]], writes=[bs_t[ss]])
                r0 = tb * TB + ti * 128
                P.dma("sp", lambda e, r0=r0, ss=ss: e.dma_start(out=SS[r0:r0 + 128, :], in_=s_t[ss]), reads=[bs_t[ss]], writes=[bSS])

        phase_start()
        NEG = -1.0e30
        sb = [arena.f32(2048) for _ in range(2)]; bsb = [Buf("sb0"), Buf("sb1")]
        top = arena.f32(256); btop = Buf("top")
        tmp = arena.f32(128); btmp = Buf("tmp")
        cand = arena.f32(8 * 256); bcand = Buf("cand")
        tmpc = arena.f32(256); btmpc = Buf("tmpc")
        best = arena.f32(8 * 24); bbest = Buf("best")
        sm = arena.f32(64); bsm = Buf("sm")
        ex = arena.f32(128); bex = Buf("ex")
        pen = arena.f32(128); bpen = Buf("pen")
        d1 = arena.f32(8 * 128); bd1 = Buf("d1")
        s2m = arena.f32(8 * 128); bs2m = Buf("s2m")
        Z3 = [arena.f32(2048) for _ in range(2)]; bZ3 = [Buf("Z30"), Buf("Z31")]
        E3 = [arena.f32(2048) for _ in range(2)]; bE3 = [Buf("E30"), Buf("E31")]
        M3 = [arena.f32(2048) for _ in range(2)]; bM3 = [Buf("M30"), Buf("M31")]
        Gc = [arena.f32(2048) for _ in range(2)]; bGc = [Buf("Gc0"), Buf("Gc1")]
        gout = [arena.bf16(NE) for _ in range(2)]; bgout = [Buf("go0"), Buf("go1")]
        nz = [0]
        for tile in range(nblk * 4):
            ss = tile % 2
            s = sb[ss]
            bs = bsb[ss]
            r0 = tile * 128
            P.dma("sp", lambda e, r0=r0, s=s: e.dma_start(out=s, in_=SS[r0:r0 + 128, :]), reads=[bSS], writes=[bs])
            for hp in range(16):
                sv = s[:, hp * 128:(hp + 1) * 128]
                P.op("dve", lambda e, hp=hp, sv=sv: e.max(out=top[:, hp * 16:hp * 16 + 8], in_=sv), reads=[bs], writes=[btop])
                P.op("dve", lambda e, hp=hp, sv=sv: e.match_replace(out=tmp, in_to_replace=top[:, hp * 16:hp * 16 + 8], in_values=sv, imm_value=NEG),
                     reads=[bs, btop], writes=[btmp])
                P.op("dve", lambda e, hp=hp: e.max(out=top[:, hp * 16 + 8:hp * 16 + 16], in_=tmp), reads=[btmp], writes=[btop])
            cand4 = cand.rearrange("p (h a b) -> p h a b", h=8, a=16)
            P.op("dve", lambda e, cand4=cand4: e.tensor_tensor(out=cand4, in0=bc_ap(top, [[32, 8], [1, 16], [0, 16]]),
                                                              in1=bc_ap(top[:, 16:], [[32, 8], [0, 16], [1, 16]]), op=ALU.add),
                 reads=[btop], writes=[bcand])
            for h in range(8):
                cvw = cand[:, h * 256:(h + 1) * 256]
                P.op("dve", lambda e, h=h, cvw=cvw: e.max(out=best[:, h * 24:h * 24 + 8], in_=cvw), reads=[bcand], writes=[bbest])
                P.op("dve", lambda e, h=h, cvw=cvw: e.match_replace(out=tmpc, in_to_replace=best[:, h * 24:h * 24 + 8], in_values=cvw, imm_value=NEG),
                     reads=[bcand, bbest], writes=[btmpc])
                P.op("dve", lambda e, h=h: e.max(out=best[:, h * 24 + 8:h * 24 + 16], in_=tmpc), reads=[btmpc], writes=[bbest])
                P.op("dve", lambda e, h=h: e.match_replace(out=tmpc, in_to_replace=best[:, h * 24 + 8:h * 24 + 16], in_values=tmpc, imm_value=NEG),
                     reads=[bbest, btmpc], writes=[btmpc])
                P.op("dve", lambda e, h=h: e.max(out=best[:, h * 24 + 16:h * 24 + 24], in_=tmpc), reads=[btmpc], writes=[bbest])
            b3 = best.rearrange("p (h k) -> p h k", h=8)
            negm, tau, tm, Zs, rZ, negtau = [sm[:, i * 8:(i + 1) * 8] for i in range(6)]
            P.op("dve", lambda e: e.tensor_scalar(out=negm, in0=b3[:, :, 0], scalar1=-1.0, scalar2=None, op0=ALU.mult), reads=[bbest], writes=[bsm])
            P.op("dve", lambda e: e.tensor_tensor(out=tau, in0=b3[:, :, 15], in1=b3[:, :, 16], op=ALU.add), reads=[bbest], writes=[bsm])
            P.op("dve", lambda e: e.tensor_scalar(out=tau, in0=tau, scalar1=0.5, scalar2=None, op0=ALU.mult), reads=[bsm], writes=[bsm])
            P.op("dve", lambda e: e.tensor_tensor(out=tm, in0=tau, in1=negm, op=ALU.add), reads=[bsm], writes=[bsm])
            P.op("dve", lambda e: e.tensor_scalar(out=negtau, in0=tau, scalar1=-1.0, scalar2=None, op0=ALU.mult), reads=[bsm], writes=[bsm])
            for h in range(8):
                P.op("act", lambda e, h=h: e.activation(out=ex[:, h * 16:(h + 1) * 16], in_=best[:, h * 24:h * 24 + 16], func=AF.Exp, bias=negm[:, h:h + 1]),
                     reads=[bbest, bsm], writes=[bex])
                P.op("dve", lambda e, h=h: e.reduce_sum(out=Zs[:, h:h + 1], in_=ex[:, h * 16:(h + 1) * 16], axis=AX.X), reads=[bex], writes=[bsm])
            P.op("dve", lambda e: e.reciprocal(out=rZ, in_=Zs), reads=[bsm], writes=[bsm])
            for h in range(8):
                for p_ in range(2):
                    hp = 2 * h + p_
                    sv = s[:, hp * 128:(hp + 1) * 128]
                    thr = top[:, hp * 16 + 15:hp * 16 + 16]
                    P.op("dve", lambda e, sv=sv, thr=thr: e.tensor_scalar(out=pen, in0=sv, scalar1=thr, scalar2=None, op0=ALU.is_ge),
                         reads=[bs, btop], writes=[bpen])
                    P.op("dve", lambda e: e.tensor_scalar(out=pen, in0=pen, scalar1=-1.0, scalar2=1.0e4, op0=ALU.add, op1=ALU.mult),
                         reads=[bpen], writes=[bpen])
                    if p_ == 0:
                        dv = d1[:, h * 128:(h + 1) * 128]
                        P.op("dve", lambda e, sv=sv, h=h, dv=dv: e.scalar_tensor_tensor(out=dv, in0=sv, scalar=negtau[:, h:h + 1], in1=pen,
                                                                                    op0=ALU.add, op1=ALU.add), reads=[bs, bsm, bpen], writes=[bd1])
                    else:
                        dv = s2m[:, h * 128:(h + 1) * 128]
                        P.op("dve", lambda e, sv=sv, dv=dv: e.tensor_tensor(out=dv, in0=sv, in1=pen, op=ALU.add), reads=[bs, bpen], writes=[bs2m])
            go = gout[tile % 2]
            bgo = bgout[tile % 2]
            for cc in range(8):
                gs = cc % 2
                for h in range(8):
                    zs = nz[0] % 2
                    nz[0] += 1
                    z3v = Z3[zs].rearrange("p (c i) -> p c i", c=16)
                    in0 = bc_ap(s2m[:, h * 128:(h + 1) * 128], [[0, 16], [1, 128]])
                    in1 = bc_ap(d1[:, h * 128 + cc * 16:h * 128 + cc * 16 + 16], [[1, 16], [0, 128]])
                    P.op("pool", lambda e, z3v=z3v, in0=in0, in1=in1: e.tensor_tensor(out=z3v, in0=in0, in1=in1, op=ALU.add),
                         reads=[bs2m, bd1], writes=[bZ3[zs]])
                    P.op("act", lambda e, zs=zs, h=h: e.activation(out=E3[zs], in_=Z3[zs], func=AF.Exp, bias=tm[:, h:h + 1]),
                         reads=[bZ3[zs], bsm], writes=[bE3[zs]])
                    P.op("dve", lambda e, zs=zs: e.scalar_tensor_tensor(out=M3[zs], in0=Z3[zs], scalar=0.0, in1=E3[zs], op0=ALU.is_ge, op1=ALU.mult),
                         reads=[bZ3[zs], bE3[zs]], writes=[bM3[zs]])
                    if h == 0:
                        P.op("dve", lambda e, zs=zs, gs=gs: e.tensor_scalar(out=Gc[gs], in0=M3[zs], scalar1=rZ[:, 0:1], scalar2=None, op0=ALU.mult),
                             reads=[bM3[zs], bsm], writes=[bGc[gs]])
                    else:
                        P.op("dve", lambda e, zs=zs, gs=gs, h=h: e.scalar_tensor_tensor(out=Gc[gs], in0=M3[zs], scalar=rZ[:, h:h + 1], in1=Gc[gs],
                                                                                      op0=ALU.mult, op1=ALU.add),
                             reads=[bM3[zs], bsm], writes=[bGc[gs]])
                P.op("act", lambda e, go=go, cc=cc, gs=gs: e.copy(out=go[:, cc * 2048:(cc + 1) * 2048], in_=Gc[gs]), reads=[bGc[gs]], writes=[bgo])
            P.dma("sp", lambda e, r0=r0, go=go: e.dma_start(out=GS[r0:r0 + 128, :], in_=go), reads=[bgo], writes=[bGS])

        phase_start()
        xnT2 = arena.bf16(32 * TB).rearrange("p (k t) -> p k t", k=32); bxn2 = Buf("xnT2")
        acc = arena.f32(4 * D).rearrange("p (i d) -> p i d", i=4); bacc = [Buf(f"acc{i}") for i in range(4)]
        ut = [arena.bf16(32 * 256).rearrange("p (k e) -> p k e", k=32) for _ in range(2)]; but = [Buf("ut0"), Buf("ut1")]
        vt = [arena.bf16(2 * D).rearrange("p (c d) -> p c d", c=2) for _ in range(2)]; bvt = [Buf("vt0"), Buf("vt1")]
        gt = [arena.bf16(4 * 256).rearrange("p (i e) -> p i e", i=4) for _ in range(2)]; bgt = [Buf("gt0"), Buf("gt1")]
        ga = [arena.f32(256) for _ in range(2)]; bga = [Buf("ga0"), Buf("ga1")]
        wv = [arena.bf16(256) for _ in range(2)]; bwv = [Buf("w0"), Buf("w1")]
        wT = [arena.bf16(256).rearrange("p (c t) -> p c t", c=2) for _ in range(2)]; bwT = [Buf("wT0"), Buf("wT1")]
        hres = arena.f32(D); bhres = Buf("hres")
        UTv = UT.rearrange("b p (k e) -> b p k e", k=32)
        GSv = GS.rearrange("(i p) e -> p i e", p=128)
        n2 = [0, 0, 0]
        for tb in range(nblk):
            P.dma("sp", lambda e, tb=tb: e.dma_start(out=xnT2, in_=XT[tb].rearrange("p (k t) -> p k t", k=32)), reads=[bXT], writes=[bxn2])

            def ld(eb, tb=tb):
                def f(slot):
                    P.dma("sp", lambda e: e.dma_start(out=ut[slot], in_=UTv[eb // 2][:, :, (eb % 2) * 256:(eb % 2 + 1) * 256]), reads=[bUT], writes=[but[slot]])
                    P.dma("pool", lambda e: e.dma_start(out=vt[slot], in_=peer_v[L][eb * 256:(eb + 1) * 256, :].rearrange("(c p) d -> p c d", p=128)),
                          writes=[bvt[slot]])
                    P.dma("sp", lambda e: e.dma_start(out=gt[slot], in_=GSv[:, tb * 4:(tb + 1) * 4, eb * 256:(eb + 1) * 256]), reads=[bGS], writes=[bgt[slot]])
                return f

            def stage1(eb, ti, pa):
                slot = eb % 2
                for k in range(32):
                    P.op("pe", lambda e, k=k: e.matmul(ps[pa][:, 0:256], xnT2[:, k, ti * 128:(ti + 1) * 128], ut[slot][:, k, :],
                                                       start=(k == 0), stop=(k == 31)),
                         reads=[bxn2, but[slot]], writes=[bps[pa]], sig=(k == 31))
                P.op("act", lambda e: e.activation(out=ga[pa], in_=ps[pa][:, 0:256], func=AF.Gelu), reads=[bps[pa]], writes=[bga[pa]])
                P.op("dve", lambda e: e.tensor_tensor(out=wv[pa], in0=ga[pa], in1=gt[slot][:, ti, :], op=ALU.mult),
                     reads=[bga[pa], bgt[slot]], writes=[bwv[pa]])

            def stage2(eb, ti, pa):
                slot = eb % 2
                pvw = ps[2][:].bitcast(BF16)
                for c in range(2):
                    P.op("pe", lambda e, c=c: e.transpose(pvw[:, c * 128:(c + 1) * 128], wv[pa][:, c * 128:(c + 1) * 128], ident_b),
                         reads=[bwv[pa], bconst], writes=[bps[2]], sig=(c == 1))
                P.op("act", lambda e: e.copy(out=wT[pa], in_=pvw[:, 0:256].rearrange("p (c t) -> p c t", c=2)),
                     reads=[bps[2]], writes=[bwT[pa]])
                for cb in range(8):
                    po = 3 + n2[1] % 5
                    n2[1] += 1
                    for c in range(2):
                        P.op("pe", lambda e, c=c, cb=cb, po=po: e.matmul(ps[po][:], wT[pa][:, c, :], vt[slot][:, c, cb * 512:(cb + 1) * 512],
                                                                        start=(c == 0), stop=(c == 1)),
                             reads=[bwT[pa], bvt[slot]], writes=[bps[po]], sig=(c == 1))
                    av = acc[:, ti, cb * 512:(cb + 1) * 512]
                    if eb == 0:
                        P.op("dve", lambda e, av=av, po=po: e.tensor_copy(out=av, in_=ps[po][:]), reads=[bps[po]], writes=[bacc[ti]])
                    else:
                        P.op("dve", lambda e, av=av, po=po: e.tensor_tensor(out=av, in0=av, in1=ps[po][:], op=ALU.add), reads=[bps[po]], writes=[bacc[ti]])

            pairs = [(eb, ti) for eb in range(64) for ti in range(4)]
            ld(0)(0)
            stage1(0, 0, 0)
            for idx, (eb, ti) in enumerate(pairs):
                if ti == 0 and eb + 1 < 64:
                    ld(eb + 1)((eb + 1) % 2)
                if idx + 1 < len(pairs):
                    stage1(pairs[idx + 1][0], pairs[idx + 1][1], (idx + 1) % 2)
                stage2(eb, ti, idx % 2)
            for ti in range(4):
                r0 = tb * TB + ti * 128
                P.dma("sp", lambda e, r0=r0: e.dma_start(out=hres, in_=hin[r0:r0 + 128, :]), writes=[bhres])
                P.op("dve", lambda e, ti=ti: e.tensor_tensor(out=hres, in0=hres, in1=acc[:, ti, :], op=ALU.add), reads=[bacc[ti]], writes=[bhres])
                P.dma("sp", lambda e, r0=r0: e.dma_start(out=hout[r0:r0 + 128, :], in_=hres), reads=[bhres], writes=[Buf()])

    cur = x
    if "attn" in stages:
        cur = h1
    if "peer0" in stages:
        peer_layer(0, cur, h2, NBQ)
        cur = h2
    if "conv" in stages:
        conv_layer(cur, h3)
        cur = h3
    if "peer1" in stages:
        peer_layer(1, cur, h4, NBO)
        cur = h4

    P.barrier()
    P.emit(nc, st)
    st.close()
    return nc


def host_consts():
    ident = np.eye(128, dtype=np.float32)
    ones = np.ones((128, 128), np.float32)
    rot = np.zeros((128, 128), np.float32)
    for m in range(128):
        half = (m % 64) // 32
        if half == 0:
            rot[m + 32, m] = -1.0
        else:
            rot[m - 32, m] = 1.0
    consts = np.concatenate([ident, ones, rot, np.zeros((128, 256), np.float32)], axis=1)
    pos = np.arange(T)
    row = (pos // 64).astype(np.float32)
    col = (pos % 64).astype(np.float32)
    inv_freq = (10000.0 ** (-np.arange(0, 64, 2, dtype=np.float32) / 64)).astype(np.float32)
    ang = np.zeros((128, T), np.float32)
    for d in range(128):
        axis = d // 64
        f = d % 32
        ang[d] = (row if axis == 0 else col) * inv_freq[f]
    rope = np.stack([np.cos(ang), np.sin(ang)]).astype(np.float32)
    return consts, rope


def make_in_maps(inputs, ncores, stages):
    consts, rope = host_consts()
    maps = []
    gains = np.stack([inputs["attn_q_gain"][0], inputs["attn_k_gain"][0]], axis=1).astype(np.float32)
    for c in range(ncores):
        b, half = (c // 2, c % 2) if ncores == 8 else (c, 0)
        xs = inputs["x"][b]
        rp = rope
        if half == 1:
            xs = xs[::-1]
            rp = rope[:, :, ::-1]
        m = {"x": np.ascontiguousarray(xs), "mixer_norm_g": inputs["mixer_norm_g"],
             "ffn_norm_g": inputs["ffn_norm_g"], "consts": consts, "rope_cs": np.ascontiguousarray(rp), "qk_gain": gains}
        if "attn" in stages:
            m["attn_w_qkv"] = inputs["attn_w_qkv"][0]
            m["attn_w_o"] = inputs["attn_w_o"][0]
        if "conv" in stages:
            m["conv_w_in"] = inputs["conv_w_in"][0]
            m["conv_w_out"] = inputs["conv_w_out"][0]
            cw = inputs["conv_w"][0]
            if half == 1:
                cw = cw[::-1]
            wb = np.concatenate([cw, inputs["conv_b"]], axis=0)
            m["conv_wb"] = np.ascontiguousarray(wb.T.reshape(32, 128, 4))
        if "peer0" in stages or "peer1" in stages:
            m["peer_w_query"] = inputs["peer_w_query"]
            m["peer_sub_keys"] = np.ascontiguousarray(inputs["peer_sub_keys"].reshape(2, 16, 128, 128))
            m["peer_u"] = inputs["peer_u"]
            m["peer_v"] = inputs["peer_v"]
        maps.append(m)
    return maps


def kernel(**inputs):
    inputs = {k: np.asarray(v) for k, v in inputs.items()}
    stages = ("attn", "peer0", "conv", "peer1")
    nc = build(stages)
    maps = make_in_maps(inputs, NCORES, stages)
    res = run_bass_kernel_spmd(nc, maps, core_ids=list(range(NCORES)))
    out = np.empty((4, T, D), np.float32)
    for c in range(NCORES):
        b, half = c // 2, c % 2
        y = res.results[c]["y"]
        if half == 0:
            out[b, :TOWN] = y
        else:
            out[b, TOWN:] = y[::-1]
    return out
```

```python
from contextlib import ExitStack
import numpy as np
import concourse.bass as bass
import concourse.mybir as mybir
from concourse.bass_utils import run_bass_kernel_spmd

F32 = mybir.dt.float32
BF16 = mybir.dt.bfloat16
ALU = mybir.AluOpType
AF = mybir.ActivationFunctionType
AX = mybir.AxisListType

ENGS = ["pe", "act", "dve", "pool", "sp"]
DMA_RING = 8
SAME_ENGINE_SYNC = True

D = 4096
T = 4096
NT = T // 128
TB = 512
NB = T // TB
NE = 16384
EPS = 1e-6
NCORES = 8
NBQ = 5
NBO = 4
TL = NBQ * TB
TOWN = NBO * TB


class Buf:
    __slots__ = ("name", "last_w", "readers")

    def __init__(self, name=""):
        self.name = name
        self.last_w = None
        self.readers = {}


class Prog:
    def __init__(self):
        self.ops = {e: [] for e in ENGS}
        self.cnt = {e: 0 for e in ENGS}
        self.seen = {e: {} for e in ENGS}
        self.dma_n = {e: 0 for e in ENGS}
        self.last = {}
        self.semkeys = set()
        self.pending = {}

    def _deps(self, reads, writes):
        deps = {}

        def add(k, v):
            if deps.get(k, 0) < v:
                deps[k] = v
        for b in reads:
            if b.last_w is not None:
                add(*b.last_w)
        for b in writes:
            if b.last_w is not None:
                add(*b.last_w)
            for k, v in b.readers.items():
                add(k, v)
        return deps

    def _commit(self, ev, reads, writes):
        k, v = ev
        self.last[k] = v
        for b in reads:
            if b.readers.get(k, 0) < v:
                b.readers[k] = v
        for b in writes:
            b.last_w = ev
            b.readers = {}

    def _waits(self, eng, deps):
        waits = []
        seen = self.seen[eng]
        for k, v in deps.items():
            if k == eng and (eng == "pe" or not SAME_ENGINE_SYNC):
                continue
            if seen.get(k, 0) >= v:
                continue
            seen[k] = v
            waits.append((k, v))
        return waits

    def op(self, eng, fn, reads=(), writes=(), sig=True):
        deps = self._deps(reads, writes)
        waits = self._waits(eng, deps)
        if not sig:
            self.ops[eng].append((waits, fn, None, 0))
            self.pending.setdefault(eng, []).append((tuple(reads), tuple(writes)))
            return
        self.cnt[eng] += 1
        ev = (eng, self.cnt[eng])
        self.semkeys.add(eng)
        self.ops[eng].append((waits, fn, eng, 1))
        for r_, w_ in self.pending.pop(eng, []):
            self._commit(ev, r_, w_)
        self._commit(ev, reads, writes)

    def dma(self, q, fn, reads=(), writes=()):
        deps = self._deps(reads, writes)
        n = self.dma_n[q]
        self.dma_n[q] += 1
        slot, rnd = n % DMA_RING, n // DMA_RING
        key = ("dma", q, slot)
        self.semkeys.add(key)
        if rnd > 0 and deps.get(key, 0) < 16 * rnd:
            deps[key] = 16 * rnd
        waits = self._waits(q, deps)
        ev = (key, 16 * (rnd + 1))
        self.ops[q].append((waits, fn, key, 16))
        self._commit(ev, reads, writes)

    def barrier(self):
        assert not any(self.pending.values()), "silent op without a following signalling op"
        for e in ENGS:
            waits = self._waits(e, dict(self.last))
            if waits:
                self.ops[e].append((waits, None, None, 0))

    def emit(self, nc, stack):
        sems = {}
        for k in sorted(self.semkeys, key=str):
            nm = "s_" + (k if isinstance(k, str) else "_".join(str(x) for x in k))
            sems[k] = stack.enter_context(nc.semaphore(nm))
        block = stack.enter_context(nc.Block())

        def run(name):
            def body(eng):
                for waits, fn, key, inc in self.ops[name]:
                    for k, v in waits:
                        eng.wait_ge(sems[k], v)
                    if fn is not None:
                        ins = fn(eng)
                        if inc:
                            ins.then_inc(sems[key], inc)
            return body
        m = {"pe": block.tensor, "act": block.scalar, "dve": block.vector,
             "pool": block.gpsimd, "sp": block.sync}
        for e in ENGS:
            if self.ops[e]:
                m[e](run(e))


class Arena:
    def __init__(self, nc, st, nbytes):
        self.cap = nbytes
        self.t = st.enter_context(nc.sbuf_tensor("arena", [128, nbytes // 4], F32))
        self.off = 0

    def reset(self):
        self.off = 0

    def _take(self, nbytes):
        nbytes = (nbytes + 63) // 64 * 64
        o = self.off
        self.off += nbytes
        assert self.off <= self.cap, (self.off, self.cap)
        return o

    def f32(self, n):
        o = self._take(n * 4)
        return self.t[:, o // 4:o // 4 + n]

    def bf16(self, n):
        assert n % 2 == 0
        o = self._take(n * 2)
        return self.t[:, o // 4:o // 4 + n // 2].bitcast(BF16)


def bc_ap(ap, dims):
    return bass.AP(tensor=ap.tensor, offset=ap.offset, ap=[list(ap.ap[0])] + [list(d) for d in dims])


def row_bcast(dram_ap_1d, n):
    return bass.AP(tensor=dram_ap_1d.tensor, offset=dram_ap_1d.offset, ap=[[0, 128], [1, n]])


def build(stages=("attn", "peer0", "conv", "peer1"), out_stage=None, dbg=False):
    nc = bass.Bass("TRN2", target_bir_lowering=False)
    P = Prog()
    st = ExitStack()

    def din(name, shape, dt=F32):
        return nc.dram_tensor(name, list(shape), dt, kind="ExternalInput").ap()

    out_stage = out_stage or stages[-1]

    def dscr(name, shape, dt=F32, stage=None):
        kind = "ExternalOutput" if (stage is not None and stage == out_stage) else "Internal"
        nm = "y" if kind == "ExternalOutput" else name
        return nc.dram_tensor(nm, list(shape), dt, kind=kind).ap()

    x = din("x", [T, D])
    mixer_g = din("mixer_norm_g", [2, D])
    ffn_g = din("ffn_norm_g", [2, D])
    consts = din("consts", [128, 5 * 128])
    rope = din("rope_cs", [2, 128, T])
    gains = din("qk_gain", [128, 2])
    if "attn" in stages:
        w_qkv = din("attn_w_qkv", [D, 6144])
        w_o = din("attn_w_o", [D, D])
    if "conv" in stages:
        w_in = din("conv_w_in", [D, 3 * D])
        conv_wb = din("conv_wb", [32, 128, 4])
        w_out = din("conv_w_out", [D, D])
    if "peer0" in stages or "peer1" in stages:
        w_query = din("peer_w_query", [2, D, 2048])
        sub_keys = din("peer_sub_keys", [2, 16, 128, 128])
        peer_u = din("peer_u", [2, NE, D])
        peer_v = din("peer_v", [2, NE, D])

    h1 = dscr("h1", [T, D], stage="attn")
    h2 = dscr("h2", [T, D], stage="peer0")
    h3 = dscr("h3", [TOWN, D], stage="conv")
    h4 = dscr("h4", [TOWN, D], stage="peer1")

    arena = Arena(nc, st, 206 * 1024)
    ps = [st.enter_context(nc.psum_tensor(f"ps{i}", [128, 512], F32)) for i in range(8)]
    bps = [Buf(f"ps{i}") for i in range(8)]

    c_f32 = arena.f32(5 * 128)
    ident_f = c_f32[:, 0:128]
    rot_f = c_f32[:, 256:384]
    ident_b = arena.bf16(128)
    ones_b = arena.bf16(128)
    gain_t = arena.f32(2)
    bconst = Buf("const")
    P.dma("sp", lambda e: e.dma_start(out=c_f32, in_=consts), writes=[bconst])
    P.dma("sp", lambda e: e.dma_start(out=gain_t, in_=gains), writes=[bconst])
    P.op("dve", lambda e: e.tensor_copy(out=ident_b, in_=c_f32[:, 0:128]), reads=[bconst], writes=[bconst])
    P.op("dve", lambda e: e.tensor_copy(out=ones_b, in_=c_f32[:, 128:256]), reads=[bconst], writes=[bconst])
    base_off = arena.off

    dumps = {}

    def dump(name, ap, buf, dt):
        if not dbg or name in dumps:
            return
        shp = list(ap.shape)
        d_ = nc.dram_tensor("dbg_" + name, shp, dt, kind="ExternalOutput").ap()
        dumps[name] = d_
        P.dma("sp", lambda e: e.dma_start(out=d_, in_=ap), reads=[buf], writes=[Buf()])

    def phase_start():
        P.barrier()
        arena.off = base_off

    def norm_block(src, g_b, bg, tb, xnT, bxnT, xt_bufs, keep_x=None):
        for ti in range(4):
            row0 = tb * TB + ti * 128
            if keep_x is not None:
                xt, bxt = keep_x[ti]
            else:
                xt, bxt = xt_bufs[ti % len(xt_bufs)]
            P.dma("sp", lambda e, xt=xt, row0=row0: e.dma_start(out=xt, in_=src[row0:row0 + 128, :]), writes=[bxt])
            junk, bjunk = norm_block.junk
            ssq, bssq = norm_block.ssq
            P.op("act", lambda e, xt=xt: e.activation(out=junk, in_=xt, func=AF.Square),
                 reads=[bxt], writes=[bjunk])
            P.op("dve", lambda e: e.reduce_sum(out=ssq[:, 0:1], in_=junk, axis=AX.X), reads=[bjunk], writes=[bssq])
            P.op("dve", lambda e: e.tensor_scalar(out=ssq[:, 1:2], in0=ssq[:, 0:1], scalar1=1.0 / D, scalar2=EPS,
                                                  op0=ALU.mult, op1=ALU.add), reads=[bssq], writes=[bssq])
            P.op("act", lambda e: e.activation(out=ssq[:, 3:4], in_=ssq[:, 1:2], func=AF.Sqrt), reads=[bssq], writes=[bssq])
            P.op("dve", lambda e: e.reciprocal(out=ssq[:, 2:3], in_=ssq[:, 3:4]), reads=[bssq], writes=[bssq])
            xn, bxn = norm_block.xn
            P.op("dve", lambda e, xt=xt: e.scalar_tensor_tensor(out=xn, in0=xt, scalar=ssq[:, 2:3], in1=g_b,
                                                                op0=ALU.mult, op1=ALU.mult),
                 reads=[bxt, bssq, bg], writes=[bxn])
            dump("xt", xt, bxt, F32); dump("ssq", ssq, bssq, F32); dump("xn", xn, bxn, BF16)
            for k8 in range(4):
                pi = 6 + (k8 % 2)
                pv = ps[pi][:].bitcast(BF16)
                for kk in range(8):
                    k = k8 * 8 + kk
                    P.op("pe", lambda e, k=k, kk=kk, pv=pv: e.transpose(pv[:, kk * 128:(kk + 1) * 128],
                                                                      xn[:, k * 128:(k + 1) * 128], ident_b),
                         reads=[bxn, bconst], writes=[bps[pi]], sig=(kk == 7))
                eng = "act" if k8 % 2 == 0 else "dve"
                dst = xnT[:, k8 * 8:(k8 + 1) * 8, ti * 128:(ti + 1) * 128]
                srcv = pv.rearrange("p (k t) -> p k t", k=8)
                if eng == "act":
                    P.op("act", lambda e, dst=dst, srcv=srcv: e.copy(out=dst, in_=srcv), reads=[bps[pi]], writes=[bxnT])
                else:
                    P.op("dve", lambda e, dst=dst, srcv=srcv: e.tensor_copy(out=dst, in_=srcv), reads=[bps[pi]], writes=[bxnT])

    def norm_setup():
        norm_block.junk = (arena.bf16(D), Buf("junk"))
        norm_block.ssq = (arena.f32(4), Buf("ssq"))
        norm_block.xn = (arena.bf16(D), Buf("xn"))

    def load_gain_row(g_dram_row):
        g_b = arena.f32(D)
        bg = Buf("g")
        P.dma("sp", lambda e: e.dma_start(out=g_b, in_=row_bcast(g_dram_row, D)), writes=[bg])
        return g_b, bg

    def wstream(loads, compute):
        n = len(loads)
        loads[0](0)
        for i in range(n):
            if i + 1 < n:
                loads[i + 1]((i + 1) % 2)
            compute(i, i % 2)

    def proj_residual(srcT, bsrcT, w_dram, res_src, dst, nblk):
        phase_start()
        ot = arena.bf16(32 * TB).rearrange("p (h t) -> p h t", h=32); bot = Buf("ot")
        wC = [arena.bf16(32 * 512).rearrange("p (k c) -> p k c", k=32) for _ in range(2)]
        bwC = [Buf("wo0"), Buf("wo1")]
        xts = [(arena.f32(D), Buf(f"xr{i}")) for i in range(4)]
        wo_v = w_dram.rearrange("(h p) c -> p h c", p=128)
        cnt = [0]
        for tb in range(nblk):
            P.dma("sp", lambda e, tb=tb: e.dma_start(out=ot, in_=srcT[:, :, tb * TB:(tb + 1) * TB].rearrange("h p t -> p h t")),
                  reads=[bsrcT], writes=[bot])
            for ti in range(4):
                r0 = tb * TB + ti * 128
                P.dma("sp", lambda e, ti=ti, r0=r0: e.dma_start(out=xts[ti][0], in_=res_src[r0:r0 + 128, :]), writes=[xts[ti][1]])

            def ld(cb):
                def f(slot):
                    P.dma("pool", lambda e: e.dma_start(out=wC[slot], in_=wo_v[:, :, cb * 512:(cb + 1) * 512]), writes=[bwC[slot]])
                return f

            def comp(cb, slot):
                for ti in range(4):
                    pq = cnt[0] % 4
                    cnt[0] += 1
                    for h in range(32):
                        P.op("pe", lambda e, h=h, ti=ti, pq=pq: e.matmul(ps[pq][:], ot[:, h, ti * 128:(ti + 1) * 128], wC[slot][:, h, :],
                                                                         start=(h == 0), stop=(h == 31)),
                                 sig=(h == 31),
                             reads=[bot, bwC[slot]], writes=[bps[pq]])
                    xs = xts[ti][0][:, cb * 512:(cb + 1) * 512]
                    P.op("dve", lambda e, xs=xs, pq=pq: e.tensor_tensor(out=xs, in0=xs, in1=ps[pq][:], op=ALU.add),
                         reads=[bps[pq]], writes=[xts[ti][1]])
            wstream([ld(cb) for cb in range(8)], comp)
            for ti in range(4):
                r0 = tb * TB + ti * 128
                P.dma("sp", lambda e, ti=ti, r0=r0: e.dma_start(out=dst[r0:r0 + 128, :], in_=xts[ti][0]), reads=[xts[ti][1]], writes=[Buf()])

    if "attn" in stages:
        QT = nc.dram_tensor("QT", [32, 128, T], BF16, kind=("ExternalOutput" if dbg else "Internal")).ap()
        KT = nc.dram_tensor("KT", [8, 128, T], BF16, kind=("ExternalOutput" if dbg else "Internal")).ap()
        VS = nc.dram_tensor("VS", [T, 1024], BF16, kind=("ExternalOutput" if dbg else "Internal")).ap()
        OT = nc.dram_tensor("OT", [32, 128, T], BF16, kind=("ExternalOutput" if dbg else "Internal")).ap()
        bQT, bKT, bVS, bOT = Buf("QT"), Buf("KT"), Buf("VS"), Buf("OT")

        phase_start()
        g_b, bg = load_gain_row(mixer_g[0])
        norm_setup()
        cs = arena.f32(2 * T).rearrange("p (c t) -> p c t", c=2)
        bcs = Buf("cs")
        P.dma("sp", lambda e: e.dma_start(out=cs, in_=rope.rearrange("c p t -> p c t")), writes=[bcs])
        xnT = arena.bf16(32 * TB).rearrange("p (k t) -> p k t", k=32)
        bxnT = Buf("xnT")
        xt_bufs = [(arena.f32(D), Buf("xt0"))]
        wt = [arena.bf16(32 * 512).rearrange("p (k c) -> p k c", k=32) for _ in range(2)]
        bwt = [Buf("wt0"), Buf("wt1")]
        sq = arena.bf16(512); bsq = Buf("sq")
        rstd = arena.f32(512); brstd = Buf("rstd")
        qn = arena.f32(512); bqn = Buf("qn")
        t1 = arena.f32(512); bt1 = Buf("t1")
        t2 = arena.f32(512); bt2 = Buf("t2")
        qo = [arena.bf16(512) for _ in range(2)]; bqo = [Buf("qo0"), Buf("qo1")]
        vo = [arena.bf16(512) for _ in range(2)]; bvo = [Buf("vo0"), Buf("vo1")]
        wq_v = w_qkv.rearrange("(k p) c -> p k c", p=128)
        cnt = {"q": 0, "v": 0}
        for tb in range(1 if dbg == 2 else NB):
            norm_block(x, g_b, bg, tb, xnT, bxnT, xt_bufs)

            dump("xnT", xnT, bxnT, BF16)

            def ld(cb):
                def f(slot):
                    P.dma("pool", lambda e: e.dma_start(out=wt[slot], in_=wq_v[:, :, cb * 512:(cb + 1) * 512]),
                          writes=[bwt[slot]])
                    dump("wt", wt[slot], bwt[slot], BF16)
                return f

            def comp(cb, slot, tb=tb):
                if cb < 10:
                    for j in range(4):
                        hb = cb * 4 + j
                        pq = cnt["q"] % 2
                        cnt["q"] += 1
                        for k in range(32):
                            P.op("pe", lambda e, k=k, j=j, pq=pq: e.matmul(ps[pq][:], wt[slot][:, k, j * 128:(j + 1) * 128],
                                                                           xnT[:, k, :], start=(k == 0), stop=(k == 31)),
                                 sig=(k == 31),
                                 reads=[bwt[slot], bxnT], writes=[bps[pq]])
                        gcol = gain_t[:, 0:1] if hb < 32 else gain_t[:, 1:2]
                        P.op("act", lambda e, pq=pq: e.activation(out=sq, in_=ps[pq][:], func=AF.Square),
                             reads=[bps[pq]], writes=[bsq])
                        P.op("pe", lambda e: e.matmul(ps[2][:], ones_b, sq, start=True, stop=True),
                             reads=[bsq, bconst], writes=[bps[2]])
                        P.op("dve", lambda e: e.tensor_scalar(out=rstd, in0=ps[2][:], scalar1=1.0 / 128, scalar2=EPS,
                                                              op0=ALU.mult, op1=ALU.add), reads=[bps[2]], writes=[brstd])
                        P.op("act", lambda e: e.activation(out=rstd, in_=rstd, func=AF.Sqrt), reads=[brstd], writes=[brstd])
                        P.op("dve", lambda e: e.reciprocal(out=rstd, in_=rstd), reads=[brstd], writes=[brstd])
                        P.op("dve", lambda e, pq=pq, gcol=gcol: e.scalar_tensor_tensor(out=qn, in0=ps[pq][:], scalar=gcol, in1=rstd,
                                                                                       op0=ALU.mult, op1=ALU.mult),
                             reads=[bps[pq], brstd, bconst], writes=[bqn])
                        dump("rstd", rstd, brstd, F32); dump("qn", qn, bqn, F32); dump("sq", sq, bsq, BF16)
                        P.op("pe", lambda e: e.matmul(ps[3][:], rot_f, qn, start=True, stop=True),
                             reads=[bqn, bconst], writes=[bps[3]])
                        tsl = slice(tb * TB, (tb + 1) * TB)
                        P.op("pool", lambda e, tsl=tsl: e.tensor_tensor(out=t1, in0=qn, in1=cs[:, 0, tsl], op=ALU.mult),
                             reads=[bqn, bcs], writes=[bt1])
                        P.op("dve", lambda e, tsl=tsl: e.tensor_tensor(out=t2, in0=ps[3][:], in1=cs[:, 1, tsl], op=ALU.mult),
                             reads=[bps[3], bcs], writes=[bt2])
                        qs = hb % 2
                        P.op("dve", lambda e, qs=qs: e.tensor_tensor(out=qo[qs], in0=t1, in1=t2, op=ALU.add),
                             reads=[bt1, bt2], writes=[bqo[qs]])
                        dump("t1", t1, bt1, F32); dump("t2", t2, bt2, F32); dump("qo", qo[qs], bqo[qs], BF16)
                        if hb < 32:
                            dst, bd = QT[hb][:, tb * TB:(tb + 1) * TB], bQT
                        else:
                            dst, bd = KT[hb - 32][:, tb * TB:(tb + 1) * TB], bKT
                        P.dma("sp", lambda e, qs=qs, dst=dst: e.dma_start(out=dst, in_=qo[qs]), reads=[bqo[qs]], writes=[bd])
                else:
                    for ti in range(4):
                        pq = 4 + cnt["v"] % 2
                        vs = cnt["v"] % 2
                        cnt["v"] += 1
                        for k in range(32):
                            P.op("pe", lambda e, k=k, ti=ti, pq=pq: e.matmul(ps[pq][:], xnT[:, k, ti * 128:(ti + 1) * 128],
                                                                             wt[slot][:, k, :], start=(k == 0), stop=(k == 31)),
                                 sig=(k == 31),
                                 reads=[bwt[slot], bxnT], writes=[bps[pq]])
                        P.op("act", lambda e, pq=pq, vs=vs: e.copy(out=vo[vs], in_=ps[pq][:]), reads=[bps[pq]], writes=[bvo[vs]])
                        dump("vo", vo[vs], bvo[vs], BF16)
                        r0 = tb * TB + ti * 128
                        c0 = (cb - 10) * 512
                        P.dma("sp", lambda e, vs=vs, r0=r0, c0=c0: e.dma_start(out=VS[r0:r0 + 128, c0:c0 + 512], in_=vo[vs]),
                              reads=[bvo[vs]], writes=[bVS])
            cbs = list(range(12)) if (tb < NBQ or dbg) else list(range(8, 12))
            wstream([ld(cb) for cb in cbs], lambda i, slot, cbs=cbs: comp(cbs[i], slot))

        if dbg in (2, 3):
            P.barrier(); P.emit(nc, st); st.close(); return nc
        phase_start()
        kt = arena.bf16(T); bkt = Buf("kt")
        vt = arena.bf16(32 * 130).rearrange("p (c d) -> p c d", c=32); bvt = Buf("vt")
        qt = arena.bf16(4 * T).rearrange("p (h t) -> p h t", h=4); bqt = Buf("qt")
        pT = [arena.bf16(512) for _ in range(3)]; bpT = [Buf(f"pT{i}") for i in range(3)]
        rden = arena.f32(4); brden = Buf("rden")
        on = arena.bf16(128 * 4).rearrange("p (h d) -> p h d", h=4); bon = Buf("on")
        oT = [arena.bf16(4 * 512).rearrange("p (h t) -> p h t", h=4) for _ in range(2)]; boT = [Buf("oT0"), Buf("oT1")]
        scale = 128.0 ** -0.5
        for g in range(8):
            P.dma("sp", lambda e, g=g: e.dma_start(out=kt, in_=KT[g]), writes=[bkt], reads=[bKT])
            P.dma("sp", lambda e, g=g: e.dma_start(out=vt[:, :, 0:128],
                                                   in_=VS[:, g * 128:(g + 1) * 128].rearrange("(c p) d -> p c d", p=128)),
                  writes=[bvt], reads=[bVS])
            P.op("pool", lambda e: e.memset(vt[:, :, 128:129], 1.0), writes=[bvt])
            P.dma("sp", lambda e, g=g: e.dma_start(out=qt, in_=QT[4 * g:4 * g + 4].rearrange("h p t -> p h t")),
                  writes=[bqt], reads=[bQT])
            NQ = NT if dbg else NBQ * 4
            seq = [(qi, c) for qi in range(NQ) for c in range(32)]

            def S_(idx):
                qi, c = seq[idx]
                sp_, pslot = idx % 2, idx % 3
                P.op("pe", lambda e: e.matmul(ps[sp_][:], kt[:, c * 128:(c + 1) * 128], qt[:, :, qi * 128:(qi + 1) * 128], start=True, stop=True),
                     reads=[bkt, bqt], writes=[bps[sp_]])
                P.op("act", lambda e: e.activation(out=pT[pslot], in_=ps[sp_][:], func=AF.Exp, scale=scale),
                     reads=[bps[sp_]], writes=[bpT[pslot]])

            def PV_(idx, g=g):
                qi, c = seq[idx]
                pslot = idx % 3
                for j in range(4):
                    P.op("pe", lambda e, j=j: e.matmul(ps[2 + j][:, 0:129], pT[pslot][:, j * 128:(j + 1) * 128], vt[:, c, 0:129],
                                                       start=(c == 0), stop=(c == 31)),
                         reads=[bpT[pslot], bvt], writes=[bps[2 + j]], sig=(j == 3))
                if c != 31:
                    return
                os_ = (qi // 4) % 2
                for j in range(4):
                    P.op("dve", lambda e, j=j: e.reciprocal(out=rden[:, j:j + 1], in_=ps[2 + j][:, 128:129]),
                         reads=[bps[2 + j]], writes=[brden])
                    P.op("dve", lambda e, j=j: e.tensor_scalar(out=on[:, j, :], in0=ps[2 + j][:, 0:128], scalar1=rden[:, j:j + 1],
                                                               scalar2=None, op0=ALU.mult),
                         reads=[bps[2 + j], brden], writes=[bon])
                pb = 6 + qi % 2
                pv = ps[pb][:].bitcast(BF16)
                for j in range(4):
                    P.op("pe", lambda e, j=j: e.transpose(pv[:, j * 128:(j + 1) * 128], on[:, j, :], ident_b),
                         reads=[bon, bconst], writes=[bps[pb]], sig=(j == 3))
                P.op("dve", lambda e: e.tensor_copy(out=oT[os_][:, :, (qi % 4) * 128:(qi % 4 + 1) * 128],
                                                    in_=pv[:, 0:512].rearrange("p (h t) -> p h t", h=4)),
                     reads=[bps[pb]], writes=[boT[os_]])
                if qi % 4 == 3:
                    t0 = (qi // 4) * 512
                    P.dma("sp", lambda e: e.dma_start(out=OT[4 * g:4 * g + 4][:, :, t0:t0 + 512].rearrange("h p t -> p h t"), in_=oT[os_]),
                          reads=[boT[os_]], writes=[bOT])
            S_(0)
            for idx in range(len(seq)):
                if idx + 1 < len(seq):
                    S_(idx + 1)
                PV_(idx)

        if dbg == 4:
            P.barrier(); P.emit(nc, st); st.close(); return nc
        proj_residual(OT, bOT, w_o, x, h1, NBQ)

    def conv_layer(hin, hout):
        UTs = nc.dram_tensor("cUT", [32, 128, T], F32, kind="Internal").ap()
        BTs = nc.dram_tensor("cBT", [32, 128, T], F32, kind="Internal").ap()
        YT = nc.dram_tensor("cYT", [32, 128, T], BF16, kind="Internal").ap()
        bUTs, bBTs, bYT = Buf("cUT"), Buf("cBT"), Buf("cYT")
        phase_start()
        g_b, bg = load_gain_row(mixer_g[1])
        norm_setup()
        xnT = arena.bf16(32 * TB).rearrange("p (k t) -> p k t", k=32); bxnT = Buf("xnT")
        xt_bufs = [(arena.f32(D), Buf("xt0"))]
        wt = [arena.bf16(32 * 512).rearrange("p (k c) -> p k c", k=32) for _ in range(2)]
        bwt = [Buf("wi0"), Buf("wi1")]
        cbuf = [arena.f32(512) for _ in range(4)]; bcbuf = [Buf(f"cb{i}") for i in range(4)]
        obuf = [arena.f32(512) for _ in range(2)]; bobuf = [Buf("ob0"), Buf("ob1")]
        wi_v = w_in.rearrange("(k p) c -> p k c", p=128)
        cnt = [0, 0]
        for tb in range(NBQ):
            norm_block(hin, g_b, bg, tb, xnT, bxnT, xt_bufs)
            items = []
            for cb4 in range(8):
                for kind in ("c", "x", "b"):
                    items.append((cb4, kind))

            def ld(it):
                cb4, kind = it
                col0 = {"b": 0, "c": D, "x": 2 * D}[kind] + cb4 * 512

                def f(slot):
                    P.dma("pool", lambda e: e.dma_start(out=wt[slot], in_=wi_v[:, :, col0:col0 + 512]), writes=[bwt[slot]])
                return f

            def comp(i, slot, tb=tb, items=items):
                cb4, kind = items[i]
                for j in range(4):
                    pq = cnt[0] % 4
                    cnt[0] += 1
                    for k in range(32):
                        P.op("pe", lambda e, k=k, j=j, pq=pq: e.matmul(ps[pq][:], wt[slot][:, k, j * 128:(j + 1) * 128], xnT[:, k, :],
                                                                       start=(k == 0), stop=(k == 31)),
                                 sig=(k == 31),
                             reads=[bwt[slot], bxnT], writes=[bps[pq]])
                    chunk = cb4 * 4 + j
                    if kind == "c":
                        P.op("act", lambda e, j=j, pq=pq: e.copy(out=cbuf[j], in_=ps[pq][:]), reads=[bps[pq]], writes=[bcbuf[j]])
                    else:
                        os_ = cnt[1] % 2
                        cnt[1] += 1
                        if kind == "x":
                            P.op("dve", lambda e, j=j, pq=pq, os_=os_: e.tensor_tensor(out=obuf[os_], in0=cbuf[j], in1=ps[pq][:], op=ALU.mult),
                                 reads=[bps[pq], bcbuf[j]], writes=[bobuf[os_]])
                            dst_, bd = UTs[chunk][:, tb * TB:(tb + 1) * TB], bUTs
                        else:
                            P.op("act", lambda e, pq=pq, os_=os_: e.copy(out=obuf[os_], in_=ps[pq][:]), reads=[bps[pq]], writes=[bobuf[os_]])
                            dst_, bd = BTs[chunk][:, tb * TB:(tb + 1) * TB], bBTs
                        P.dma("sp", lambda e, os_=os_, dst_=dst_: e.dma_start(out=dst_, in_=obuf[os_]), reads=[bobuf[os_]], writes=[bd])
            wstream([ld(it) for it in items], comp)

        phase_start()
        ub = [arena.f32(TL + 2) for _ in range(2)]; bub = [Buf("ub0"), Buf("ub1")]
        bt = [arena.f32(TL) for _ in range(2)]; bbt = [Buf("bt0"), Buf("bt1")]
        cv = arena.f32(TL); bcv = Buf("cv")
        yb = [arena.bf16(TL) for _ in range(2)]; byb = [Buf("yb0"), Buf("yb1")]
        wb = [arena.f32(4) for _ in range(2)]; bwb = [Buf("wb0"), Buf("wb1")]
        for s_ in range(2):
            P.op("pool", lambda e, s_=s_: e.memset(ub[s_][:, 0:1], 0.0), writes=[bub[s_]])
            P.op("pool", lambda e, s_=s_: e.memset(ub[s_][:, TL + 1:TL + 2], 0.0), writes=[bub[s_]])
        for ch in range(32):
            s_ = ch % 2
            P.dma("sp", lambda e, ch=ch, s_=s_: e.dma_start(out=ub[s_][:, 1:TL + 1], in_=UTs[ch][:, 0:TL]), reads=[bUTs], writes=[bub[s_]])
            P.dma("sp", lambda e, ch=ch, s_=s_: e.dma_start(out=bt[s_], in_=BTs[ch][:, 0:TL]), reads=[bBTs], writes=[bbt[s_]])
            P.dma("sp", lambda e, ch=ch, s_=s_: e.dma_start(out=wb[s_], in_=conv_wb[ch]), writes=[bwb[s_]])
            P.op("dve", lambda e, s_=s_: e.tensor_scalar(out=cv, in0=ub[s_][:, 0:TL], scalar1=wb[s_][:, 0:1], scalar2=wb[s_][:, 3:4],
                                                         op0=ALU.mult, op1=ALU.add), reads=[bub[s_], bwb[s_]], writes=[bcv])
            P.op("dve", lambda e, s_=s_: e.scalar_tensor_tensor(out=cv, in0=ub[s_][:, 1:TL + 1], scalar=wb[s_][:, 1:2], in1=cv,
                                                                op0=ALU.mult, op1=ALU.add), reads=[bub[s_], bwb[s_]], writes=[bcv])
            P.op("dve", lambda e, s_=s_: e.scalar_tensor_tensor(out=cv, in0=ub[s_][:, 2:TL + 2], scalar=wb[s_][:, 2:3], in1=cv,
                                                                op0=ALU.mult, op1=ALU.add), reads=[bub[s_], bwb[s_]], writes=[bcv])
            P.op("pool", lambda e, s_=s_: e.tensor_tensor(out=yb[s_], in0=cv, in1=bt[s_], op=ALU.mult), reads=[bcv, bbt[s_]], writes=[byb[s_]])
            P.dma("sp", lambda e, ch=ch, s_=s_: e.dma_start(out=YT[ch][:, 0:TL], in_=yb[s_]), reads=[byb[s_]], writes=[bYT])
        proj_residual(YT, bYT, w_out, hin, hout, NBO)

    def peer_layer(L, hin, hout, nblk):
        UT = nc.dram_tensor(f"pUT{L}", [64, 128, 32 * 256], BF16, kind="Internal").ap()
        XT = nc.dram_tensor(f"pXT{L}", [NB, 128, 32 * TB], BF16, kind="Internal").ap()
        SS = nc.dram_tensor(f"pSS{L}", [T, 2048], F32, kind="Internal").ap()
        GS = nc.dram_tensor(f"pGS{L}", [T, NE], BF16, kind="Internal").ap()
        bUT, bXT, bSS, bGS = Buf("pUT"), Buf("pXT"), Buf("pSS"), Buf("pGS")

        phase_start()
        ub = [arena.bf16(4 * D).rearrange("p (c d) -> p c d", c=4) for _ in range(2)]; bub = [Buf("ub0"), Buf("ub1")]
        utb = [arena.bf16(32 * 512).rearrange("p (k e) -> p k e", k=32) for _ in range(2)]; butb = [Buf("utb0"), Buf("utb1")]
        ncp = [0]

        def ld0(eb2):
            def f(slot):
                P.dma("pool", lambda e: e.dma_start(out=ub[slot], in_=peer_u[L][eb2 * 512:(eb2 + 1) * 512, :].rearrange("(c p) d -> p c d", p=128)),
                      writes=[bub[slot]])
            return f

        def comp0(eb2, slot):
            for c in range(4):
                for k8 in range(4):
                    pi = ncp[0] % 4
                    ncp[0] += 1
                    pv = ps[pi][:].bitcast(BF16)
                    for kk in range(8):
                        k = k8 * 8 + kk
                        P.op("pe", lambda e, k=k, kk=kk, c=c, pv=pv: e.transpose(pv[:, kk * 128:(kk + 1) * 128], ub[slot][:, c, k * 128:(k + 1) * 128], ident_b),
                             reads=[bub[slot], bconst], writes=[bps[pi]], sig=(kk == 7))
                    dst_ = utb[slot][:, k8 * 8:(k8 + 1) * 8, c * 128:(c + 1) * 128]
                    srcv = pv.rearrange("p (k t) -> p k t", k=8)
                    if ncp[0] % 2 == 0:
                        P.op("act", lambda e, dst_=dst_, srcv=srcv: e.copy(out=dst_, in_=srcv), reads=[bps[pi]], writes=[butb[slot]])
                    else:
                        P.op("dve", lambda e, dst_=dst_, srcv=srcv: e.tensor_copy(out=dst_, in_=srcv), reads=[bps[pi]], writes=[butb[slot]])
            for hf in range(2):
                P.dma("sp", lambda e, hf=hf: e.dma_start(out=UT[eb2 * 2 + hf].rearrange("p (k e) -> p k e", k=32),
                                                         in_=utb[slot][:, :, hf * 256:(hf + 1) * 256]), reads=[butb[slot]], writes=[bUT])
        wstream([ld0(eb2) for eb2 in range(32)], comp0)

        phase_start()
        g_b, bg = load_gain_row(ffn_g[L])
        norm_setup()
        xnT = arena.bf16(32 * TB).rearrange("p (k t) -> p k t", k=32); bxnT = Buf("xnT")
        xt_bufs = [(arena.f32(D), Buf("xt0"))]
        wq = [arena.bf16(32 * 256).rearrange("p (k c) -> p k c", k=32) for _ in range(2)]; bwq = [Buf("wq0"), Buf("wq1")]
        qT = arena.f32(16 * TB).rearrange("p (h t) -> p h t", h=16); bqT = Buf("qT")
        sk = arena.f32(16 * 128).rearrange("p (h d) -> p h d", h=16); bsk = Buf("sk")
        skT = arena.f32(16 * 128).rearrange("p (h n) -> p h n", h=16); bskT = Buf("skT")
        s_t = [arena.f32(2048) for _ in range(2)]; bs_t = [Buf("st0"), Buf("st1")]
        P.dma("sp", lambda e: e.dma_start(out=sk, in_=sub_keys[L].rearrange("h n d -> n h d")), writes=[bsk])
        for hp in range(16):
            pi = 4 + (hp // 4) % 2
            P.op("pe", lambda e, hp=hp, pi=pi: e.transpose(ps[pi][:, (hp % 4) * 128:(hp % 4 + 1) * 128], sk[:, hp, :], ident_f),
                 reads=[bsk, bconst], writes=[bps[pi]])
            if hp % 4 == 3:
                P.op("dve", lambda e, hp=hp, pi=pi: e.tensor_copy(out=skT[:, hp - 3:hp + 1, :], in_=ps[pi][:].rearrange("p (h n) -> p h n", h=4)),
                     reads=[bps[pi]], writes=[bskT])
        wq_v = w_query[L].rearrange("(k p) c -> p k c", p=128)
        cn = [0, 0]
        for tb in range(nblk):
            norm_block(hin, g_b, bg, tb, xnT, bxnT, xt_bufs)
            P.dma("sp", lambda e, tb=tb: e.dma_start(out=XT[tb].rearrange("p (k t) -> p k t", k=32), in_=xnT), reads=[bxnT], writes=[bXT])

            def ld(ct):
                def f(slot):
                    P.dma("pool", lambda e: e.dma_start(out=wq[slot], in_=wq_v[:, :, ct * 256:(ct + 1) * 256]), writes=[bwq[slot]])
                return f

            def comp(ct, slot):
                for j in range(2):
                    hp = ct * 2 + j
                    pq = cn[0] % 2
                    cn[0] += 1
                    for k in range(32):
                        P.op("pe", lambda e, k=k, j=j, pq=pq: e.matmul(ps[pq][:], wq[slot][:, k, j * 128:(j + 1) * 128], xnT[:, k, :],
                                                                       start=(k == 0), stop=(k == 31)),
                                 sig=(k == 31),
                             reads=[bwq[slot], bxnT], writes=[bps[pq]])
                    if hp % 2 == 0:
                        P.op("act", lambda e, hp=hp, pq=pq: e.copy(out=qT[:, hp, :], in_=ps[pq][:]), reads=[bps[pq]], writes=[bqT])
                    else:
                        P.op("dve", lambda e, hp=hp, pq=pq: e.tensor_copy(out=qT[:, hp, :], in_=ps[pq][:]), reads=[bps[pq]], writes=[bqT])
            wstream([ld(ct) for ct in range(8)], comp)
            for ti in range(4):
                ss = cn[1] % 2
                cn[1] += 1
                for hp4 in range(4):
                    pi = 2 + hp4 % 2
                    for j in range(4):
                        hp = hp4 * 4 + j
                        P.op("pe", lambda e, hp=hp, j=j, ti=ti, pi=pi: e.matmul(ps[pi][:, j * 128:(j + 1) * 128], qT[:, hp, ti * 128:(ti + 1) * 128],
                                                                              skT[:, hp, :], start=True, stop=True),
                             reads=[bqT, bskT], writes=[bps[pi]])
                    P.op("dve", lambda e, hp4=hp4, pi=pi, ss=ss: e.tensor_copy(out=s_t[ss][:, hp4 * 512:(hp4 + 1) * 512], in_=ps[pi][:]),
                         reads=[bps[pi]], writes=[bs_t[ss]])
                r0 = tb * TB + ti * 128
                P.dma("sp", lambda e, r0=r0, ss=ss: e.dma_start(out=SS[r0:r0 + 128, :], in_=s_t[ss]), reads=[bs_t[ss]], writes=[bSS])

        phase_start()
        NEG = -1.0e30
        sb = [arena.f32(2048) for _ in range(2)]; bsb = [Buf("sb0"), Buf("sb1")]
        top = arena.f32(256); btop = Buf("top")
        tmp = arena.f32(128); btmp = Buf("tmp")
        cand = arena.f32(8 * 256); bcand = Buf("cand")
        tmpc = arena.f32(256); btmpc = Buf("tmpc")
        best = arena.f32(8 * 24); bbest = Buf("best")
        sm = arena.f32(64); bsm = Buf("sm")
        ex = arena.f32(128); bex = Buf("ex")
        pen = arena.f32(128); bpen = Buf("pen")
        d1 = arena.f32(8 * 128); bd1 = Buf("d1")
        s2m = arena.f32(8 * 128); bs2m = Buf("s2m")
        Z3 = [arena.f32(2048) for _ in range(2)]; bZ3 = [Buf("Z30"), Buf("Z31")]
        E3 = [arena.f32(2048) for _ in range(2)]; bE3 = [Buf("E30"), Buf("E31")]
        M3 = [arena.bf16(2048) for _ in range(3)]; bM3 = [Buf("M30"), Buf("M31"), Buf("M32")]
        dg = arena.bf16(8 * 128).rearrange("p (h t) -> p h t", h=8); bdg = Buf("dg")
        gout = [arena.bf16(NE) for _ in range(2)]; bgout = [Buf("go0"), Buf("go1")]
        nz = [0]
        for tile in range(nblk * 4):
            ss = tile % 2
            s = sb[ss]
            bs = bsb[ss]
            r0 = tile * 128
            P.dma("sp", lambda e, r0=r0, s=s: e.dma_start(out=s, in_=SS[r0:r0 + 128, :]), reads=[bSS], writes=[bs])
            for hp in range(16):
                sv = s[:, hp * 128:(hp + 1) * 128]
                P.op("dve", lambda e, hp=hp, sv=sv: e.max(out=top[:, hp * 16:hp * 16 + 8], in_=sv), reads=[bs], writes=[btop])
                P.op("dve", lambda e, hp=hp, sv=sv: e.match_replace(out=tmp, in_to_replace=top[:, hp * 16:hp * 16 + 8], in_values=sv, imm_value=NEG),
                     reads=[bs, btop], writes=[btmp])
                P.op("dve", lambda e, hp=hp: e.max(out=top[:, hp * 16 + 8:hp * 16 + 16], in_=tmp), reads=[btmp], writes=[btop])
            cand4 = cand.rearrange("p (h a b) -> p h a b", h=8, a=16)
            P.op("dve", lambda e, cand4=cand4: e.tensor_tensor(out=cand4, in0=bc_ap(top, [[32, 8], [1, 16], [0, 16]]),
                                                              in1=bc_ap(top[:, 16:], [[32, 8], [0, 16], [1, 16]]), op=ALU.add),
                 reads=[btop], writes=[bcand])
            for h in range(8):
                cvw = cand[:, h * 256:(h + 1) * 256]
                P.op("dve", lambda e, h=h, cvw=cvw: e.max(out=best[:, h * 24:h * 24 + 8], in_=cvw), reads=[bcand], writes=[bbest])
                P.op("dve", lambda e, h=h, cvw=cvw: e.match_replace(out=tmpc, in_to_replace=best[:, h * 24:h * 24 + 8], in_values=cvw, imm_value=NEG),
                     reads=[bcand, bbest], writes=[btmpc])
                P.op("dve", lambda e, h=h: e.max(out=best[:, h * 24 + 8:h * 24 + 16], in_=tmpc), reads=[btmpc], writes=[bbest])
                P.op("dve", lambda e, h=h: e.match_replace(out=tmpc, in_to_replace=best[:, h * 24 + 8:h * 24 + 16], in_values=tmpc, imm_value=NEG),
                     reads=[bbest, btmpc], writes=[btmpc])
                P.op("dve", lambda e, h=h: e.max(out=best[:, h * 24 + 16:h * 24 + 24], in_=tmpc), reads=[btmpc], writes=[bbest])
            b3 = best.rearrange("p (h k) -> p h k", h=8)
            negm, tau, tm, Zs, rZ, negtau = [sm[:, i * 8:(i + 1) * 8] for i in range(6)]
            P.op("dve", lambda e: e.tensor_scalar(out=negm, in0=b3[:, :, 0], scalar1=-1.0, scalar2=None, op0=ALU.mult), reads=[bbest], writes=[bsm])
            P.op("dve", lambda e: e.tensor_tensor(out=tau, in0=b3[:, :, 15], in1=b3[:, :, 16], op=ALU.add), reads=[bbest], writes=[bsm])
            P.op("dve", lambda e: e.tensor_scalar(out=tau, in0=tau, scalar1=0.5, scalar2=None, op0=ALU.mult), reads=[bsm], writes=[bsm])
            P.op("dve", lambda e: e.tensor_tensor(out=tm, in0=tau, in1=negm, op=ALU.add), reads=[bsm], writes=[bsm])
            P.op("dve", lambda e: e.tensor_scalar(out=negtau, in0=tau, scalar1=-1.0, scalar2=None, op0=ALU.mult), reads=[bsm], writes=[bsm])
            for h in range(8):
                P.op("act", lambda e, h=h: e.activation(out=ex[:, h * 16:(h + 1) * 16], in_=best[:, h * 24:h * 24 + 16], func=AF.Exp, bias=negm[:, h:h + 1]),
                     reads=[bbest, bsm], writes=[bex])
                P.op("dve", lambda e, h=h: e.reduce_sum(out=Zs[:, h:h + 1], in_=ex[:, h * 16:(h + 1) * 16], axis=AX.X), reads=[bex], writes=[bsm])
            P.op("dve", lambda e: e.reciprocal(out=rZ, in_=Zs), reads=[bsm], writes=[bsm])
            for h in range(8):
                for p_ in range(2):
                    hp = 2 * h + p_
                    sv = s[:, hp * 128:(hp + 1) * 128]
                    thr = top[:, hp * 16 + 15:hp * 16 + 16]
                    P.op("dve", lambda e, sv=sv, thr=thr: e.tensor_scalar(out=pen, in0=sv, scalar1=thr, scalar2=None, op0=ALU.is_ge),
                         reads=[bs, btop], writes=[bpen])
                    P.op("dve", lambda e: e.tensor_scalar(out=pen, in0=pen, scalar1=-1.0, scalar2=1.0e4, op0=ALU.add, op1=ALU.mult),
                         reads=[bpen], writes=[bpen])
                    if p_ == 0:
                        dv = d1[:, h * 128:(h + 1) * 128]
                        P.op("dve", lambda e, sv=sv, h=h, dv=dv: e.scalar_tensor_tensor(out=dv, in0=sv, scalar=negtau[:, h:h + 1], in1=pen,
                                                                                    op0=ALU.add, op1=ALU.add), reads=[bs, bsm, bpen], writes=[bd1])
                    else:
                        dv = s2m[:, h * 128:(h + 1) * 128]
                        P.op("dve", lambda e, sv=sv, dv=dv: e.tensor_tensor(out=dv, in0=sv, in1=pen, op=ALU.add), reads=[bs, bpen], writes=[bs2m])
            go = gout[tile % 2]
            bgo = bgout[tile % 2]
            for h in range(8):
                P.op("dve", lambda e, h=h: e.tensor_scalar(out=dg[:, h, :], in0=ident_f, scalar1=rZ[:, h:h + 1], scalar2=None, op0=ALU.mult),
                     reads=[bsm, bconst], writes=[bdg])
            for cc in range(8):
                pb0 = (cc % 2) * 4
                for h in range(8):
                    zs = nz[0] % 2
                    ms = nz[0] % 3
                    nz[0] += 1
                    z3v = Z3[zs].rearrange("p (c i) -> p c i", c=16)
                    in0 = bc_ap(s2m[:, h * 128:(h + 1) * 128], [[0, 16], [1, 128]])
                    in1 = bc_ap(d1[:, h * 128 + cc * 16:h * 128 + cc * 16 + 16], [[1, 16], [0, 128]])
                    zeng = "pool" if nz[0] % 4 != 0 else "dve"
                    P.op(zeng, lambda e, z3v=z3v, in0=in0, in1=in1: e.tensor_tensor(out=z3v, in0=in0, in1=in1, op=ALU.add),
                         reads=[bs2m, bd1], writes=[bZ3[zs]])
                    P.op("act", lambda e, zs=zs, h=h: e.activation(out=E3[zs], in_=Z3[zs], func=AF.Exp, bias=tm[:, h:h + 1]),
                         reads=[bZ3[zs], bsm], writes=[bE3[zs]])
                    P.op("dve", lambda e, zs=zs, ms=ms: e.scalar_tensor_tensor(out=M3[ms], in0=Z3[zs], scalar=0.0, in1=E3[zs], op0=ALU.is_ge, op1=ALU.mult),
                         reads=[bZ3[zs], bE3[zs]], writes=[bM3[ms]])
                    for b_ in range(4):
                        P.op("pe", lambda e, b_=b_, h=h, ms=ms, pb0=pb0: e.matmul(ps[pb0 + b_][:], dg[:, h, :], M3[ms][:, b_ * 512:(b_ + 1) * 512],
                                                                                 start=(h == 0), stop=(h == 7)),
                             reads=[bdg, bM3[ms]], writes=[bps[pb0 + b_]], sig=(b_ == 3))
                for b_ in range(4):
                    P.op("act", lambda e, go=go, cc=cc, b_=b_, pb0=pb0: e.copy(out=go[:, cc * 2048 + b_ * 512:cc * 2048 + (b_ + 1) * 512], in_=ps[pb0 + b_][:]),
                         reads=[bps[pb0 + b_]], writes=[bgo])
            P.dma("sp", lambda e, r0=r0, go=go: e.dma_start(out=GS[r0:r0 + 128, :], in_=go), reads=[bgo], writes=[bGS])

        phase_start()
        xnT2 = arena.bf16(32 * TB).rearrange("p (k t) -> p k t", k=32); bxn2 = Buf("xnT2")
        acc = arena.f32(4 * D).rearrange("p (i d) -> p i d", i=4); bacc = [[Buf(f"acc{i}_{c}") for c in range(8)] for i in range(4)]
        tmpa = [arena.f32(512) for _ in range(2)]; btmpa = [Buf("tmpa0"), Buf("tmpa1")]
        ut = [arena.bf16(32 * 256).rearrange("p (k e) -> p k e", k=32) for _ in range(2)]; but = [Buf("ut0"), Buf("ut1")]
        vt = [arena.bf16(2 * D).rearrange("p (c d) -> p c d", c=2) for _ in range(2)]; bvt = [Buf("vt0"), Buf("vt1")]
        gt = [arena.bf16(4 * 256).rearrange("p (i e) -> p i e", i=4) for _ in range(2)]; bgt = [Buf("gt0"), Buf("gt1")]
        ga = [arena.f32(256) for _ in range(2)]; bga = [Buf("ga0"), Buf("ga1")]
        wv = [arena.bf16(256) for _ in range(2)]; bwv = [Buf("w0"), Buf("w1")]
        wT = [arena.bf16(256).rearrange("p (c t) -> p c t", c=2) for _ in range(2)]; bwT = [Buf("wT0"), Buf("wT1")]
        hres = arena.f32(D); bhres = Buf("hres")
        UTv = UT.rearrange("b p (k e) -> b p k e", k=32)
        GSv = GS.rearrange("(i p) e -> p i e", p=128)
        n2 = [0, 0, 0]
        for tb in range(nblk):
            P.dma("sp", lambda e, tb=tb: e.dma_start(out=xnT2, in_=XT[tb].rearrange("p (k t) -> p k t", k=32)), reads=[bXT], writes=[bxn2])

            def ld(eb, tb=tb):
                def f(slot):
                    P.dma("sp", lambda e: e.dma_start(out=ut[slot], in_=UTv[eb]), reads=[bUT], writes=[but[slot]])
                    P.dma("pool", lambda e: e.dma_start(out=vt[slot], in_=peer_v[L][eb * 256:(eb + 1) * 256, :].rearrange("(c p) d -> p c d", p=128)),
                          writes=[bvt[slot]])
                    P.dma("sp", lambda e: e.dma_start(out=gt[slot], in_=GSv[:, tb * 4:(tb + 1) * 4, eb * 256:(eb + 1) * 256]), reads=[bGS], writes=[bgt[slot]])
                return f

            def stage1(eb, ti, pa):
                slot = eb % 2
                for k in range(32):
                    P.op("pe", lambda e, k=k: e.matmul(ps[pa][:, 0:256], xnT2[:, k, ti * 128:(ti + 1) * 128], ut[slot][:, k, :],
                                                       start=(k == 0), stop=(k == 31)),
                         reads=[bxn2, but[slot]], writes=[bps[pa]], sig=(k == 31))
                P.op("act", lambda e: e.activation(out=ga[pa], in_=ps[pa][:, 0:256], func=AF.Gelu), reads=[bps[pa]], writes=[bga[pa]])
                P.op("dve", lambda e: e.tensor_tensor(out=wv[pa], in0=ga[pa], in1=gt[slot][:, ti, :], op=ALU.mult),
                     reads=[bga[pa], bgt[slot]], writes=[bwv[pa]])

            def stage2(eb, ti, pa):
                slot = eb % 2
                pvw = ps[2][:].bitcast(BF16)
                for c in range(2):
                    P.op("pe", lambda e, c=c: e.transpose(pvw[:, c * 128:(c + 1) * 128], wv[pa][:, c * 128:(c + 1) * 128], ident_b),
                         reads=[bwv[pa], bconst], writes=[bps[2]], sig=(c == 1))
                P.op("act", lambda e: e.copy(out=wT[pa], in_=pvw[:, 0:256].rearrange("p (c t) -> p c t", c=2)),
                     reads=[bps[2]], writes=[bwT[pa]])
                for cb in range(8):
                    po = 3 + n2[1] % 5
                    n2[1] += 1
                    for c in range(2):
                        P.op("pe", lambda e, c=c, cb=cb, po=po: e.matmul(ps[po][:], wT[pa][:, c, :], vt[slot][:, c, cb * 512:(cb + 1) * 512],
                                                                        start=(c == 0), stop=(c == 1)),
                             reads=[bwT[pa], bvt[slot]], writes=[bps[po]], sig=(c == 1))
                    av = acc[:, ti, cb * 512:(cb + 1) * 512]
                    if eb == 0:
                        eng0 = "act" if cb % 3 == 1 else "dve"
                        if eng0 == "act":
                            P.op("act", lambda e, av=av, po=po: e.copy(out=av, in_=ps[po][:]), reads=[bps[po]], writes=[bacc[ti][cb]])
                        else:
                            P.op("dve", lambda e, av=av, po=po: e.tensor_copy(out=av, in_=ps[po][:]), reads=[bps[po]], writes=[bacc[ti][cb]])
                    elif cb % 3 == 1:
                        ts_ = n2[2] % 2
                        n2[2] += 1
                        P.op("act", lambda e, po=po, ts_=ts_: e.copy(out=tmpa[ts_], in_=ps[po][:]), reads=[bps[po]], writes=[btmpa[ts_]])
                        P.op("pool", lambda e, av=av, ts_=ts_: e.tensor_tensor(out=av, in0=av, in1=tmpa[ts_], op=ALU.add),
                             reads=[btmpa[ts_]], writes=[bacc[ti][cb]])
                    else:
                        P.op("dve", lambda e, av=av, po=po: e.tensor_tensor(out=av, in0=av, in1=ps[po][:], op=ALU.add), reads=[bps[po]], writes=[bacc[ti][cb]])

            pairs = [(eb, ti) for eb in range(64) for ti in range(4)]
            ld(0)(0)
            stage1(0, 0, 0)
            for idx, (eb, ti) in enumerate(pairs):
                if ti == 0 and eb + 1 < 64:
                    ld(eb + 1)((eb + 1) % 2)
                if idx + 1 < len(pairs):
                    stage1(pairs[idx + 1][0], pairs[idx + 1][1], (idx + 1) % 2)
                stage2(eb, ti, idx % 2)
            for ti in range(4):
                r0 = tb * TB + ti * 128
                P.dma("sp", lambda e, r0=r0: e.dma_start(out=hres, in_=hin[r0:r0 + 128, :]), writes=[bhres])
                P.op("dve", lambda e, ti=ti: e.tensor_tensor(out=hres, in0=hres, in1=acc[:, ti, :], op=ALU.add), reads=bacc[ti], writes=[bhres])
                P.dma("sp", lambda e, r0=r0: e.dma_start(out=hout[r0:r0 + 128, :], in_=hres), reads=[bhres], writes=[Buf()])

    cur = x
    if "attn" in stages:
        cur = h1
    if "peer0" in stages:
        peer_layer(0, cur, h2, NBQ)
        cur = h2
    if "conv" in stages:
        conv_layer(cur, h3)
        cur = h3
    if "peer1" in stages:
        peer_layer(1, cur, h4, NBO)
        cur = h4

    P.barrier()
    P.emit(nc, st)
    st.close()
    return nc


def host_consts():
    ident = np.eye(128, dtype=np.float32)
    ones = np.ones((128, 128), np.float32)
    rot = np.zeros((128, 128), np.float32)
    for m in range(128):
        half = (m % 64) // 32
        if half == 0:
            rot[m + 32, m] = -1.0
        else:
            rot[m - 32, m] = 1.0
    consts = np.concatenate([ident, ones, rot, np.zeros((128, 256), np.float32)], axis=1)
    pos = np.arange(T)
    row = (pos // 64).astype(np.float32)
    col = (pos % 64).astype(np.float32)
    inv_freq = (10000.0 ** (-np.arange(0, 64, 2, dtype=np.float32) / 64)).astype(np.float32)
    ang = np.zeros((128, T), np.float32)
    for d in range(128):
        axis = d // 64
        f = d % 32
        ang[d] = (row if axis == 0 else col) * inv_freq[f]
    rope = np.stack([np.cos(ang), np.sin(ang)]).astype(np.float32)
    return consts, rope


def make_in_maps(inputs, ncores, stages):
    consts, rope = host_consts()
    maps = []
    gains = np.stack([inputs["attn_q_gain"][0], inputs["attn_k_gain"][0]], axis=1).astype(np.float32)
    for c in range(ncores):
        b, half = (c // 2, c % 2) if ncores == 8 else (c, 0)
        xs = inputs["x"][b]
        rp = rope
        if half == 1:
            xs = xs[::-1]
            rp = rope[:, :, ::-1]
        m = {"x": np.ascontiguousarray(xs), "mixer_norm_g": inputs["mixer_norm_g"],
             "ffn_norm_g": inputs["ffn_norm_g"], "consts": consts, "rope_cs": np.ascontiguousarray(rp), "qk_gain": gains}
        if "attn" in stages:
            m["attn_w_qkv"] = inputs["attn_w_qkv"][0]
            m["attn_w_o"] = inputs["attn_w_o"][0]
        if "conv" in stages:
            m["conv_w_in"] = inputs["conv_w_in"][0]
            m["conv_w_out"] = inputs["conv_w_out"][0]
            cw = inputs["conv_w"][0]
            if half == 1:
                cw = cw[::-1]
            wb = np.concatenate([cw, inputs["conv_b"]], axis=0)
            m["conv_wb"] = np.ascontiguousarray(wb.T.reshape(32, 128, 4))
        if "peer0" in stages or "peer1" in stages:
            m["peer_w_query"] = inputs["peer_w_query"]
            m["peer_sub_keys"] = np.ascontiguousarray(inputs["peer_sub_keys"].reshape(2, 16, 128, 128))
            m["peer_u"] = inputs["peer_u"]
            m["peer_v"] = inputs["peer_v"]
        maps.append(m)
    return maps


def kernel(**inputs):
    inputs = {k: np.asarray(v) for k, v in inputs.items()}
    stages = ("attn", "peer0", "conv", "peer1")
    nc = build(stages)
    maps = make_in_maps(inputs, NCORES, stages)
    res = run_bass_kernel_spmd(nc, maps, core_ids=list(range(NCORES)))
    out = np.empty((4, T, D), np.float32)
    for c in range(NCORES):
        b, half = c // 2, c % 2
        y = res.results[c]["y"]
        if half == 0:
            out[b, :TOWN] = y
        else:
            out[b, TOWN:] = y[::-1]
    return out
```

```python
from contextlib import ExitStack
import numpy as np
import concourse.bass as bass
import concourse.mybir as mybir
from concourse.bass_utils import run_bass_kernel_spmd

F32 = mybir.dt.float32
BF16 = mybir.dt.bfloat16
ALU = mybir.AluOpType
AF = mybir.ActivationFunctionType
AX = mybir.AxisListType

ENGS = ["pe", "act", "dve", "pool", "sp"]
DMA_RING = 8
SAME_ENGINE_SYNC = True

D = 4096
T = 4096
NT = T // 128
TB = 512
NB = T // TB
NE = 16384
EPS = 1e-6
NCORES = 8
NBQ = 5
NBO = 4
TL = NBQ * TB
TOWN = NBO * TB


class Buf:
    __slots__ = ("name", "last_w", "readers")

    def __init__(self, name=""):
        self.name = name
        self.last_w = None
        self.readers = {}


class Prog:
    def __init__(self):
        self.ops = {e: [] for e in ENGS}
        self.cnt = {e: 0 for e in ENGS}
        self.seen = {e: {} for e in ENGS}
        self.dma_n = {e: 0 for e in ENGS}
        self.last = {}
        self.semkeys = set()
        self.pending = {}

    def _deps(self, reads, writes):
        deps = {}

        def add(k, v):
            if deps.get(k, 0) < v:
                deps[k] = v
        for b in reads:
            if b.last_w is not None:
                add(*b.last_w)
        for b in writes:
            if b.last_w is not None:
                add(*b.last_w)
            for k, v in b.readers.items():
                add(k, v)
        return deps

    def _commit(self, ev, reads, writes):
        k, v = ev
        self.last[k] = v
        for b in reads:
            if b.readers.get(k, 0) < v:
                b.readers[k] = v
        for b in writes:
            b.last_w = ev
            b.readers = {}

    def _waits(self, eng, deps):
        waits = []
        seen = self.seen[eng]
        for k, v in deps.items():
            if k == eng and (eng == "pe" or not SAME_ENGINE_SYNC):
                continue
            if seen.get(k, 0) >= v:
                continue
            seen[k] = v
            waits.append((k, v))
        return waits

    def op(self, eng, fn, reads=(), writes=(), sig=True):
        deps = self._deps(reads, writes)
        waits = self._waits(eng, deps)
        if not sig:
            self.ops[eng].append((waits, fn, None, 0))
            self.pending.setdefault(eng, []).append((tuple(reads), tuple(writes)))
            return
        self.cnt[eng] += 1
        ev = (eng, self.cnt[eng])
        self.semkeys.add(eng)
        self.ops[eng].append((waits, fn, eng, 1))
        for r_, w_ in self.pending.pop(eng, []):
            self._commit(ev, r_, w_)
        self._commit(ev, reads, writes)

    def dma(self, q, fn, reads=(), writes=()):
        deps = self._deps(reads, writes)
        n = self.dma_n[q]
        self.dma_n[q] += 1
        slot, rnd = n % DMA_RING, n // DMA_RING
        key = ("dma", q, slot)
        self.semkeys.add(key)
        if rnd > 0 and deps.get(key, 0) < 16 * rnd:
            deps[key] = 16 * rnd
        waits = self._waits(q, deps)
        ev = (key, 16 * (rnd + 1))
        self.ops[q].append((waits, fn, key, 16))
        self._commit(ev, reads, writes)

    def barrier(self):
        assert not any(self.pending.values()), "silent op without a following signalling op"
        for e in ENGS:
            waits = self._waits(e, dict(self.last))
            if waits:
                self.ops[e].append((waits, None, None, 0))

    def emit(self, nc, stack):
        sems = {}
        for k in sorted(self.semkeys, key=str):
            nm = "s_" + (k if isinstance(k, str) else "_".join(str(x) for x in k))
            sems[k] = stack.enter_context(nc.semaphore(nm))
        block = stack.enter_context(nc.Block())

        def run(name):
            def body(eng):
                for waits, fn, key, inc in self.ops[name]:
                    for k, v in waits:
                        eng.wait_ge(sems[k], v)
                    if fn is not None:
                        ins = fn(eng)
                        if inc:
                            ins.then_inc(sems[key], inc)
            return body
        m = {"pe": block.tensor, "act": block.scalar, "dve": block.vector,
             "pool": block.gpsimd, "sp": block.sync}
        for e in ENGS:
            if self.ops[e]:
                m[e](run(e))


class Arena:
    def __init__(self, nc, st, nbytes):
        self.cap = nbytes
        self.t = st.enter_context(nc.sbuf_tensor("arena", [128, nbytes // 4], F32))
        self.off = 0

    def reset(self):
        self.off = 0

    def _take(self, nbytes):
        nbytes = (nbytes + 63) // 64 * 64
        o = self.off
        self.off += nbytes
        assert self.off <= self.cap, (self.off, self.cap)
        return o

    def f32(self, n):
        o = self._take(n * 4)
        return self.t[:, o // 4:o // 4 + n]

    def bf16(self, n):
        assert n % 2 == 0
        o = self._take(n * 2)
        return self.t[:, o // 4:o // 4 + n // 2].bitcast(BF16)


def bc_ap(ap, dims):
    return bass.AP(tensor=ap.tensor, offset=ap.offset, ap=[list(ap.ap[0])] + [list(d) for d in dims])


def row_bcast(dram_ap_1d, n):
    return bass.AP(tensor=dram_ap_1d.tensor, offset=dram_ap_1d.offset, ap=[[0, 128], [1, n]])


def build(stages=("attn", "peer0", "conv", "peer1"), out_stage=None, dbg=False):
    nc = bass.Bass("TRN2", target_bir_lowering=False)
    P = Prog()
    st = ExitStack()

    def din(name, shape, dt=F32):
        return nc.dram_tensor(name, list(shape), dt, kind="ExternalInput").ap()

    out_stage = out_stage or stages[-1]

    def dscr(name, shape, dt=F32, stage=None):
        kind = "ExternalOutput" if (stage is not None and stage == out_stage) else "Internal"
        nm = "y" if kind == "ExternalOutput" else name
        return nc.dram_tensor(nm, list(shape), dt, kind=kind).ap()

    x = din("x", [T, D])
    mixer_g = din("mixer_norm_g", [2, D])
    ffn_g = din("ffn_norm_g", [2, D])
    consts = din("consts", [128, 5 * 128])
    rope = din("rope_cs", [2, 128, T])
    gains = din("qk_gain", [128, 2])
    if "attn" in stages:
        w_qkv = din("attn_w_qkv", [D, 6144])
        w_o = din("attn_w_o", [D, D])
    if "conv" in stages:
        w_in = din("conv_w_in", [D, 3 * D])
        conv_wb = din("conv_wb", [32, 128, 4])
        w_out = din("conv_w_out", [D, D])
    if "peer0" in stages or "peer1" in stages:
        w_query = din("peer_w_query", [2, D, 2048])
        sub_keys = din("peer_sub_keys", [2, 16, 128, 128])
        peer_u = din("peer_u", [2, NE, D])
        peer_v = din("peer_v", [2, NE, D])

    h1 = dscr("h1", [T, D], stage="attn")
    h2 = dscr("h2", [T, D], stage="peer0")
    h3 = dscr("h3", [TOWN, D], stage="conv")
    h4 = dscr("h4", [TOWN, D], stage="peer1")

    arena = Arena(nc, st, 206 * 1024)
    ps = [st.enter_context(nc.psum_tensor(f"ps{i}", [128, 512], F32)) for i in range(8)]
    bps = [Buf(f"ps{i}") for i in range(8)]

    c_f32 = arena.f32(5 * 128)
    ident_f = c_f32[:, 0:128]
    rot_f = c_f32[:, 256:384]
    ident_b = arena.bf16(128)
    ones_b = arena.bf16(128)
    gain_t = arena.f32(2)
    bconst = Buf("const")
    P.dma("sp", lambda e: e.dma_start(out=c_f32, in_=consts), writes=[bconst])
    P.dma("sp", lambda e: e.dma_start(out=gain_t, in_=gains), writes=[bconst])
    P.op("dve", lambda e: e.tensor_copy(out=ident_b, in_=c_f32[:, 0:128]), reads=[bconst], writes=[bconst])
    P.op("dve", lambda e: e.tensor_copy(out=ones_b, in_=c_f32[:, 128:256]), reads=[bconst], writes=[bconst])
    base_off = arena.off

    dumps = {}

    def dump(name, ap, buf, dt):
        if not dbg or name in dumps:
            return
        shp = list(ap.shape)
        d_ = nc.dram_tensor("dbg_" + name, shp, dt, kind="ExternalOutput").ap()
        dumps[name] = d_
        P.dma("sp", lambda e: e.dma_start(out=d_, in_=ap), reads=[buf], writes=[Buf()])

    def phase_start():
        P.barrier()
        arena.off = base_off

    def norm_block(src, g_b, bg, tb, xnT, bxnT, xt_bufs, keep_x=None):
        for ti in range(4):
            row0 = tb * TB + ti * 128
            if keep_x is not None:
                xt, bxt = keep_x[ti]
            else:
                xt, bxt = xt_bufs[ti % len(xt_bufs)]
            P.dma("sp", lambda e, xt=xt, row0=row0: e.dma_start(out=xt, in_=src[row0:row0 + 128, :]), writes=[bxt])
            junk, bjunk = norm_block.junk
            ssq, bssq = norm_block.ssq
            P.op("act", lambda e, xt=xt: e.activation(out=junk, in_=xt, func=AF.Square),
                 reads=[bxt], writes=[bjunk])
            P.op("dve", lambda e: e.reduce_sum(out=ssq[:, 0:1], in_=junk, axis=AX.X), reads=[bjunk], writes=[bssq])
            P.op("dve", lambda e: e.tensor_scalar(out=ssq[:, 1:2], in0=ssq[:, 0:1], scalar1=1.0 / D, scalar2=EPS,
                                                  op0=ALU.mult, op1=ALU.add), reads=[bssq], writes=[bssq])
            P.op("act", lambda e: e.activation(out=ssq[:, 3:4], in_=ssq[:, 1:2], func=AF.Sqrt), reads=[bssq], writes=[bssq])
            P.op("dve", lambda e: e.reciprocal(out=ssq[:, 2:3], in_=ssq[:, 3:4]), reads=[bssq], writes=[bssq])
            xn, bxn = norm_block.xn
            P.op("dve", lambda e, xt=xt: e.scalar_tensor_tensor(out=xn, in0=xt, scalar=ssq[:, 2:3], in1=g_b,
                                                                op0=ALU.mult, op1=ALU.mult),
                 reads=[bxt, bssq, bg], writes=[bxn])
            dump("xt", xt, bxt, F32); dump("ssq", ssq, bssq, F32); dump("xn", xn, bxn, BF16)
            for k8 in range(4):
                pi = 6 + (k8 % 2)
                pv = ps[pi][:].bitcast(BF16)
                for kk in range(8):
                    k = k8 * 8 + kk
                    P.op("pe", lambda e, k=k, kk=kk, pv=pv: e.transpose(pv[:, kk * 128:(kk + 1) * 128],
                                                                      xn[:, k * 128:(k + 1) * 128], ident_b),
                         reads=[bxn, bconst], writes=[bps[pi]], sig=(kk == 7))
                eng = "act" if k8 % 2 == 0 else "dve"
                dst = xnT[:, k8 * 8:(k8 + 1) * 8, ti * 128:(ti + 1) * 128]
                srcv = pv.rearrange("p (k t) -> p k t", k=8)
                if eng == "act":
                    P.op("act", lambda e, dst=dst, srcv=srcv: e.copy(out=dst, in_=srcv), reads=[bps[pi]], writes=[bxnT])
                else:
                    P.op("dve", lambda e, dst=dst, srcv=srcv: e.tensor_copy(out=dst, in_=srcv), reads=[bps[pi]], writes=[bxnT])

    def norm_setup():
        norm_block.junk = (arena.bf16(D), Buf("junk"))
        norm_block.ssq = (arena.f32(4), Buf("ssq"))
        norm_block.xn = (arena.bf16(D), Buf("xn"))

    def load_gain_row(g_dram_row):
        g_b = arena.f32(D)
        bg = Buf("g")
        P.dma("sp", lambda e: e.dma_start(out=g_b, in_=row_bcast(g_dram_row, D)), writes=[bg])
        return g_b, bg

    def wstream(loads, compute):
        n = len(loads)
        loads[0](0)
        for i in range(n):
            if i + 1 < n:
                loads[i + 1]((i + 1) % 2)
            compute(i, i % 2)

    def proj_residual(srcT, bsrcT, w_dram, res_src, dst, nblk):
        phase_start()
        ot = arena.bf16(32 * TB).rearrange("p (h t) -> p h t", h=32); bot = Buf("ot")
        wC = [arena.bf16(32 * 512).rearrange("p (k c) -> p k c", k=32) for _ in range(2)]
        bwC = [Buf("wo0"), Buf("wo1")]
        xts = [(arena.f32(D), Buf(f"xr{i}")) for i in range(4)]
        wo_v = w_dram.rearrange("(h p) c -> p h c", p=128)
        cnt = [0]
        for tb in range(nblk):
            P.dma("sp", lambda e, tb=tb: e.dma_start(out=ot, in_=srcT[:, :, tb * TB:(tb + 1) * TB].rearrange("h p t -> p h t")),
                  reads=[bsrcT], writes=[bot])
            for ti in range(4):
                r0 = tb * TB + ti * 128
                P.dma("sp", lambda e, ti=ti, r0=r0: e.dma_start(out=xts[ti][0], in_=res_src[r0:r0 + 128, :]), writes=[xts[ti][1]])

            def ld(cb):
                def f(slot):
                    P.dma("pool", lambda e: e.dma_start(out=wC[slot], in_=wo_v[:, :, cb * 512:(cb + 1) * 512]), writes=[bwC[slot]])
                return f

            def comp(cb, slot):
                for ti in range(4):
                    pq = cnt[0] % 4
                    cnt[0] += 1
                    for h in range(32):
                        P.op("pe", lambda e, h=h, ti=ti, pq=pq: e.matmul(ps[pq][:], ot[:, h, ti * 128:(ti + 1) * 128], wC[slot][:, h, :],
                                                                         start=(h == 0), stop=(h == 31)),
                                 sig=(h == 31),
                             reads=[bot, bwC[slot]], writes=[bps[pq]])
                    xs = xts[ti][0][:, cb * 512:(cb + 1) * 512]
                    P.op("dve", lambda e, xs=xs, pq=pq: e.tensor_tensor(out=xs, in0=xs, in1=ps[pq][:], op=ALU.add),
                         reads=[bps[pq]], writes=[xts[ti][1]])
            wstream([ld(cb) for cb in range(8)], comp)
            for ti in range(4):
                r0 = tb * TB + ti * 128
                P.dma("sp", lambda e, ti=ti, r0=r0: e.dma_start(out=dst[r0:r0 + 128, :], in_=xts[ti][0]), reads=[xts[ti][1]], writes=[Buf()])

    if "attn" in stages:
        QT = nc.dram_tensor("QT", [32, 128, T], BF16, kind=("ExternalOutput" if dbg else "Internal")).ap()
        KT = nc.dram_tensor("KT", [8, 128, T], BF16, kind=("ExternalOutput" if dbg else "Internal")).ap()
        VS = nc.dram_tensor("VS", [T, 1024], BF16, kind=("ExternalOutput" if dbg else "Internal")).ap()
        OT = nc.dram_tensor("OT", [32, 128, T], BF16, kind=("ExternalOutput" if dbg else "Internal")).ap()
        bQT, bKT, bVS, bOT = Buf("QT"), Buf("KT"), Buf("VS"), Buf("OT")

        phase_start()
        g_b, bg = load_gain_row(mixer_g[0])
        norm_setup()
        cs = arena.f32(2 * T).rearrange("p (c t) -> p c t", c=2)
        bcs = Buf("cs")
        P.dma("sp", lambda e: e.dma_start(out=cs, in_=rope.rearrange("c p t -> p c t")), writes=[bcs])
        xnT = arena.bf16(32 * TB).rearrange("p (k t) -> p k t", k=32)
        bxnT = Buf("xnT")
        xt_bufs = [(arena.f32(D), Buf("xt0"))]
        wt = [arena.bf16(32 * 512).rearrange("p (k c) -> p k c", k=32) for _ in range(2)]
        bwt = [Buf("wt0"), Buf("wt1")]
        sq = arena.bf16(512); bsq = Buf("sq")
        rstd = arena.f32(512); brstd = Buf("rstd")
        qn = arena.f32(512); bqn = Buf("qn")
        t1 = arena.f32(512); bt1 = Buf("t1")
        t2 = arena.f32(512); bt2 = Buf("t2")
        qo = [arena.bf16(512) for _ in range(2)]; bqo = [Buf("qo0"), Buf("qo1")]
        vo = [arena.bf16(512) for _ in range(2)]; bvo = [Buf("vo0"), Buf("vo1")]
        wq_v = w_qkv.rearrange("(k p) c -> p k c", p=128)
        cnt = {"q": 0, "v": 0}
        for tb in range(1 if dbg == 2 else NB):
            norm_block(x, g_b, bg, tb, xnT, bxnT, xt_bufs)

            dump("xnT", xnT, bxnT, BF16)

            def ld(cb):
                def f(slot):
                    P.dma("pool", lambda e: e.dma_start(out=wt[slot], in_=wq_v[:, :, cb * 512:(cb + 1) * 512]),
                          writes=[bwt[slot]])
                    dump("wt", wt[slot], bwt[slot], BF16)
                return f

            def comp(cb, slot, tb=tb):
                if cb < 10:
                    def mm_(j):
                        hb = cb * 4 + j
                        pq = cnt["q"] % 2
                        cnt["q"] += 1
                        for k in range(32):
                            P.op("pe", lambda e, k=k, j=j, pq=pq: e.matmul(ps[pq][:], wt[slot][:, k, j * 128:(j + 1) * 128],
                                                                           xnT[:, k, :], start=(k == 0), stop=(k == 31)),
                                 sig=(k == 31),
                                 reads=[bwt[slot], bxnT], writes=[bps[pq]])
                        return hb, pq

                    def post_(j, hb, pq):
                        gcol = gain_t[:, 0:1] if hb < 32 else gain_t[:, 1:2]
                        P.op("act", lambda e, pq=pq: e.activation(out=sq, in_=ps[pq][:], func=AF.Square),
                             reads=[bps[pq]], writes=[bsq])
                        P.op("pe", lambda e: e.matmul(ps[2][:], ones_b, sq, start=True, stop=True),
                             reads=[bsq, bconst], writes=[bps[2]])
                        P.op("dve", lambda e: e.tensor_scalar(out=rstd, in0=ps[2][:], scalar1=1.0 / 128, scalar2=EPS,
                                                              op0=ALU.mult, op1=ALU.add), reads=[bps[2]], writes=[brstd])
                        P.op("act", lambda e: e.activation(out=rstd, in_=rstd, func=AF.Sqrt), reads=[brstd], writes=[brstd])
                        P.op("dve", lambda e: e.reciprocal(out=rstd, in_=rstd), reads=[brstd], writes=[brstd])
                        P.op("dve", lambda e, pq=pq, gcol=gcol: e.scalar_tensor_tensor(out=qn, in0=ps[pq][:], scalar=gcol, in1=rstd,
                                                                                       op0=ALU.mult, op1=ALU.mult),
                             reads=[bps[pq], brstd, bconst], writes=[bqn])
                        dump("rstd", rstd, brstd, F32); dump("qn", qn, bqn, F32); dump("sq", sq, bsq, BF16)
                        P.op("pe", lambda e: e.matmul(ps[3][:], rot_f, qn, start=True, stop=True),
                             reads=[bqn, bconst], writes=[bps[3]])
                        tsl = slice(tb * TB, (tb + 1) * TB)
                        P.op("pool", lambda e, tsl=tsl: e.tensor_tensor(out=t1, in0=qn, in1=cs[:, 0, tsl], op=ALU.mult),
                             reads=[bqn, bcs], writes=[bt1])
                        P.op("dve", lambda e, tsl=tsl: e.tensor_tensor(out=t2, in0=ps[3][:], in1=cs[:, 1, tsl], op=ALU.mult),
                             reads=[bps[3], bcs], writes=[bt2])
                        qs = hb % 2
                        P.op("dve", lambda e, qs=qs: e.tensor_tensor(out=qo[qs], in0=t1, in1=t2, op=ALU.add),
                             reads=[bt1, bt2], writes=[bqo[qs]])
                        dump("t1", t1, bt1, F32); dump("t2", t2, bt2, F32); dump("qo", qo[qs], bqo[qs], BF16)
                        if hb < 32:
                            dst, bd = QT[hb][:, tb * TB:(tb + 1) * TB], bQT
                        else:
                            dst, bd = KT[hb - 32][:, tb * TB:(tb + 1) * TB], bKT
                        P.dma("sp", lambda e, qs=qs, dst=dst: e.dma_start(out=dst, in_=qo[qs]), reads=[bqo[qs]], writes=[bd])
                    nxt = mm_(0)
                    for j in range(4):
                        cur_ = nxt
                        if j + 1 < 4:
                            nxt = mm_(j + 1)
                        post_(j, *cur_)
                else:
                    for ti in range(4):
                        pq = 4 + cnt["v"] % 2
                        vs = cnt["v"] % 2
                        cnt["v"] += 1
                        for k in range(32):
                            P.op("pe", lambda e, k=k, ti=ti, pq=pq: e.matmul(ps[pq][:], xnT[:, k, ti * 128:(ti + 1) * 128],
                                                                             wt[slot][:, k, :], start=(k == 0), stop=(k == 31)),
                                 sig=(k == 31),
                                 reads=[bwt[slot], bxnT], writes=[bps[pq]])
                        P.op("act", lambda e, pq=pq, vs=vs: e.copy(out=vo[vs], in_=ps[pq][:]), reads=[bps[pq]], writes=[bvo[vs]])
                        dump("vo", vo[vs], bvo[vs], BF16)
                        r0 = tb * TB + ti * 128
                        c0 = (cb - 10) * 512
                        P.dma("sp", lambda e, vs=vs, r0=r0, c0=c0: e.dma_start(out=VS[r0:r0 + 128, c0:c0 + 512], in_=vo[vs]),
                              reads=[bvo[vs]], writes=[bVS])
            cbs = list(range(12)) if (tb < NBQ or dbg) else list(range(8, 12))
            wstream([ld(cb) for cb in cbs], lambda i, slot, cbs=cbs: comp(cbs[i], slot))

        if dbg in (2, 3):
            P.barrier(); P.emit(nc, st); st.close(); return nc
        phase_start()
        kt = arena.bf16(T); bkt = Buf("kt")
        vt = arena.bf16(32 * 130).rearrange("p (c d) -> p c d", c=32); bvt = Buf("vt")
        qt = arena.bf16(4 * T).rearrange("p (h t) -> p h t", h=4); bqt = Buf("qt")
        pT = [arena.bf16(512) for _ in range(3)]; bpT = [Buf(f"pT{i}") for i in range(3)]
        rden = arena.f32(4); brden = Buf("rden")
        on = arena.bf16(128 * 4).rearrange("p (h d) -> p h d", h=4); bon = Buf("on")
        oT = [arena.bf16(4 * 512).rearrange("p (h t) -> p h t", h=4) for _ in range(2)]; boT = [Buf("oT0"), Buf("oT1")]
        scale = 128.0 ** -0.5
        for g in range(8):
            P.dma("sp", lambda e, g=g: e.dma_start(out=kt, in_=KT[g]), writes=[bkt], reads=[bKT])
            P.dma("sp", lambda e, g=g: e.dma_start(out=vt[:, :, 0:128],
                                                   in_=VS[:, g * 128:(g + 1) * 128].rearrange("(c p) d -> p c d", p=128)),
                  writes=[bvt], reads=[bVS])
            P.op("pool", lambda e: e.memset(vt[:, :, 128:129], 1.0), writes=[bvt])
            P.dma("sp", lambda e, g=g: e.dma_start(out=qt, in_=QT[4 * g:4 * g + 4].rearrange("h p t -> p h t")),
                  writes=[bqt], reads=[bQT])
            NQ = NT if dbg else NBQ * 4
            seq = [(qi, c) for qi in range(NQ) for c in range(32)]

            def S_(idx):
                qi, c = seq[idx]
                sp_, pslot = idx % 2, idx % 3
                P.op("pe", lambda e: e.matmul(ps[sp_][:], kt[:, c * 128:(c + 1) * 128], qt[:, :, qi * 128:(qi + 1) * 128], start=True, stop=True),
                     reads=[bkt, bqt], writes=[bps[sp_]])
                P.op("act", lambda e: e.activation(out=pT[pslot], in_=ps[sp_][:], func=AF.Exp, scale=scale),
                     reads=[bps[sp_]], writes=[bpT[pslot]])

            def PV_(idx, g=g):
                qi, c = seq[idx]
                pslot = idx % 3
                for j in range(4):
                    P.op("pe", lambda e, j=j: e.matmul(ps[2 + j][:, 0:129], pT[pslot][:, j * 128:(j + 1) * 128], vt[:, c, 0:129],
                                                       start=(c == 0), stop=(c == 31)),
                         reads=[bpT[pslot], bvt], writes=[bps[2 + j]], sig=(j == 3))
                if c != 31:
                    return
                os_ = (qi // 4) % 2
                for j in range(4):
                    P.op("dve", lambda e, j=j: e.reciprocal(out=rden[:, j:j + 1], in_=ps[2 + j][:, 128:129]),
                         reads=[bps[2 + j]], writes=[brden])
                    P.op("dve", lambda e, j=j: e.tensor_scalar(out=on[:, j, :], in0=ps[2 + j][:, 0:128], scalar1=rden[:, j:j + 1],
                                                               scalar2=None, op0=ALU.mult),
                         reads=[bps[2 + j], brden], writes=[bon])
                pb = 6 + qi % 2
                pv = ps[pb][:].bitcast(BF16)
                for j in range(4):
                    P.op("pe", lambda e, j=j: e.transpose(pv[:, j * 128:(j + 1) * 128], on[:, j, :], ident_b),
                         reads=[bon, bconst], writes=[bps[pb]], sig=(j == 3))
                P.op("dve", lambda e: e.tensor_copy(out=oT[os_][:, :, (qi % 4) * 128:(qi % 4 + 1) * 128],
                                                    in_=pv[:, 0:512].rearrange("p (h t) -> p h t", h=4)),
                     reads=[bps[pb]], writes=[boT[os_]])
                if qi % 4 == 3:
                    t0 = (qi // 4) * 512
                    P.dma("sp", lambda e: e.dma_start(out=OT[4 * g:4 * g + 4][:, :, t0:t0 + 512].rearrange("h p t -> p h t"), in_=oT[os_]),
                          reads=[boT[os_]], writes=[bOT])
            S_(0)
            for idx in range(len(seq)):
                if idx + 1 < len(seq):
                    S_(idx + 1)
                PV_(idx)

        if dbg == 4:
            P.barrier(); P.emit(nc, st); st.close(); return nc
        proj_residual(OT, bOT, w_o, x, h1, NBQ)

    def conv_layer(hin, hout):
        UTs = nc.dram_tensor("cUT", [32, 128, T], F32, kind="Internal").ap()
        BTs = nc.dram_tensor("cBT", [32, 128, T], F32, kind="Internal").ap()
        YT = nc.dram_tensor("cYT", [32, 128, T], BF16, kind="Internal").ap()
        bUTs, bBTs, bYT = Buf("cUT"), Buf("cBT"), Buf("cYT")
        phase_start()
        g_b, bg = load_gain_row(mixer_g[1])
        norm_setup()
        xnT = arena.bf16(32 * TB).rearrange("p (k t) -> p k t", k=32); bxnT = Buf("xnT")
        xt_bufs = [(arena.f32(D), Buf("xt0"))]
        wt = [arena.bf16(32 * 512).rearrange("p (k c) -> p k c", k=32) for _ in range(2)]
        bwt = [Buf("wi0"), Buf("wi1")]
        cbuf = [arena.f32(512) for _ in range(4)]; bcbuf = [Buf(f"cb{i}") for i in range(4)]
        obuf = [arena.f32(512) for _ in range(2)]; bobuf = [Buf("ob0"), Buf("ob1")]
        wi_v = w_in.rearrange("(k p) c -> p k c", p=128)
        cnt = [0, 0]
        for tb in range(NBQ):
            norm_block(hin, g_b, bg, tb, xnT, bxnT, xt_bufs)
            items = []
            for cb4 in range(8):
                for kind in ("c", "x", "b"):
                    items.append((cb4, kind))

            def ld(it):
                cb4, kind = it
                col0 = {"b": 0, "c": D, "x": 2 * D}[kind] + cb4 * 512

                def f(slot):
                    P.dma("pool", lambda e: e.dma_start(out=wt[slot], in_=wi_v[:, :, col0:col0 + 512]), writes=[bwt[slot]])
                return f

            def comp(i, slot, tb=tb, items=items):
                cb4, kind = items[i]
                for j in range(4):
                    pq = cnt[0] % 4
                    cnt[0] += 1
                    for k in range(32):
                        P.op("pe", lambda e, k=k, j=j, pq=pq: e.matmul(ps[pq][:], wt[slot][:, k, j * 128:(j + 1) * 128], xnT[:, k, :],
                                                                       start=(k == 0), stop=(k == 31)),
                                 sig=(k == 31),
                             reads=[bwt[slot], bxnT], writes=[bps[pq]])
                    chunk = cb4 * 4 + j
                    if kind == "c":
                        P.op("act", lambda e, j=j, pq=pq: e.copy(out=cbuf[j], in_=ps[pq][:]), reads=[bps[pq]], writes=[bcbuf[j]])
                    else:
                        os_ = cnt[1] % 2
                        cnt[1] += 1
                        if kind == "x":
                            P.op("dve", lambda e, j=j, pq=pq, os_=os_: e.tensor_tensor(out=obuf[os_], in0=cbuf[j], in1=ps[pq][:], op=ALU.mult),
                                 reads=[bps[pq], bcbuf[j]], writes=[bobuf[os_]])
                            dst_, bd = UTs[chunk][:, tb * TB:(tb + 1) * TB], bUTs
                        else:
                            P.op("act", lambda e, pq=pq, os_=os_: e.copy(out=obuf[os_], in_=ps[pq][:]), reads=[bps[pq]], writes=[bobuf[os_]])
                            dst_, bd = BTs[chunk][:, tb * TB:(tb + 1) * TB], bBTs
                        P.dma("sp", lambda e, os_=os_, dst_=dst_: e.dma_start(out=dst_, in_=obuf[os_]), reads=[bobuf[os_]], writes=[bd])
            wstream([ld(it) for it in items], comp)

        phase_start()
        ub = [arena.f32(TL + 2) for _ in range(2)]; bub = [Buf("ub0"), Buf("ub1")]
        bt = [arena.f32(TL) for _ in range(2)]; bbt = [Buf("bt0"), Buf("bt1")]
        cv = arena.f32(TL); bcv = Buf("cv")
        yb = [arena.bf16(TL) for _ in range(2)]; byb = [Buf("yb0"), Buf("yb1")]
        wb = [arena.f32(4) for _ in range(2)]; bwb = [Buf("wb0"), Buf("wb1")]
        for s_ in range(2):
            P.op("pool", lambda e, s_=s_: e.memset(ub[s_][:, 0:1], 0.0), writes=[bub[s_]])
            P.op("pool", lambda e, s_=s_: e.memset(ub[s_][:, TL + 1:TL + 2], 0.0), writes=[bub[s_]])
        for ch in range(32):
            s_ = ch % 2
            P.dma("sp", lambda e, ch=ch, s_=s_: e.dma_start(out=ub[s_][:, 1:TL + 1], in_=UTs[ch][:, 0:TL]), reads=[bUTs], writes=[bub[s_]])
            P.dma("sp", lambda e, ch=ch, s_=s_: e.dma_start(out=bt[s_], in_=BTs[ch][:, 0:TL]), reads=[bBTs], writes=[bbt[s_]])
            P.dma("sp", lambda e, ch=ch, s_=s_: e.dma_start(out=wb[s_], in_=conv_wb[ch]), writes=[bwb[s_]])
            P.op("dve", lambda e, s_=s_: e.tensor_scalar(out=cv, in0=ub[s_][:, 0:TL], scalar1=wb[s_][:, 0:1], scalar2=wb[s_][:, 3:4],
                                                         op0=ALU.mult, op1=ALU.add), reads=[bub[s_], bwb[s_]], writes=[bcv])
            P.op("dve", lambda e, s_=s_: e.scalar_tensor_tensor(out=cv, in0=ub[s_][:, 1:TL + 1], scalar=wb[s_][:, 1:2], in1=cv,
                                                                op0=ALU.mult, op1=ALU.add), reads=[bub[s_], bwb[s_]], writes=[bcv])
            P.op("dve", lambda e, s_=s_: e.scalar_tensor_tensor(out=cv, in0=ub[s_][:, 2:TL + 2], scalar=wb[s_][:, 2:3], in1=cv,
                                                                op0=ALU.mult, op1=ALU.add), reads=[bub[s_], bwb[s_]], writes=[bcv])
            P.op("pool", lambda e, s_=s_: e.tensor_tensor(out=yb[s_], in0=cv, in1=bt[s_], op=ALU.mult), reads=[bcv, bbt[s_]], writes=[byb[s_]])
            P.dma("sp", lambda e, ch=ch, s_=s_: e.dma_start(out=YT[ch][:, 0:TL], in_=yb[s_]), reads=[byb[s_]], writes=[bYT])
        proj_residual(YT, bYT, w_out, hin, hout, NBO)

    def peer_layer(L, hin, hout, nblk):
        UT = nc.dram_tensor(f"pUT{L}", [64, 128, 32 * 256], BF16, kind="Internal").ap()
        XT = nc.dram_tensor(f"pXT{L}", [NB, 128, 32 * TB], BF16, kind="Internal").ap()
        SS = nc.dram_tensor(f"pSS{L}", [T, 2048], F32, kind="Internal").ap()
        GS = nc.dram_tensor(f"pGS{L}", [T, NE], BF16, kind="Internal").ap()
        bUT, bXT, bSS, bGS = Buf("pUT"), Buf("pXT"), Buf("pSS"), Buf("pGS")

        phase_start()
        ub = [arena.bf16(4 * D).rearrange("p (c d) -> p c d", c=4) for _ in range(2)]; bub = [Buf("ub0"), Buf("ub1")]
        utb = [arena.bf16(32 * 512).rearrange("p (k e) -> p k e", k=32) for _ in range(2)]; butb = [Buf("utb0"), Buf("utb1")]
        ncp = [0]

        def ld0(eb2):
            def f(slot):
                P.dma("pool", lambda e: e.dma_start(out=ub[slot], in_=peer_u[L][eb2 * 512:(eb2 + 1) * 512, :].rearrange("(c p) d -> p c d", p=128)),
                      writes=[bub[slot]])
            return f

        def comp0(eb2, slot):
            for c in range(4):
                for k8 in range(4):
                    pi = ncp[0] % 4
                    ncp[0] += 1
                    pv = ps[pi][:].bitcast(BF16)
                    for kk in range(8):
                        k = k8 * 8 + kk
                        P.op("pe", lambda e, k=k, kk=kk, c=c, pv=pv: e.transpose(pv[:, kk * 128:(kk + 1) * 128], ub[slot][:, c, k * 128:(k + 1) * 128], ident_b),
                             reads=[bub[slot], bconst], writes=[bps[pi]], sig=(kk == 7))
                    dst_ = utb[slot][:, k8 * 8:(k8 + 1) * 8, c * 128:(c + 1) * 128]
                    srcv = pv.rearrange("p (k t) -> p k t", k=8)
                    if ncp[0] % 2 == 0:
                        P.op("act", lambda e, dst_=dst_, srcv=srcv: e.copy(out=dst_, in_=srcv), reads=[bps[pi]], writes=[butb[slot]])
                    else:
                        P.op("dve", lambda e, dst_=dst_, srcv=srcv: e.tensor_copy(out=dst_, in_=srcv), reads=[bps[pi]], writes=[butb[slot]])
            for hf in range(2):
                P.dma("sp", lambda e, hf=hf: e.dma_start(out=UT[eb2 * 2 + hf].rearrange("p (k e) -> p k e", k=32),
                                                         in_=utb[slot][:, :, hf * 256:(hf + 1) * 256]), reads=[butb[slot]], writes=[bUT])
        wstream([ld0(eb2) for eb2 in range(32)], comp0)

        phase_start()
        g_b, bg = load_gain_row(ffn_g[L])
        norm_setup()
        xnT = arena.bf16(32 * TB).rearrange("p (k t) -> p k t", k=32); bxnT = Buf("xnT")
        xt_bufs = [(arena.f32(D), Buf("xt0"))]
        wq = [arena.bf16(32 * 256).rearrange("p (k c) -> p k c", k=32) for _ in range(2)]; bwq = [Buf("wq0"), Buf("wq1")]
        qT = arena.f32(16 * TB).rearrange("p (h t) -> p h t", h=16); bqT = Buf("qT")
        sk = arena.f32(16 * 128).rearrange("p (h d) -> p h d", h=16); bsk = Buf("sk")
        skT = arena.f32(16 * 128).rearrange("p (h n) -> p h n", h=16); bskT = Buf("skT")
        s_t = [arena.f32(2048) for _ in range(2)]; bs_t = [Buf("st0"), Buf("st1")]
        P.dma("sp", lambda e: e.dma_start(out=sk, in_=sub_keys[L].rearrange("h n d -> n h d")), writes=[bsk])
        for hp in range(16):
            pi = 4 + (hp // 4) % 2
            P.op("pe", lambda e, hp=hp, pi=pi: e.transpose(ps[pi][:, (hp % 4) * 128:(hp % 4 + 1) * 128], sk[:, hp, :], ident_f),
                 reads=[bsk, bconst], writes=[bps[pi]])
            if hp % 4 == 3:
                P.op("dve", lambda e, hp=hp, pi=pi: e.tensor_copy(out=skT[:, hp - 3:hp + 1, :], in_=ps[pi][:].rearrange("p (h n) -> p h n", h=4)),
                     reads=[bps[pi]], writes=[bskT])
        wq_v = w_query[L].rearrange("(k p) c -> p k c", p=128)
        cn = [0, 0]
        for tb in range(nblk):
            norm_block(hin, g_b, bg, tb, xnT, bxnT, xt_bufs)
            P.dma("sp", lambda e, tb=tb: e.dma_start(out=XT[tb].rearrange("p (k t) -> p k t", k=32), in_=xnT), reads=[bxnT], writes=[bXT])

            def ld(ct):
                def f(slot):
                    P.dma("pool", lambda e: e.dma_start(out=wq[slot], in_=wq_v[:, :, ct * 256:(ct + 1) * 256]), writes=[bwq[slot]])
                return f

            def comp(ct, slot):
                for j in range(2):
                    hp = ct * 2 + j
                    pq = cn[0] % 2
                    cn[0] += 1
                    for k in range(32):
                        P.op("pe", lambda e, k=k, j=j, pq=pq: e.matmul(ps[pq][:], wq[slot][:, k, j * 128:(j + 1) * 128], xnT[:, k, :],
                                                                       start=(k == 0), stop=(k == 31)),
                                 sig=(k == 31),
                             reads=[bwq[slot], bxnT], writes=[bps[pq]])
                    if hp % 2 == 0:
                        P.op("act", lambda e, hp=hp, pq=pq: e.copy(out=qT[:, hp, :], in_=ps[pq][:]), reads=[bps[pq]], writes=[bqT])
                    else:
                        P.op("dve", lambda e, hp=hp, pq=pq: e.tensor_copy(out=qT[:, hp, :], in_=ps[pq][:]), reads=[bps[pq]], writes=[bqT])
            wstream([ld(ct) for ct in range(8)], comp)
            for ti in range(4):
                ss = cn[1] % 2
                cn[1] += 1
                for hp4 in range(4):
                    pi = 2 + hp4 % 2
                    for j in range(4):
                        hp = hp4 * 4 + j
                        P.op("pe", lambda e, hp=hp, j=j, ti=ti, pi=pi: e.matmul(ps[pi][:, j * 128:(j + 1) * 128], qT[:, hp, ti * 128:(ti + 1) * 128],
                                                                              skT[:, hp, :], start=True, stop=True),
                             reads=[bqT, bskT], writes=[bps[pi]])
                    P.op("dve", lambda e, hp4=hp4, pi=pi, ss=ss: e.tensor_copy(out=s_t[ss][:, hp4 * 512:(hp4 + 1) * 512], in_=ps[pi][:]),
                         reads=[bps[pi]], writes=[bs_t[ss]])
                r0 = tb * TB + ti * 128
                P.dma("sp", lambda e, r0=r0, ss=ss: e.dma_start(out=SS[r0:r0 + 128, :], in_=s_t[ss]), reads=[bs_t[ss]], writes=[bSS])

        phase_start()
        NEG = -1.0e30
        sb = [arena.f32(2048) for _ in range(2)]; bsb = [Buf("sb0"), Buf("sb1")]
        top = arena.f32(256); btop = Buf("top")
        tmp = arena.f32(128); btmp = Buf("tmp")
        cand = arena.f32(8 * 256); bcand = Buf("cand")
        tmpc = arena.f32(256); btmpc = Buf("tmpc")
        best = arena.f32(8 * 24); bbest = Buf("best")
        sm = arena.f32(64); bsm = Buf("sm")
        ex = arena.f32(128); bex = Buf("ex")
        pen = arena.f32(128); bpen = Buf("pen")
        d1 = arena.f32(8 * 128); bd1 = Buf("d1")
        s2m = arena.f32(8 * 128); bs2m = Buf("s2m")
        Z3 = [arena.f32(2048) for _ in range(2)]; bZ3 = [Buf("Z30"), Buf("Z31")]
        E3 = [arena.f32(2048) for _ in range(2)]; bE3 = [Buf("E30"), Buf("E31")]
        M3 = [arena.bf16(2048) for _ in range(3)]; bM3 = [Buf("M30"), Buf("M31"), Buf("M32")]
        dg = arena.bf16(8 * 128).rearrange("p (h t) -> p h t", h=8); bdg = Buf("dg")
        gout = [arena.bf16(NE) for _ in range(2)]; bgout = [Buf("go0"), Buf("go1")]
        nz = [0]
        for tile in range(nblk * 4):
            ss = tile % 2
            s = sb[ss]
            bs = bsb[ss]
            r0 = tile * 128
            P.dma("sp", lambda e, r0=r0, s=s: e.dma_start(out=s, in_=SS[r0:r0 + 128, :]), reads=[bSS], writes=[bs])
            for hp in range(16):
                sv = s[:, hp * 128:(hp + 1) * 128]
                P.op("dve", lambda e, hp=hp, sv=sv: e.max(out=top[:, hp * 16:hp * 16 + 8], in_=sv), reads=[bs], writes=[btop])
                P.op("dve", lambda e, hp=hp, sv=sv: e.match_replace(out=tmp, in_to_replace=top[:, hp * 16:hp * 16 + 8], in_values=sv, imm_value=NEG),
                     reads=[bs, btop], writes=[btmp])
                P.op("dve", lambda e, hp=hp: e.max(out=top[:, hp * 16 + 8:hp * 16 + 16], in_=tmp), reads=[btmp], writes=[btop])
            cand4 = cand.rearrange("p (h a b) -> p h a b", h=8, a=16)
            P.op("dve", lambda e, cand4=cand4: e.tensor_tensor(out=cand4, in0=bc_ap(top, [[32, 8], [1, 16], [0, 16]]),
                                                              in1=bc_ap(top[:, 16:], [[32, 8], [0, 16], [1, 16]]), op=ALU.add),
                 reads=[btop], writes=[bcand])
            for h in range(8):
                cvw = cand[:, h * 256:(h + 1) * 256]
                P.op("dve", lambda e, h=h, cvw=cvw: e.max(out=best[:, h * 24:h * 24 + 8], in_=cvw), reads=[bcand], writes=[bbest])
                P.op("dve", lambda e, h=h, cvw=cvw: e.match_replace(out=tmpc, in_to_replace=best[:, h * 24:h * 24 + 8], in_values=cvw, imm_value=NEG),
                     reads=[bcand, bbest], writes=[btmpc])
                P.op("dve", lambda e, h=h: e.max(out=best[:, h * 24 + 8:h * 24 + 16], in_=tmpc), reads=[btmpc], writes=[bbest])
                P.op("dve", lambda e, h=h: e.match_replace(out=tmpc, in_to_replace=best[:, h * 24 + 8:h * 24 + 16], in_values=tmpc, imm_value=NEG),
                     reads=[bbest, btmpc], writes=[btmpc])
                P.op("dve", lambda e, h=h: e.max(out=best[:, h * 24 + 16:h * 24 + 24], in_=tmpc), reads=[btmpc], writes=[bbest])
            b3 = best.rearrange("p (h k) -> p h k", h=8)
            negm, tau, tm, Zs, rZ, negtau = [sm[:, i * 8:(i + 1) * 8] for i in range(6)]
            P.op("dve", lambda e: e.tensor_scalar(out=negm, in0=b3[:, :, 0], scalar1=-1.0, scalar2=None, op0=ALU.mult), reads=[bbest], writes=[bsm])
            P.op("dve", lambda e: e.tensor_tensor(out=tau, in0=b3[:, :, 15], in1=b3[:, :, 16], op=ALU.add), reads=[bbest], writes=[bsm])
            P.op("dve", lambda e: e.tensor_scalar(out=tau, in0=tau, scalar1=0.5, scalar2=None, op0=ALU.mult), reads=[bsm], writes=[bsm])
            P.op("dve", lambda e: e.tensor_tensor(out=tm, in0=tau, in1=negm, op=ALU.add), reads=[bsm], writes=[bsm])
            P.op("dve", lambda e: e.tensor_scalar(out=negtau, in0=tau, scalar1=-1.0, scalar2=None, op0=ALU.mult), reads=[bsm], writes=[bsm])
            for h in range(8):
                P.op("act", lambda e, h=h: e.activation(out=ex[:, h * 16:(h + 1) * 16], in_=best[:, h * 24:h * 24 + 16], func=AF.Exp, bias=negm[:, h:h + 1]),
                     reads=[bbest, bsm], writes=[bex])
                P.op("dve", lambda e, h=h: e.reduce_sum(out=Zs[:, h:h + 1], in_=ex[:, h * 16:(h + 1) * 16], axis=AX.X), reads=[bex], writes=[bsm])
            P.op("dve", lambda e: e.reciprocal(out=rZ, in_=Zs), reads=[bsm], writes=[bsm])
            for h in range(8):
                for p_ in range(2):
                    hp = 2 * h + p_
                    sv = s[:, hp * 128:(hp + 1) * 128]
                    thr = top[:, hp * 16 + 15:hp * 16 + 16]
                    P.op("dve", lambda e, sv=sv, thr=thr: e.tensor_scalar(out=pen, in0=sv, scalar1=thr, scalar2=None, op0=ALU.is_ge),
                         reads=[bs, btop], writes=[bpen])
                    P.op("dve", lambda e: e.tensor_scalar(out=pen, in0=pen, scalar1=-1.0, scalar2=1.0e4, op0=ALU.add, op1=ALU.mult),
                         reads=[bpen], writes=[bpen])
                    if p_ == 0:
                        dv = d1[:, h * 128:(h + 1) * 128]
                        P.op("dve", lambda e, sv=sv, h=h, dv=dv: e.scalar_tensor_tensor(out=dv, in0=sv, scalar=negtau[:, h:h + 1], in1=pen,
                                                                                    op0=ALU.add, op1=ALU.add), reads=[bs, bsm, bpen], writes=[bd1])
                    else:
                        dv = s2m[:, h * 128:(h + 1) * 128]
                        P.op("dve", lambda e, sv=sv, dv=dv: e.tensor_tensor(out=dv, in0=sv, in1=pen, op=ALU.add), reads=[bs, bpen], writes=[bs2m])
            go = gout[tile % 2]
            bgo = bgout[tile % 2]
            for h in range(8):
                P.op("dve", lambda e, h=h: e.tensor_scalar(out=dg[:, h, :], in0=ident_f, scalar1=rZ[:, h:h + 1], scalar2=None, op0=ALU.mult),
                     reads=[bsm, bconst], writes=[bdg])
            for cc in range(8):
                pb0 = (cc % 2) * 4
                for h in range(8):
                    zs = nz[0] % 2
                    ms = nz[0] % 3
                    nz[0] += 1
                    z3v = Z3[zs].rearrange("p (c i) -> p c i", c=16)
                    in0 = bc_ap(s2m[:, h * 128:(h + 1) * 128], [[0, 16], [1, 128]])
                    in1 = bc_ap(d1[:, h * 128 + cc * 16:h * 128 + cc * 16 + 16], [[1, 16], [0, 128]])
                    zeng = "pool" if nz[0] % 4 != 0 else "dve"
                    P.op(zeng, lambda e, z3v=z3v, in0=in0, in1=in1: e.tensor_tensor(out=z3v, in0=in0, in1=in1, op=ALU.add),
                         reads=[bs2m, bd1], writes=[bZ3[zs]])
                    P.op("act", lambda e, zs=zs, h=h: e.activation(out=E3[zs], in_=Z3[zs], func=AF.Exp, bias=tm[:, h:h + 1]),
                         reads=[bZ3[zs], bsm], writes=[bE3[zs]])
                    P.op("dve", lambda e, zs=zs, ms=ms: e.scalar_tensor_tensor(out=M3[ms], in0=Z3[zs], scalar=0.0, in1=E3[zs], op0=ALU.is_ge, op1=ALU.mult),
                         reads=[bZ3[zs], bE3[zs]], writes=[bM3[ms]])
                    for b_ in range(4):
                        P.op("pe", lambda e, b_=b_, h=h, ms=ms, pb0=pb0: e.matmul(ps[pb0 + b_][:], dg[:, h, :], M3[ms][:, b_ * 512:(b_ + 1) * 512],
                                                                                 start=(h == 0), stop=(h == 7)),
                             reads=[bdg, bM3[ms]], writes=[bps[pb0 + b_]], sig=(b_ == 3))
                for b_ in range(4):
                    P.op("act", lambda e, go=go, cc=cc, b_=b_, pb0=pb0: e.copy(out=go[:, cc * 2048 + b_ * 512:cc * 2048 + (b_ + 1) * 512], in_=ps[pb0 + b_][:]),
                         reads=[bps[pb0 + b_]], writes=[bgo])
            P.dma("sp", lambda e, r0=r0, go=go: e.dma_start(out=GS[r0:r0 + 128, :], in_=go), reads=[bgo], writes=[bGS])

        phase_start()
        xnT2 = arena.bf16(32 * TB).rearrange("p (k t) -> p k t", k=32); bxn2 = Buf("xnT2")
        acc = arena.f32(4 * D).rearrange("p (i d) -> p i d", i=4); bacc = [[Buf(f"acc{i}_{c}") for c in range(8)] for i in range(4)]
        tmpa = [arena.f32(512) for _ in range(2)]; btmpa = [Buf("tmpa0"), Buf("tmpa1")]
        ut = [arena.bf16(32 * 256).rearrange("p (k e) -> p k e", k=32) for _ in range(2)]; but = [Buf("ut0"), Buf("ut1")]
        vt = [arena.bf16(2 * D).rearrange("p (c d) -> p c d", c=2) for _ in range(2)]; bvt = [Buf("vt0"), Buf("vt1")]
        gt = [arena.bf16(4 * 256).rearrange("p (i e) -> p i e", i=4) for _ in range(2)]; bgt = [Buf("gt0"), Buf("gt1")]
        ga = [arena.f32(256) for _ in range(2)]; bga = [Buf("ga0"), Buf("ga1")]
        wv = [arena.bf16(256) for _ in range(2)]; bwv = [Buf("w0"), Buf("w1")]
        wT = [arena.bf16(256).rearrange("p (c t) -> p c t", c=2) for _ in range(2)]; bwT = [Buf("wT0"), Buf("wT1")]
        hres = arena.f32(D); bhres = Buf("hres")
        UTv = UT.rearrange("b p (k e) -> b p k e", k=32)
        GSv = GS.rearrange("(i p) e -> p i e", p=128)
        n2 = [0, 0, 0]
        for tb in range(nblk):
            P.dma("sp", lambda e, tb=tb: e.dma_start(out=xnT2, in_=XT[tb].rearrange("p (k t) -> p k t", k=32)), reads=[bXT], writes=[bxn2])

            def ld(eb, tb=tb):
                def f(slot):
                    P.dma("sp", lambda e: e.dma_start(out=ut[slot], in_=UTv[eb]), reads=[bUT], writes=[but[slot]])
                    P.dma("pool", lambda e: e.dma_start(out=vt[slot], in_=peer_v[L][eb * 256:(eb + 1) * 256, :].rearrange("(c p) d -> p c d", p=128)),
                          writes=[bvt[slot]])
                    P.dma("sp", lambda e: e.dma_start(out=gt[slot], in_=GSv[:, tb * 4:(tb + 1) * 4, eb * 256:(eb + 1) * 256]), reads=[bGS], writes=[bgt[slot]])
                return f

            def stage1(eb, ti, pa):
                slot = eb % 2
                for k in range(32):
                    P.op("pe", lambda e, k=k: e.matmul(ps[pa][:, 0:256], xnT2[:, k, ti * 128:(ti + 1) * 128], ut[slot][:, k, :],
                                                       start=(k == 0), stop=(k == 31)),
                         reads=[bxn2, but[slot]], writes=[bps[pa]], sig=(k == 31))
                P.op("act", lambda e: e.activation(out=ga[pa], in_=ps[pa][:, 0:256], func=AF.Gelu), reads=[bps[pa]], writes=[bga[pa]])
                P.op("dve", lambda e: e.tensor_tensor(out=wv[pa], in0=ga[pa], in1=gt[slot][:, ti, :], op=ALU.mult),
                     reads=[bga[pa], bgt[slot]], writes=[bwv[pa]])

            def stage2a(eb, ti, pa):
                pvw = ps[2][:].bitcast(BF16)
                for c in range(2):
                    P.op("pe", lambda e, c=c: e.transpose(pvw[:, c * 128:(c + 1) * 128], wv[pa][:, c * 128:(c + 1) * 128], ident_b),
                         reads=[bwv[pa], bconst], writes=[bps[2]], sig=(c == 1))
                P.op("act", lambda e: e.copy(out=wT[pa], in_=pvw[:, 0:256].rearrange("p (c t) -> p c t", c=2)),
                     reads=[bps[2]], writes=[bwT[pa]])

            def stage2(eb, ti, pa):
                slot = eb % 2
                for cb in range(8):
                    po = 3 + n2[1] % 5
                    n2[1] += 1
                    for c in range(2):
                        P.op("pe", lambda e, c=c, cb=cb, po=po: e.matmul(ps[po][:], wT[pa][:, c, :], vt[slot][:, c, cb * 512:(cb + 1) * 512],
                                                                        start=(c == 0), stop=(c == 1)),
                             reads=[bwT[pa], bvt[slot]], writes=[bps[po]], sig=(c == 1))
                    av = acc[:, ti, cb * 512:(cb + 1) * 512]
                    if eb == 0:
                        eng0 = "act" if cb % 3 == 1 else "dve"
                        if eng0 == "act":
                            P.op("act", lambda e, av=av, po=po: e.copy(out=av, in_=ps[po][:]), reads=[bps[po]], writes=[bacc[ti][cb]])
                        else:
                            P.op("dve", lambda e, av=av, po=po: e.tensor_copy(out=av, in_=ps[po][:]), reads=[bps[po]], writes=[bacc[ti][cb]])
                    elif cb % 3 == 1:
                        ts_ = n2[2] % 2
                        n2[2] += 1
                        P.op("act", lambda e, po=po, ts_=ts_: e.copy(out=tmpa[ts_], in_=ps[po][:]), reads=[bps[po]], writes=[btmpa[ts_]])
                        P.op("pool", lambda e, av=av, ts_=ts_: e.tensor_tensor(out=av, in0=av, in1=tmpa[ts_], op=ALU.add),
                             reads=[btmpa[ts_]], writes=[bacc[ti][cb]])
                    else:
                        P.op("dve", lambda e, av=av, po=po: e.tensor_tensor(out=av, in0=av, in1=ps[po][:], op=ALU.add), reads=[bps[po]], writes=[bacc[ti][cb]])

            pairs = [(eb, ti) for eb in range(64) for ti in range(4)]
            ld(0)(0)
            stage1(0, 0, 0)
            for idx, (eb, ti) in enumerate(pairs):
                if ti == 0 and eb + 1 < 64:
                    ld(eb + 1)((eb + 1) % 2)
                stage2a(eb, ti, idx % 2)
                if idx + 1 < len(pairs):
                    stage1(pairs[idx + 1][0], pairs[idx + 1][1], (idx + 1) % 2)
                stage2(eb, ti, idx % 2)
            for ti in range(4):
                r0 = tb * TB + ti * 128
                P.dma("sp", lambda e, r0=r0: e.dma_start(out=hres, in_=hin[r0:r0 + 128, :]), writes=[bhres])
                P.op("dve", lambda e, ti=ti: e.tensor_tensor(out=hres, in0=hres, in1=acc[:, ti, :], op=ALU.add), reads=bacc[ti], writes=[bhres])
                P.dma("sp", lambda e, r0=r0: e.dma_start(out=hout[r0:r0 + 128, :], in_=hres), reads=[bhres], writes=[Buf()])

    cur = x
    if "attn" in stages:
        cur = h1
    if "peer0" in stages:
        peer_layer(0, cur, h2, NBQ)
        cur = h2
    if "conv" in stages:
        conv_layer(cur, h3)
        cur = h3
    if "peer1" in stages:
        peer_layer(1, cur, h4, NBO)
        cur = h4

    P.barrier()
    P.emit(nc, st)
    st.close()
    return nc


def host_consts():
    ident = np.eye(128, dtype=np.float32)
    ones = np.ones((128, 128), np.float32)
    rot = np.zeros((128, 128), np.float32)
    for m in range(128):
        half = (m % 64) // 32
        if half == 0:
            rot[m + 32, m] = -1.0
        else:
            rot[m - 32, m] = 1.0
    consts = np.concatenate([ident, ones, rot, np.zeros((128, 256), np.float32)], axis=1)
    pos = np.arange(T)
    row = (pos // 64).astype(np.float32)
    col = (pos % 64).astype(np.float32)
    inv_freq = (10000.0 ** (-np.arange(0, 64, 2, dtype=np.float32) / 64)).astype(np.float32)
    ang = np.zeros((128, T), np.float32)
    for d in range(128):
        axis = d // 64
        f = d % 32
        ang[d] = (row if axis == 0 else col) * inv_freq[f]
    rope = np.stack([np.cos(ang), np.sin(ang)]).astype(np.float32)
    return consts, rope


def make_in_maps(inputs, ncores, stages):
    consts, rope = host_consts()
    maps = []
    gains = np.stack([inputs["attn_q_gain"][0], inputs["attn_k_gain"][0]], axis=1).astype(np.float32)
    for c in range(ncores):
        b, half = (c // 2, c % 2) if ncores == 8 else (c, 0)
        xs = inputs["x"][b]
        rp = rope
        if half == 1:
            xs = xs[::-1]
            rp = rope[:, :, ::-1]
        m = {"x": np.ascontiguousarray(xs), "mixer_norm_g": inputs["mixer_norm_g"],
             "ffn_norm_g": inputs["ffn_norm_g"], "consts": consts, "rope_cs": np.ascontiguousarray(rp), "qk_gain": gains}
        if "attn" in stages:
            m["attn_w_qkv"] = inputs["attn_w_qkv"][0]
            m["attn_w_o"] = inputs["attn_w_o"][0]
        if "conv" in stages:
            m["conv_w_in"] = inputs["conv_w_in"][0]
            m["conv_w_out"] = inputs["conv_w_out"][0]
            cw = inputs["conv_w"][0]
            if half == 1:
                cw = cw[::-1]
            wb = np.concatenate([cw, inputs["conv_b"]], axis=0)
            m["conv_wb"] = np.ascontiguousarray(wb.T.reshape(32, 128, 4))
        if "peer0" in stages or "peer1" in stages:
            m["peer_w_query"] = inputs["peer_w_query"]
            m["peer_sub_keys"] = np.ascontiguousarray(inputs["peer_sub_keys"].reshape(2, 16, 128, 128))
            m["peer_u"] = inputs["peer_u"]
            m["peer_v"] = inputs["peer_v"]
        maps.append(m)
    return maps


def kernel(**inputs):
    inputs = {k: np.asarray(v) for k, v in inputs.items()}
    stages = ("attn", "peer0", "conv", "peer1")
    nc = build(stages)
    maps = make_in_maps(inputs, NCORES, stages)
    res = run_bass_kernel_spmd(nc, maps, core_ids=list(range(NCORES)))
    out = np.empty((4, T, D), np.float32)
    for c in range(NCORES):
        b, half = c // 2, c % 2
        y = res.results[c]["y"]
        if half == 0:
            out[b, :TOWN] = y
        else:
            out[b, TOWN:] = y[::-1]
    return out
```

```python
from contextlib import ExitStack
import numpy as np
import concourse.bass as bass
import concourse.mybir as mybir
from concourse.bass_utils import run_bass_kernel_spmd

F32 = mybir.dt.float32
BF16 = mybir.dt.bfloat16
ALU = mybir.AluOpType
AF = mybir.ActivationFunctionType
AX = mybir.AxisListType

ENGS = ["pe", "act", "dve", "pool", "sp"]
DMA_RING = 8
SAME_ENGINE_SYNC = True

D = 4096
T = 4096
NT = T // 128
TB = 512
NB = T // TB
NE = 16384
EPS = 1e-6
NCORES = 8
NBQ = 5
NBO = 4
TL = NBQ * TB
TOWN = NBO * TB


class Buf:
    __slots__ = ("name", "last_w", "readers")

    def __init__(self, name=""):
        self.name = name
        self.last_w = None
        self.readers = {}


class Prog:
    def __init__(self):
        self.ops = {e: [] for e in ENGS}
        self.cnt = {e: 0 for e in ENGS}
        self.seen = {e: {} for e in ENGS}
        self.dma_n = {e: 0 for e in ENGS}
        self.last = {}
        self.semkeys = set()
        self.pending = {}

    def _deps(self, reads, writes):
        deps = {}

        def add(k, v):
            if deps.get(k, 0) < v:
                deps[k] = v
        for b in reads:
            if b.last_w is not None:
                add(*b.last_w)
        for b in writes:
            if b.last_w is not None:
                add(*b.last_w)
            for k, v in b.readers.items():
                add(k, v)
        return deps

    def _commit(self, ev, reads, writes):
        k, v = ev
        self.last[k] = v
        for b in reads:
            if b.readers.get(k, 0) < v:
                b.readers[k] = v
        for b in writes:
            b.last_w = ev
            b.readers = {}

    def _waits(self, eng, deps):
        waits = []
        seen = self.seen[eng]
        for k, v in deps.items():
            if k == eng and (eng == "pe" or not SAME_ENGINE_SYNC):
                continue
            if seen.get(k, 0) >= v:
                continue
            seen[k] = v
            waits.append((k, v))
        return waits

    def op(self, eng, fn, reads=(), writes=(), sig=True):
        deps = self._deps(reads, writes)
        waits = self._waits(eng, deps)
        if not sig:
            self.ops[eng].append((waits, fn, None, 0))
            self.pending.setdefault(eng, []).append((tuple(reads), tuple(writes)))
            return
        self.cnt[eng] += 1
        ev = (eng, self.cnt[eng])
        self.semkeys.add(eng)
        self.ops[eng].append((waits, fn, eng, 1))
        for r_, w_ in self.pending.pop(eng, []):
            self._commit(ev, r_, w_)
        self._commit(ev, reads, writes)

    def dma(self, q, fn, reads=(), writes=()):
        deps = self._deps(reads, writes)
        n = self.dma_n[q]
        self.dma_n[q] += 1
        slot, rnd = n % DMA_RING, n // DMA_RING
        key = ("dma", q, slot)
        self.semkeys.add(key)
        if rnd > 0 and deps.get(key, 0) < 16 * rnd:
            deps[key] = 16 * rnd
        waits = self._waits(q, deps)
        ev = (key, 16 * (rnd + 1))
        self.ops[q].append((waits, fn, key, 16))
        self._commit(ev, reads, writes)

    def barrier(self):
        assert not any(self.pending.values()), "silent op without a following signalling op"
        for e in ENGS:
            waits = self._waits(e, dict(self.last))
            if waits:
                self.ops[e].append((waits, None, None, 0))

    def emit(self, nc, stack):
        sems = {}
        for k in sorted(self.semkeys, key=str):
            nm = "s_" + (k if isinstance(k, str) else "_".join(str(x) for x in k))
            sems[k] = stack.enter_context(nc.semaphore(nm))
        block = stack.enter_context(nc.Block())

        def run(name):
            def body(eng):
                for waits, fn, key, inc in self.ops[name]:
                    for k, v in waits:
                        eng.wait_ge(sems[k], v)
                    if fn is not None:
                        ins = fn(eng)
                        if inc:
                            ins.then_inc(sems[key], inc)
            return body
        m = {"pe": block.tensor, "act": block.scalar, "dve": block.vector,
             "pool": block.gpsimd, "sp": block.sync}
        for e in ENGS:
            if self.ops[e]:
                m[e](run(e))


class Arena:
    def __init__(self, nc, st, nbytes):
        self.cap = nbytes
        self.t = st.enter_context(nc.sbuf_tensor("arena", [128, nbytes // 4], F32))
        self.off = 0

    def reset(self):
        self.off = 0

    def _take(self, nbytes):
        nbytes = (nbytes + 63) // 64 * 64
        o = self.off
        self.off += nbytes
        assert self.off <= self.cap, (self.off, self.cap)
        return o

    def f32(self, n):
        o = self._take(n * 4)
        return self.t[:, o // 4:o // 4 + n]

    def bf16(self, n):
        assert n % 2 == 0
        o = self._take(n * 2)
        return self.t[:, o // 4:o // 4 + n // 2].bitcast(BF16)


def bc_ap(ap, dims):
    return bass.AP(tensor=ap.tensor, offset=ap.offset, ap=[list(ap.ap[0])] + [list(d) for d in dims])


def row_bcast(dram_ap_1d, n):
    return bass.AP(tensor=dram_ap_1d.tensor, offset=dram_ap_1d.offset, ap=[[0, 128], [1, n]])


def build(stages=("attn", "peer0", "conv", "peer1"), out_stage=None, dbg=False):
    nc = bass.Bass("TRN2", target_bir_lowering=False)
    P = Prog()
    st = ExitStack()

    def din(name, shape, dt=F32):
        return nc.dram_tensor(name, list(shape), dt, kind="ExternalInput").ap()

    out_stage = out_stage or stages[-1]

    def dscr(name, shape, dt=F32, stage=None):
        kind = "ExternalOutput" if (stage is not None and stage == out_stage) else "Internal"
        nm = "y" if kind == "ExternalOutput" else name
        return nc.dram_tensor(nm, list(shape), dt, kind=kind).ap()

    x = din("x", [T, D])
    mixer_g = din("mixer_norm_g", [2, D])
    ffn_g = din("ffn_norm_g", [2, D])
    consts = din("consts", [128, 5 * 128])
    rope = din("rope_cs", [2, 128, T])
    gains = din("qk_gain", [128, 2])
    if "attn" in stages:
        w_qkv = din("attn_w_qkv", [D, 6144])
        w_o = din("attn_w_o", [D, D])
    if "conv" in stages:
        w_in = din("conv_w_in", [D, 3 * D])
        conv_wb = din("conv_wb", [32, 128, 4])
        w_out = din("conv_w_out", [D, D])
    if "peer0" in stages or "peer1" in stages:
        w_query = din("peer_w_query", [2, D, 2048])
        sub_keys = din("peer_sub_keys", [2, 16, 128, 128])
        peer_u = din("peer_u", [2, NE, D])
        peer_v = din("peer_v", [2, NE, D])

    h1 = dscr("h1", [T, D], stage="attn")
    h2 = dscr("h2", [T, D], stage="peer0")
    h3 = dscr("h3", [TOWN, D], stage="conv")
    h4 = dscr("h4", [TOWN, D], stage="peer1")

    arena = Arena(nc, st, 206 * 1024)
    ps = [st.enter_context(nc.psum_tensor(f"ps{i}", [128, 512], F32)) for i in range(8)]
    bps = [Buf(f"ps{i}") for i in range(8)]

    c_f32 = arena.f32(5 * 128)
    ident_f = c_f32[:, 0:128]
    rot_f = c_f32[:, 256:384]
    ident_b = arena.bf16(128)
    ones_b = arena.bf16(128)
    gain_t = arena.f32(2)
    bconst = Buf("const")
    P.dma("sp", lambda e: e.dma_start(out=c_f32, in_=consts), writes=[bconst])
    P.dma("sp", lambda e: e.dma_start(out=gain_t, in_=gains), writes=[bconst])
    P.op("dve", lambda e: e.tensor_copy(out=ident_b, in_=c_f32[:, 0:128]), reads=[bconst], writes=[bconst])
    P.op("dve", lambda e: e.tensor_copy(out=ones_b, in_=c_f32[:, 128:256]), reads=[bconst], writes=[bconst])
    base_off = arena.off

    dumps = {}

    def dump(name, ap, buf, dt):
        if not dbg or name in dumps:
            return
        shp = list(ap.shape)
        d_ = nc.dram_tensor("dbg_" + name, shp, dt, kind="ExternalOutput").ap()
        dumps[name] = d_
        P.dma("sp", lambda e: e.dma_start(out=d_, in_=ap), reads=[buf], writes=[Buf()])

    def phase_start():
        P.barrier()
        arena.off = base_off

    def norm_block(src, g_b, bg, tb, xnT, bxnT, xt_bufs, keep_x=None):
        for ti in range(4):
            row0 = tb * TB + ti * 128
            if keep_x is not None:
                xt, bxt = keep_x[ti]
            else:
                xt, bxt = xt_bufs[ti % len(xt_bufs)]
            P.dma("sp", lambda e, xt=xt, row0=row0: e.dma_start(out=xt, in_=src[row0:row0 + 128, :]), writes=[bxt])
            junk, bjunk = norm_block.junk
            ssq, bssq = norm_block.ssq
            P.op("act", lambda e, xt=xt: e.activation(out=junk, in_=xt, func=AF.Square),
                 reads=[bxt], writes=[bjunk])
            P.op("dve", lambda e: e.reduce_sum(out=ssq[:, 0:1], in_=junk, axis=AX.X), reads=[bjunk], writes=[bssq])
            P.op("dve", lambda e: e.tensor_scalar(out=ssq[:, 1:2], in0=ssq[:, 0:1], scalar1=1.0 / D, scalar2=EPS,
                                                  op0=ALU.mult, op1=ALU.add), reads=[bssq], writes=[bssq])
            P.op("act", lambda e: e.activation(out=ssq[:, 3:4], in_=ssq[:, 1:2], func=AF.Sqrt), reads=[bssq], writes=[bssq])
            P.op("dve", lambda e: e.reciprocal(out=ssq[:, 2:3], in_=ssq[:, 3:4]), reads=[bssq], writes=[bssq])
            xn, bxn = norm_block.xn
            P.op("dve", lambda e, xt=xt: e.scalar_tensor_tensor(out=xn, in0=xt, scalar=ssq[:, 2:3], in1=g_b,
                                                                op0=ALU.mult, op1=ALU.mult),
                 reads=[bxt, bssq, bg], writes=[bxn])
            dump("xt", xt, bxt, F32); dump("ssq", ssq, bssq, F32); dump("xn", xn, bxn, BF16)
            for k8 in range(4):
                pi = 6 + (k8 % 2)
                pv = ps[pi][:].bitcast(BF16)
                for kk in range(8):
                    k = k8 * 8 + kk
                    P.op("pe", lambda e, k=k, kk=kk, pv=pv: e.transpose(pv[:, kk * 128:(kk + 1) * 128],
                                                                      xn[:, k * 128:(k + 1) * 128], ident_b),
                         reads=[bxn, bconst], writes=[bps[pi]], sig=(kk == 7))
                eng = "act" if k8 % 2 == 0 else "dve"
                dst = xnT[:, k8 * 8:(k8 + 1) * 8, ti * 128:(ti + 1) * 128]
                srcv = pv.rearrange("p (k t) -> p k t", k=8)
                if eng == "act":
                    P.op("act", lambda e, dst=dst, srcv=srcv: e.copy(out=dst, in_=srcv), reads=[bps[pi]], writes=[bxnT])
                else:
                    P.op("dve", lambda e, dst=dst, srcv=srcv: e.tensor_copy(out=dst, in_=srcv), reads=[bps[pi]], writes=[bxnT])

    def norm_setup():
        norm_block.junk = (arena.bf16(D), Buf("junk"))
        norm_block.ssq = (arena.f32(4), Buf("ssq"))
        norm_block.xn = (arena.bf16(D), Buf("xn"))

    def load_gain_row(g_dram_row):
        g_b = arena.f32(D)
        bg = Buf("g")
        P.dma("sp", lambda e: e.dma_start(out=g_b, in_=row_bcast(g_dram_row, D)), writes=[bg])
        return g_b, bg

    def wstream(loads, compute):
        n = len(loads)
        loads[0](0)
        for i in range(n):
            if i + 1 < n:
                loads[i + 1]((i + 1) % 2)
            compute(i, i % 2)

    def proj_residual(srcT, bsrcT, w_dram, res_src, dst, nblk):
        phase_start()
        ot = arena.bf16(32 * TB).rearrange("p (h t) -> p h t", h=32); bot = Buf("ot")
        wC = [arena.bf16(32 * 512).rearrange("p (k c) -> p k c", k=32) for _ in range(2)]
        bwC = [Buf("wo0"), Buf("wo1")]
        xts = [(arena.f32(D), Buf(f"xr{i}")) for i in range(4)]
        wo_v = w_dram.rearrange("(h p) c -> p h c", p=128)
        cnt = [0]
        for tb in range(nblk):
            nti = 1 if (tb >= NBO and not dbg) else 4
            P.dma("sp", lambda e, tb=tb: e.dma_start(out=ot, in_=srcT[:, :, tb * TB:(tb + 1) * TB].rearrange("h p t -> p h t")),
                  reads=[bsrcT], writes=[bot])
            for ti in range(nti):
                r0 = tb * TB + ti * 128
                P.dma("sp", lambda e, ti=ti, r0=r0: e.dma_start(out=xts[ti][0], in_=res_src[r0:r0 + 128, :]), writes=[xts[ti][1]])

            def ld(cb):
                def f(slot):
                    P.dma("pool", lambda e: e.dma_start(out=wC[slot], in_=wo_v[:, :, cb * 512:(cb + 1) * 512]), writes=[bwC[slot]])
                return f

            def comp(cb, slot, nti=nti):
                for ti in range(nti):
                    pq = cnt[0] % 4
                    cnt[0] += 1
                    for h in range(32):
                        P.op("pe", lambda e, h=h, ti=ti, pq=pq: e.matmul(ps[pq][:], ot[:, h, ti * 128:(ti + 1) * 128], wC[slot][:, h, :],
                                                                         start=(h == 0), stop=(h == 31)),
                                 sig=(h == 31),
                             reads=[bot, bwC[slot]], writes=[bps[pq]])
                    xs = xts[ti][0][:, cb * 512:(cb + 1) * 512]
                    P.op("dve", lambda e, xs=xs, pq=pq: e.tensor_tensor(out=xs, in0=xs, in1=ps[pq][:], op=ALU.add),
                         reads=[bps[pq]], writes=[xts[ti][1]])
            wstream([ld(cb) for cb in range(8)], comp)
            for ti in range(nti):
                r0 = tb * TB + ti * 128
                P.dma("sp", lambda e, ti=ti, r0=r0: e.dma_start(out=dst[r0:r0 + 128, :], in_=xts[ti][0]), reads=[xts[ti][1]], writes=[Buf()])

    if "attn" in stages:
        QT = nc.dram_tensor("QT", [32, 128, T], BF16, kind=("ExternalOutput" if dbg else "Internal")).ap()
        KT = nc.dram_tensor("KT", [8, 128, T], BF16, kind=("ExternalOutput" if dbg else "Internal")).ap()
        VS = nc.dram_tensor("VS", [T, 1024], BF16, kind=("ExternalOutput" if dbg else "Internal")).ap()
        OT = nc.dram_tensor("OT", [32, 128, T], BF16, kind=("ExternalOutput" if dbg else "Internal")).ap()
        bQT, bKT, bVS, bOT = Buf("QT"), Buf("KT"), Buf("VS"), Buf("OT")

        phase_start()
        g_b, bg = load_gain_row(mixer_g[0])
        norm_setup()
        cs = arena.f32(2 * T).rearrange("p (c t) -> p c t", c=2)
        bcs = Buf("cs")
        P.dma("sp", lambda e: e.dma_start(out=cs, in_=rope.rearrange("c p t -> p c t")), writes=[bcs])
        xnT = arena.bf16(32 * TB).rearrange("p (k t) -> p k t", k=32)
        bxnT = Buf("xnT")
        xt_bufs = [(arena.f32(D), Buf("xt0"))]
        wt = [arena.bf16(32 * 512).rearrange("p (k c) -> p k c", k=32) for _ in range(2)]
        bwt = [Buf("wt0"), Buf("wt1")]
        sq = arena.bf16(512); bsq = Buf("sq")
        rstd = arena.f32(512); brstd = Buf("rstd")
        qn = arena.f32(512); bqn = Buf("qn")
        t1 = arena.f32(512); bt1 = Buf("t1")
        t2 = arena.f32(512); bt2 = Buf("t2")
        qo = [arena.bf16(512) for _ in range(2)]; bqo = [Buf("qo0"), Buf("qo1")]
        vo = [arena.bf16(512) for _ in range(2)]; bvo = [Buf("vo0"), Buf("vo1")]
        wq_v = w_qkv.rearrange("(k p) c -> p k c", p=128)
        cnt = {"q": 0, "v": 0}
        for tb in range(1 if dbg == 2 else NB):
            norm_block(x, g_b, bg, tb, xnT, bxnT, xt_bufs)

            dump("xnT", xnT, bxnT, BF16)

            def ld(cb):
                def f(slot):
                    P.dma("pool", lambda e: e.dma_start(out=wt[slot], in_=wq_v[:, :, cb * 512:(cb + 1) * 512]),
                          writes=[bwt[slot]])
                    dump("wt", wt[slot], bwt[slot], BF16)
                return f

            def comp(cb, slot, tb=tb):
                if cb < 10:
                    def mm_(j):
                        hb = cb * 4 + j
                        pq = cnt["q"] % 2
                        cnt["q"] += 1
                        for k in range(32):
                            P.op("pe", lambda e, k=k, j=j, pq=pq: e.matmul(ps[pq][:], wt[slot][:, k, j * 128:(j + 1) * 128],
                                                                           xnT[:, k, :], start=(k == 0), stop=(k == 31)),
                                 sig=(k == 31),
                                 reads=[bwt[slot], bxnT], writes=[bps[pq]])
                        return hb, pq

                    def post_(j, hb, pq):
                        gcol = gain_t[:, 0:1] if hb < 32 else gain_t[:, 1:2]
                        P.op("act", lambda e, pq=pq: e.activation(out=sq, in_=ps[pq][:], func=AF.Square),
                             reads=[bps[pq]], writes=[bsq])
                        P.op("pe", lambda e: e.matmul(ps[2][:], ones_b, sq, start=True, stop=True),
                             reads=[bsq, bconst], writes=[bps[2]])
                        P.op("dve", lambda e: e.tensor_scalar(out=rstd, in0=ps[2][:], scalar1=1.0 / 128, scalar2=EPS,
                                                              op0=ALU.mult, op1=ALU.add), reads=[bps[2]], writes=[brstd])
                        P.op("act", lambda e: e.activation(out=rstd, in_=rstd, func=AF.Sqrt), reads=[brstd], writes=[brstd])
                        P.op("dve", lambda e: e.reciprocal(out=rstd, in_=rstd), reads=[brstd], writes=[brstd])
                        P.op("dve", lambda e, pq=pq, gcol=gcol: e.scalar_tensor_tensor(out=qn, in0=ps[pq][:], scalar=gcol, in1=rstd,
                                                                                       op0=ALU.mult, op1=ALU.mult),
                             reads=[bps[pq], brstd, bconst], writes=[bqn])
                        dump("rstd", rstd, brstd, F32); dump("qn", qn, bqn, F32); dump("sq", sq, bsq, BF16)
                        P.op("pe", lambda e: e.matmul(ps[3][:], rot_f, qn, start=True, stop=True),
                             reads=[bqn, bconst], writes=[bps[3]])
                        tsl = slice(tb * TB, (tb + 1) * TB)
                        P.op("pool", lambda e, tsl=tsl: e.tensor_tensor(out=t1, in0=qn, in1=cs[:, 0, tsl], op=ALU.mult),
                             reads=[bqn, bcs], writes=[bt1])
                        P.op("dve", lambda e, tsl=tsl: e.tensor_tensor(out=t2, in0=ps[3][:], in1=cs[:, 1, tsl], op=ALU.mult),
                             reads=[bps[3], bcs], writes=[bt2])
                        qs = hb % 2
                        P.op("dve", lambda e, qs=qs: e.tensor_tensor(out=qo[qs], in0=t1, in1=t2, op=ALU.add),
                             reads=[bt1, bt2], writes=[bqo[qs]])
                        dump("t1", t1, bt1, F32); dump("t2", t2, bt2, F32); dump("qo", qo[qs], bqo[qs], BF16)
                        if hb < 32:
                            dst, bd = QT[hb][:, tb * TB:(tb + 1) * TB], bQT
                        else:
                            dst, bd = KT[hb - 32][:, tb * TB:(tb + 1) * TB], bKT
                        P.dma("sp", lambda e, qs=qs, dst=dst: e.dma_start(out=dst, in_=qo[qs]), reads=[bqo[qs]], writes=[bd])
                    nxt = mm_(0)
                    for j in range(4):
                        cur_ = nxt
                        if j + 1 < 4:
                            nxt = mm_(j + 1)
                        post_(j, *cur_)
                else:
                    for ti in range(4):
                        pq = 4 + cnt["v"] % 2
                        vs = cnt["v"] % 2
                        cnt["v"] += 1
                        for k in range(32):
                            P.op("pe", lambda e, k=k, ti=ti, pq=pq: e.matmul(ps[pq][:], xnT[:, k, ti * 128:(ti + 1) * 128],
                                                                             wt[slot][:, k, :], start=(k == 0), stop=(k == 31)),
                                 sig=(k == 31),
                                 reads=[bwt[slot], bxnT], writes=[bps[pq]])
                        P.op("act", lambda e, pq=pq, vs=vs: e.copy(out=vo[vs], in_=ps[pq][:]), reads=[bps[pq]], writes=[bvo[vs]])
                        dump("vo", vo[vs], bvo[vs], BF16)
                        r0 = tb * TB + ti * 128
                        c0 = (cb - 10) * 512
                        P.dma("sp", lambda e, vs=vs, r0=r0, c0=c0: e.dma_start(out=VS[r0:r0 + 128, c0:c0 + 512], in_=vo[vs]),
                              reads=[bvo[vs]], writes=[bVS])
            cbs = list(range(12)) if (tb < NBQ or dbg) else list(range(8, 12))
            wstream([ld(cb) for cb in cbs], lambda i, slot, cbs=cbs: comp(cbs[i], slot))

        if dbg in (2, 3):
            P.barrier(); P.emit(nc, st); st.close(); return nc
        phase_start()
        kt = arena.bf16(T); bkt = Buf("kt")
        vt = arena.bf16(32 * 130).rearrange("p (c d) -> p c d", c=32); bvt = Buf("vt")
        qt = arena.bf16(4 * T).rearrange("p (h t) -> p h t", h=4); bqt = Buf("qt")
        pT = [arena.bf16(512) for _ in range(3)]; bpT = [Buf(f"pT{i}") for i in range(3)]
        rden = arena.f32(4); brden = Buf("rden")
        on = arena.bf16(128 * 4).rearrange("p (h d) -> p h d", h=4); bon = Buf("on")
        oT = [arena.bf16(4 * 512).rearrange("p (h t) -> p h t", h=4) for _ in range(2)]; boT = [Buf("oT0"), Buf("oT1")]
        scale = 128.0 ** -0.5
        for g in range(8):
            P.dma("sp", lambda e, g=g: e.dma_start(out=kt, in_=KT[g]), writes=[bkt], reads=[bKT])
            P.dma("sp", lambda e, g=g: e.dma_start(out=vt[:, :, 0:128],
                                                   in_=VS[:, g * 128:(g + 1) * 128].rearrange("(c p) d -> p c d", p=128)),
                  writes=[bvt], reads=[bVS])
            P.op("pool", lambda e: e.memset(vt[:, :, 128:129], 1.0), writes=[bvt])
            P.dma("sp", lambda e, g=g: e.dma_start(out=qt, in_=QT[4 * g:4 * g + 4].rearrange("h p t -> p h t")),
                  writes=[bqt], reads=[bQT])
            NQ = NT if dbg else NBO * 4 + 1
            seq = [(qi, c) for qi in range(NQ) for c in range(32)]

            def S_(idx):
                qi, c = seq[idx]
                sp_, pslot = idx % 2, idx % 3
                P.op("pe", lambda e: e.matmul(ps[sp_][:], kt[:, c * 128:(c + 1) * 128], qt[:, :, qi * 128:(qi + 1) * 128], start=True, stop=True),
                     reads=[bkt, bqt], writes=[bps[sp_]])
                P.op("act", lambda e: e.activation(out=pT[pslot], in_=ps[sp_][:], func=AF.Exp, scale=scale),
                     reads=[bps[sp_]], writes=[bpT[pslot]])

            def PV_(idx, g=g):
                qi, c = seq[idx]
                pslot = idx % 3
                for j in range(4):
                    P.op("pe", lambda e, j=j: e.matmul(ps[2 + j][:, 0:129], pT[pslot][:, j * 128:(j + 1) * 128], vt[:, c, 0:129],
                                                       start=(c == 0), stop=(c == 31)),
                         reads=[bpT[pslot], bvt], writes=[bps[2 + j]], sig=(j == 3))
                if c != 31:
                    return
                os_ = (qi // 4) % 2
                for j in range(4):
                    P.op("dve", lambda e, j=j: e.reciprocal(out=rden[:, j:j + 1], in_=ps[2 + j][:, 128:129]),
                         reads=[bps[2 + j]], writes=[brden])
                    P.op("dve", lambda e, j=j: e.tensor_scalar(out=on[:, j, :], in0=ps[2 + j][:, 0:128], scalar1=rden[:, j:j + 1],
                                                               scalar2=None, op0=ALU.mult),
                         reads=[bps[2 + j], brden], writes=[bon])
                pb = 6 + qi % 2
                pv = ps[pb][:].bitcast(BF16)
                for j in range(4):
                    P.op("pe", lambda e, j=j: e.transpose(pv[:, j * 128:(j + 1) * 128], on[:, j, :], ident_b),
                         reads=[bon, bconst], writes=[bps[pb]], sig=(j == 3))
                P.op("dve", lambda e: e.tensor_copy(out=oT[os_][:, :, (qi % 4) * 128:(qi % 4 + 1) * 128],
                                                    in_=pv[:, 0:512].rearrange("p (h t) -> p h t", h=4)),
                     reads=[bps[pb]], writes=[boT[os_]])
                if qi % 4 == 3 or qi == NQ - 1:
                    t0 = (qi // 4) * 512
                    w_ = (qi % 4 + 1) * 128
                    P.dma("sp", lambda e: e.dma_start(out=OT[4 * g:4 * g + 4][:, :, t0:t0 + w_].rearrange("h p t -> p h t"), in_=oT[os_][:, :, 0:w_]),
                          reads=[boT[os_]], writes=[bOT])
            S_(0)
            for idx in range(len(seq)):
                if idx + 1 < len(seq):
                    S_(idx + 1)
                PV_(idx)

        if dbg == 4:
            P.barrier(); P.emit(nc, st); st.close(); return nc
        proj_residual(OT, bOT, w_o, x, h1, NBQ)

    def conv_layer(hin, hout):
        UTs = nc.dram_tensor("cUT", [32, 128, T], F32, kind="Internal").ap()
        BTs = nc.dram_tensor("cBT", [32, 128, T], F32, kind="Internal").ap()
        YT = nc.dram_tensor("cYT", [32, 128, T], BF16, kind="Internal").ap()
        bUTs, bBTs, bYT = Buf("cUT"), Buf("cBT"), Buf("cYT")
        phase_start()
        g_b, bg = load_gain_row(mixer_g[1])
        norm_setup()
        xnT = arena.bf16(32 * TB).rearrange("p (k t) -> p k t", k=32); bxnT = Buf("xnT")
        xt_bufs = [(arena.f32(D), Buf("xt0"))]
        wt = [arena.bf16(32 * 512).rearrange("p (k c) -> p k c", k=32) for _ in range(2)]
        bwt = [Buf("wi0"), Buf("wi1")]
        cbuf = [arena.f32(512) for _ in range(4)]; bcbuf = [Buf(f"cb{i}") for i in range(4)]
        obuf = [arena.f32(512) for _ in range(2)]; bobuf = [Buf("ob0"), Buf("ob1")]
        wi_v = w_in.rearrange("(k p) c -> p k c", p=128)
        cnt = [0, 0]
        for tb in range(NBQ):
            norm_block(hin, g_b, bg, tb, xnT, bxnT, xt_bufs)
            items = []
            for cb4 in range(8):
                for kind in ("c", "x", "b"):
                    items.append((cb4, kind))

            def ld(it):
                cb4, kind = it
                col0 = {"b": 0, "c": D, "x": 2 * D}[kind] + cb4 * 512

                def f(slot):
                    P.dma("pool", lambda e: e.dma_start(out=wt[slot], in_=wi_v[:, :, col0:col0 + 512]), writes=[bwt[slot]])
                return f

            def comp(i, slot, tb=tb, items=items):
                cb4, kind = items[i]
                for j in range(4):
                    pq = cnt[0] % 4
                    cnt[0] += 1
                    for k in range(32):
                        P.op("pe", lambda e, k=k, j=j, pq=pq: e.matmul(ps[pq][:], wt[slot][:, k, j * 128:(j + 1) * 128], xnT[:, k, :],
                                                                       start=(k == 0), stop=(k == 31)),
                                 sig=(k == 31),
                             reads=[bwt[slot], bxnT], writes=[bps[pq]])
                    chunk = cb4 * 4 + j
                    if kind == "c":
                        P.op("act", lambda e, j=j, pq=pq: e.copy(out=cbuf[j], in_=ps[pq][:]), reads=[bps[pq]], writes=[bcbuf[j]])
                    else:
                        os_ = cnt[1] % 2
                        cnt[1] += 1
                        if kind == "x":
                            P.op("dve", lambda e, j=j, pq=pq, os_=os_: e.tensor_tensor(out=obuf[os_], in0=cbuf[j], in1=ps[pq][:], op=ALU.mult),
                                 reads=[bps[pq], bcbuf[j]], writes=[bobuf[os_]])
                            dst_, bd = UTs[chunk][:, tb * TB:(tb + 1) * TB], bUTs
                        else:
                            P.op("act", lambda e, pq=pq, os_=os_: e.copy(out=obuf[os_], in_=ps[pq][:]), reads=[bps[pq]], writes=[bobuf[os_]])
                            dst_, bd = BTs[chunk][:, tb * TB:(tb + 1) * TB], bBTs
                        P.dma("sp", lambda e, os_=os_, dst_=dst_: e.dma_start(out=dst_, in_=obuf[os_]), reads=[bobuf[os_]], writes=[bd])
            wstream([ld(it) for it in items], comp)

        phase_start()
        ub = [arena.f32(TL + 2) for _ in range(2)]; bub = [Buf("ub0"), Buf("ub1")]
        bt = [arena.f32(TL) for _ in range(2)]; bbt = [Buf("bt0"), Buf("bt1")]
        cv = arena.f32(TL); bcv = Buf("cv")
        yb = [arena.bf16(TL) for _ in range(2)]; byb = [Buf("yb0"), Buf("yb1")]
        wb = [arena.f32(4) for _ in range(2)]; bwb = [Buf("wb0"), Buf("wb1")]
        for s_ in range(2):
            P.op("pool", lambda e, s_=s_: e.memset(ub[s_][:, 0:1], 0.0), writes=[bub[s_]])
            P.op("pool", lambda e, s_=s_: e.memset(ub[s_][:, TL + 1:TL + 2], 0.0), writes=[bub[s_]])
        for ch in range(32):
            s_ = ch % 2
            P.dma("sp", lambda e, ch=ch, s_=s_: e.dma_start(out=ub[s_][:, 1:TL + 1], in_=UTs[ch][:, 0:TL]), reads=[bUTs], writes=[bub[s_]])
            P.dma("sp", lambda e, ch=ch, s_=s_: e.dma_start(out=bt[s_], in_=BTs[ch][:, 0:TL]), reads=[bBTs], writes=[bbt[s_]])
            P.dma("sp", lambda e, ch=ch, s_=s_: e.dma_start(out=wb[s_], in_=conv_wb[ch]), writes=[bwb[s_]])
            P.op("dve", lambda e, s_=s_: e.tensor_scalar(out=cv, in0=ub[s_][:, 0:TL], scalar1=wb[s_][:, 0:1], scalar2=wb[s_][:, 3:4],
                                                         op0=ALU.mult, op1=ALU.add), reads=[bub[s_], bwb[s_]], writes=[bcv])
            P.op("dve", lambda e, s_=s_: e.scalar_tensor_tensor(out=cv, in0=ub[s_][:, 1:TL + 1], scalar=wb[s_][:, 1:2], in1=cv,
                                                                op0=ALU.mult, op1=ALU.add), reads=[bub[s_], bwb[s_]], writes=[bcv])
            P.op("dve", lambda e, s_=s_: e.scalar_tensor_tensor(out=cv, in0=ub[s_][:, 2:TL + 2], scalar=wb[s_][:, 2:3], in1=cv,
                                                                op0=ALU.mult, op1=ALU.add), reads=[bub[s_], bwb[s_]], writes=[bcv])
            P.op("pool", lambda e, s_=s_: e.tensor_tensor(out=yb[s_], in0=cv, in1=bt[s_], op=ALU.mult), reads=[bcv, bbt[s_]], writes=[byb[s_]])
            P.dma("sp", lambda e, ch=ch, s_=s_: e.dma_start(out=YT[ch][:, 0:TL], in_=yb[s_]), reads=[byb[s_]], writes=[bYT])
        proj_residual(YT, bYT, w_out, hin, hout, NBO)

    def peer_layer(L, hin, hout, nblk):
        UT = nc.dram_tensor(f"pUT{L}", [64, 128, 32 * 256], BF16, kind="Internal").ap()
        XT = nc.dram_tensor(f"pXT{L}", [NB, 128, 32 * TB], BF16, kind="Internal").ap()
        SS = nc.dram_tensor(f"pSS{L}", [T, 2048], F32, kind="Internal").ap()
        GS = nc.dram_tensor(f"pGS{L}", [T, NE], BF16, kind="Internal").ap()
        bUT, bXT, bSS, bGS = Buf("pUT"), Buf("pXT"), Buf("pSS"), Buf("pGS")

        phase_start()
        ub = [arena.bf16(4 * D).rearrange("p (c d) -> p c d", c=4) for _ in range(2)]; bub = [Buf("ub0"), Buf("ub1")]
        utb = [arena.bf16(32 * 512).rearrange("p (k e) -> p k e", k=32) for _ in range(2)]; butb = [Buf("utb0"), Buf("utb1")]
        ncp = [0]

        def ld0(eb2):
            def f(slot):
                P.dma("pool", lambda e: e.dma_start(out=ub[slot], in_=peer_u[L][eb2 * 512:(eb2 + 1) * 512, :].rearrange("(c p) d -> p c d", p=128)),
                      writes=[bub[slot]])
            return f

        def comp0(eb2, slot):
            for c in range(4):
                for k8 in range(4):
                    pi = ncp[0] % 4
                    ncp[0] += 1
                    pv = ps[pi][:].bitcast(BF16)
                    for kk in range(8):
                        k = k8 * 8 + kk
                        P.op("pe", lambda e, k=k, kk=kk, c=c, pv=pv: e.transpose(pv[:, kk * 128:(kk + 1) * 128], ub[slot][:, c, k * 128:(k + 1) * 128], ident_b),
                             reads=[bub[slot], bconst], writes=[bps[pi]], sig=(kk == 7))
                    dst_ = utb[slot][:, k8 * 8:(k8 + 1) * 8, c * 128:(c + 1) * 128]
                    srcv = pv.rearrange("p (k t) -> p k t", k=8)
                    if ncp[0] % 2 == 0:
                        P.op("act", lambda e, dst_=dst_, srcv=srcv: e.copy(out=dst_, in_=srcv), reads=[bps[pi]], writes=[butb[slot]])
                    else:
                        P.op("dve", lambda e, dst_=dst_, srcv=srcv: e.tensor_copy(out=dst_, in_=srcv), reads=[bps[pi]], writes=[butb[slot]])
            for hf in range(2):
                P.dma("sp", lambda e, hf=hf: e.dma_start(out=UT[eb2 * 2 + hf].rearrange("p (k e) -> p k e", k=32),
                                                         in_=utb[slot][:, :, hf * 256:(hf + 1) * 256]), reads=[butb[slot]], writes=[bUT])
        wstream([ld0(eb2) for eb2 in range(32)], comp0)

        phase_start()
        g_b, bg = load_gain_row(ffn_g[L])
        norm_setup()
        xnT = arena.bf16(32 * TB).rearrange("p (k t) -> p k t", k=32); bxnT = Buf("xnT")
        xt_bufs = [(arena.f32(D), Buf("xt0"))]
        wq = [arena.bf16(32 * 256).rearrange("p (k c) -> p k c", k=32) for _ in range(2)]; bwq = [Buf("wq0"), Buf("wq1")]
        qT = arena.f32(16 * TB).rearrange("p (h t) -> p h t", h=16); bqT = Buf("qT")
        sk = arena.f32(16 * 128).rearrange("p (h d) -> p h d", h=16); bsk = Buf("sk")
        skT = arena.f32(16 * 128).rearrange("p (h n) -> p h n", h=16); bskT = Buf("skT")
        s_t = [arena.f32(2048) for _ in range(2)]; bs_t = [Buf("st0"), Buf("st1")]
        P.dma("sp", lambda e: e.dma_start(out=sk, in_=sub_keys[L].rearrange("h n d -> n h d")), writes=[bsk])
        for hp in range(16):
            pi = 4 + (hp // 4) % 2
            P.op("pe", lambda e, hp=hp, pi=pi: e.transpose(ps[pi][:, (hp % 4) * 128:(hp % 4 + 1) * 128], sk[:, hp, :], ident_f),
                 reads=[bsk, bconst], writes=[bps[pi]])
            if hp % 4 == 3:
                P.op("dve", lambda e, hp=hp, pi=pi: e.tensor_copy(out=skT[:, hp - 3:hp + 1, :], in_=ps[pi][:].rearrange("p (h n) -> p h n", h=4)),
                     reads=[bps[pi]], writes=[bskT])
        wq_v = w_query[L].rearrange("(k p) c -> p k c", p=128)
        cn = [0, 0]
        for tb in range(nblk):
            norm_block(hin, g_b, bg, tb, xnT, bxnT, xt_bufs)
            P.dma("sp", lambda e, tb=tb: e.dma_start(out=XT[tb].rearrange("p (k t) -> p k t", k=32), in_=xnT), reads=[bxnT], writes=[bXT])

            def ld(ct):
                def f(slot):
                    P.dma("pool", lambda e: e.dma_start(out=wq[slot], in_=wq_v[:, :, ct * 256:(ct + 1) * 256]), writes=[bwq[slot]])
                return f

            def comp(ct, slot):
                for j in range(2):
                    hp = ct * 2 + j
                    pq = cn[0] % 2
                    cn[0] += 1
                    for k in range(32):
                        P.op("pe", lambda e, k=k, j=j, pq=pq: e.matmul(ps[pq][:], wq[slot][:, k, j * 128:(j + 1) * 128], xnT[:, k, :],
                                                                       start=(k == 0), stop=(k == 31)),
                                 sig=(k == 31),
                             reads=[bwq[slot], bxnT], writes=[bps[pq]])
                    if hp % 2 == 0:
                        P.op("act", lambda e, hp=hp, pq=pq: e.copy(out=qT[:, hp, :], in_=ps[pq][:]), reads=[bps[pq]], writes=[bqT])
                    else:
                        P.op("dve", lambda e, hp=hp, pq=pq: e.tensor_copy(out=qT[:, hp, :], in_=ps[pq][:]), reads=[bps[pq]], writes=[bqT])
            wstream([ld(ct) for ct in range(8)], comp)
            for ti in range(4):
                ss = cn[1] % 2
                cn[1] += 1
                for hp4 in range(4):
                    pi = 2 + hp4 % 2
                    for j in range(4):
                        hp = hp4 * 4 + j
                        P.op("pe", lambda e, hp=hp, j=j, ti=ti, pi=pi: e.matmul(ps[pi][:, j * 128:(j + 1) * 128], qT[:, hp, ti * 128:(ti + 1) * 128],
                                                                              skT[:, hp, :], start=True, stop=True),
                             reads=[bqT, bskT], writes=[bps[pi]])
                    P.op("dve", lambda e, hp4=hp4, pi=pi, ss=ss: e.tensor_copy(out=s_t[ss][:, hp4 * 512:(hp4 + 1) * 512], in_=ps[pi][:]),
                         reads=[bps[pi]], writes=[bs_t[ss]])
                r0 = tb * TB + ti * 128
                P.dma("sp", lambda e, r0=r0, ss=ss: e.dma_start(out=SS[r0:r0 + 128, :], in_=s_t[ss]), reads=[bs_t[ss]], writes=[bSS])

        phase_start()
        NEG = -1.0e30
        sb = [arena.f32(2048) for _ in range(2)]; bsb = [Buf("sb0"), Buf("sb1")]
        top = arena.f32(256); btop = Buf("top")
        tmp = arena.f32(128); btmp = Buf("tmp")
        cand = arena.f32(8 * 256); bcand = Buf("cand")
        tmpc = arena.f32(256); btmpc = Buf("tmpc")
        best = arena.f32(8 * 24); bbest = Buf("best")
        sm = arena.f32(64); bsm = Buf("sm")
        ex = arena.f32(128); bex = Buf("ex")
        pen = arena.f32(128); bpen = Buf("pen")
        d1 = arena.f32(8 * 128); bd1 = Buf("d1")
        s2m = arena.f32(8 * 128); bs2m = Buf("s2m")
        Z3 = [arena.f32(2048) for _ in range(2)]; bZ3 = [Buf("Z30"), Buf("Z31")]
        E3 = [arena.f32(2048) for _ in range(2)]; bE3 = [Buf("E30"), Buf("E31")]
        M3 = [arena.bf16(2048) for _ in range(3)]; bM3 = [Buf("M30"), Buf("M31"), Buf("M32")]
        dg = arena.bf16(8 * 128).rearrange("p (h t) -> p h t", h=8); bdg = Buf("dg")
        gout = [arena.bf16(NE) for _ in range(2)]; bgout = [Buf("go0"), Buf("go1")]
        nz = [0]
        for tile in range(nblk * 4 if (nblk <= NBO or dbg) else NBO * 4 + 1):
            ss = tile % 2
            s = sb[ss]
            bs = bsb[ss]
            r0 = tile * 128
            P.dma("sp", lambda e, r0=r0, s=s: e.dma_start(out=s, in_=SS[r0:r0 + 128, :]), reads=[bSS], writes=[bs])
            for hp in range(16):
                sv = s[:, hp * 128:(hp + 1) * 128]
                P.op("dve", lambda e, hp=hp, sv=sv: e.max(out=top[:, hp * 16:hp * 16 + 8], in_=sv), reads=[bs], writes=[btop])
                P.op("dve", lambda e, hp=hp, sv=sv: e.match_replace(out=tmp, in_to_replace=top[:, hp * 16:hp * 16 + 8], in_values=sv, imm_value=NEG),
                     reads=[bs, btop], writes=[btmp])
                P.op("dve", lambda e, hp=hp: e.max(out=top[:, hp * 16 + 8:hp * 16 + 16], in_=tmp), reads=[btmp], writes=[btop])
            cand4 = cand.rearrange("p (h a b) -> p h a b", h=8, a=16)
            P.op("dve", lambda e, cand4=cand4: e.tensor_tensor(out=cand4, in0=bc_ap(top, [[32, 8], [1, 16], [0, 16]]),
                                                              in1=bc_ap(top[:, 16:], [[32, 8], [0, 16], [1, 16]]), op=ALU.add),
                 reads=[btop], writes=[bcand])
            for h in range(8):
                cvw = cand[:, h * 256:(h + 1) * 256]
                P.op("dve", lambda e, h=h, cvw=cvw: e.max(out=best[:, h * 24:h * 24 + 8], in_=cvw), reads=[bcand], writes=[bbest])
                P.op("dve", lambda e, h=h, cvw=cvw: e.match_replace(out=tmpc, in_to_replace=best[:, h * 24:h * 24 + 8], in_values=cvw, imm_value=NEG),
                     reads=[bcand, bbest], writes=[btmpc])
                P.op("dve", lambda e, h=h: e.max(out=best[:, h * 24 + 8:h * 24 + 16], in_=tmpc), reads=[btmpc], writes=[bbest])
                P.op("dve", lambda e, h=h: e.match_replace(out=tmpc, in_to_replace=best[:, h * 24 + 8:h * 24 + 16], in_values=tmpc, imm_value=NEG),
                     reads=[bbest, btmpc], writes=[btmpc])
                P.op("dve", lambda e, h=h: e.max(out=best[:, h * 24 + 16:h * 24 + 24], in_=tmpc), reads=[btmpc], writes=[bbest])
            b3 = best.rearrange("p (h k) -> p h k", h=8)
            negm, tau, tm, Zs, rZ, negtau = [sm[:, i * 8:(i + 1) * 8] for i in range(6)]
            P.op("dve", lambda e: e.tensor_scalar(out=negm, in0=b3[:, :, 0], scalar1=-1.0, scalar2=None, op0=ALU.mult), reads=[bbest], writes=[bsm])
            P.op("dve", lambda e: e.tensor_tensor(out=tau, in0=b3[:, :, 15], in1=b3[:, :, 16], op=ALU.add), reads=[bbest], writes=[bsm])
            P.op("dve", lambda e: e.tensor_scalar(out=tau, in0=tau, scalar1=0.5, scalar2=None, op0=ALU.mult), reads=[bsm], writes=[bsm])
            P.op("dve", lambda e: e.tensor_tensor(out=tm, in0=tau, in1=negm, op=ALU.add), reads=[bsm], writes=[bsm])
            P.op("dve", lambda e: e.tensor_scalar(out=negtau, in0=tau, scalar1=-1.0, scalar2=None, op0=ALU.mult), reads=[bsm], writes=[bsm])
            for h in range(8):
                P.op("act", lambda e, h=h: e.activation(out=ex[:, h * 16:(h + 1) * 16], in_=best[:, h * 24:h * 24 + 16], func=AF.Exp, bias=negm[:, h:h + 1]),
                     reads=[bbest, bsm], writes=[bex])
                P.op("dve", lambda e, h=h: e.reduce_sum(out=Zs[:, h:h + 1], in_=ex[:, h * 16:(h + 1) * 16], axis=AX.X), reads=[bex], writes=[bsm])
            P.op("dve", lambda e: e.reciprocal(out=rZ, in_=Zs), reads=[bsm], writes=[bsm])
            for h in range(8):
                for p_ in range(2):
                    hp = 2 * h + p_
                    sv = s[:, hp * 128:(hp + 1) * 128]
                    thr = top[:, hp * 16 + 15:hp * 16 + 16]
                    P.op("dve", lambda e, sv=sv, thr=thr: e.tensor_scalar(out=pen, in0=sv, scalar1=thr, scalar2=None, op0=ALU.is_ge),
                         reads=[bs, btop], writes=[bpen])
                    P.op("dve", lambda e: e.tensor_scalar(out=pen, in0=pen, scalar1=-1.0, scalar2=1.0e4, op0=ALU.add, op1=ALU.mult),
                         reads=[bpen], writes=[bpen])
                    if p_ == 0:
                        dv = d1[:, h * 128:(h + 1) * 128]
                        P.op("dve", lambda e, sv=sv, h=h, dv=dv: e.scalar_tensor_tensor(out=dv, in0=sv, scalar=negtau[:, h:h + 1], in1=pen,
                                                                                    op0=ALU.add, op1=ALU.add), reads=[bs, bsm, bpen], writes=[bd1])
                    else:
                        dv = s2m[:, h * 128:(h + 1) * 128]
                        P.op("dve", lambda e, sv=sv, dv=dv: e.tensor_tensor(out=dv, in0=sv, in1=pen, op=ALU.add), reads=[bs, bpen], writes=[bs2m])
            go = gout[tile % 2]
            bgo = bgout[tile % 2]
            for h in range(8):
                P.op("dve", lambda e, h=h: e.tensor_scalar(out=dg[:, h, :], in0=ident_f, scalar1=rZ[:, h:h + 1], scalar2=None, op0=ALU.mult),
                     reads=[bsm, bconst], writes=[bdg])
            for cc in range(8):
                pb0 = (cc % 2) * 4
                for h in range(8):
                    zs = nz[0] % 2
                    ms = nz[0] % 3
                    nz[0] += 1
                    z3v = Z3[zs].rearrange("p (c i) -> p c i", c=16)
                    in0 = bc_ap(s2m[:, h * 128:(h + 1) * 128], [[0, 16], [1, 128]])
                    in1 = bc_ap(d1[:, h * 128 + cc * 16:h * 128 + cc * 16 + 16], [[1, 16], [0, 128]])
                    zeng = "pool" if nz[0] % 4 != 0 else "dve"
                    P.op(zeng, lambda e, z3v=z3v, in0=in0, in1=in1: e.tensor_tensor(out=z3v, in0=in0, in1=in1, op=ALU.add),
                         reads=[bs2m, bd1], writes=[bZ3[zs]])
                    P.op("act", lambda e, zs=zs, h=h: e.activation(out=E3[zs], in_=Z3[zs], func=AF.Exp, bias=tm[:, h:h + 1]),
                         reads=[bZ3[zs], bsm], writes=[bE3[zs]])
                    P.op("dve", lambda e, zs=zs, ms=ms: e.scalar_tensor_tensor(out=M3[ms], in0=Z3[zs], scalar=0.0, in1=E3[zs], op0=ALU.is_ge, op1=ALU.mult),
                         reads=[bZ3[zs], bE3[zs]], writes=[bM3[ms]])
                    for b_ in range(4):
                        P.op("pe", lambda e, b_=b_, h=h, ms=ms, pb0=pb0: e.matmul(ps[pb0 + b_][:], dg[:, h, :], M3[ms][:, b_ * 512:(b_ + 1) * 512],
                                                                                 start=(h == 0), stop=(h == 7)),
                             reads=[bdg, bM3[ms]], writes=[bps[pb0 + b_]], sig=(b_ == 3))
                for b_ in range(4):
                    P.op("act", lambda e, go=go, cc=cc, b_=b_, pb0=pb0: e.copy(out=go[:, cc * 2048 + b_ * 512:cc * 2048 + (b_ + 1) * 512], in_=ps[pb0 + b_][:]),
                         reads=[bps[pb0 + b_]], writes=[bgo])
            P.dma("sp", lambda e, r0=r0, go=go: e.dma_start(out=GS[r0:r0 + 128, :], in_=go), reads=[bgo], writes=[bGS])

        phase_start()
        xnT2 = arena.bf16(32 * TB).rearrange("p (k t) -> p k t", k=32); bxn2 = Buf("xnT2")
        acc = arena.f32(4 * D).rearrange("p (i d) -> p i d", i=4); bacc = [[Buf(f"acc{i}_{c}") for c in range(8)] for i in range(4)]
        tmpa = [arena.f32(512) for _ in range(2)]; btmpa = [Buf("tmpa0"), Buf("tmpa1")]
        ut = [arena.bf16(32 * 256).rearrange("p (k e) -> p k e", k=32) for _ in range(2)]; but = [Buf("ut0"), Buf("ut1")]
        vt = [arena.bf16(2 * D).rearrange("p (c d) -> p c d", c=2) for _ in range(2)]; bvt = [Buf("vt0"), Buf("vt1")]
        gt = [arena.bf16(4 * 256).rearrange("p (i e) -> p i e", i=4) for _ in range(2)]; bgt = [Buf("gt0"), Buf("gt1")]
        ga = [arena.f32(256) for _ in range(2)]; bga = [Buf("ga0"), Buf("ga1")]
        wv = [arena.bf16(256) for _ in range(2)]; bwv = [Buf("w0"), Buf("w1")]
        wT = [arena.bf16(256).rearrange("p (c t) -> p c t", c=2) for _ in range(2)]; bwT = [Buf("wT0"), Buf("wT1")]
        hres = arena.f32(D); bhres = Buf("hres")
        UTv = UT.rearrange("b p (k e) -> b p k e", k=32)
        GSv = GS.rearrange("(i p) e -> p i e", p=128)
        n2 = [0, 0, 0]
        for tb in range(nblk):
            P.dma("sp", lambda e, tb=tb: e.dma_start(out=xnT2, in_=XT[tb].rearrange("p (k t) -> p k t", k=32)), reads=[bXT], writes=[bxn2])

            def ld(eb, tb=tb):
                def f(slot):
                    P.dma("sp", lambda e: e.dma_start(out=ut[slot], in_=UTv[eb]), reads=[bUT], writes=[but[slot]])
                    P.dma("pool", lambda e: e.dma_start(out=vt[slot], in_=peer_v[L][eb * 256:(eb + 1) * 256, :].rearrange("(c p) d -> p c d", p=128)),
                          writes=[bvt[slot]])
                    P.dma("sp", lambda e: e.dma_start(out=gt[slot], in_=GSv[:, tb * 4:(tb + 1) * 4, eb * 256:(eb + 1) * 256]), reads=[bGS], writes=[bgt[slot]])
                return f

            def stage1(eb, ti, pa):
                slot = eb % 2
                for k in range(32):
                    P.op("pe", lambda e, k=k: e.matmul(ps[pa][:, 0:256], xnT2[:, k, ti * 128:(ti + 1) * 128], ut[slot][:, k, :],
                                                       start=(k == 0), stop=(k == 31)),
                         reads=[bxn2, but[slot]], writes=[bps[pa]], sig=(k == 31))
                P.op("act", lambda e: e.activation(out=ga[pa], in_=ps[pa][:, 0:256], func=AF.Gelu), reads=[bps[pa]], writes=[bga[pa]])
                P.op("dve", lambda e: e.tensor_tensor(out=wv[pa], in0=ga[pa], in1=gt[slot][:, ti, :], op=ALU.mult),
                     reads=[bga[pa], bgt[slot]], writes=[bwv[pa]])

            def stage2a(eb, ti, pa):
                pvw = ps[2][:].bitcast(BF16)
                for c in range(2):
                    P.op("pe", lambda e, c=c: e.transpose(pvw[:, c * 128:(c + 1) * 128], wv[pa][:, c * 128:(c + 1) * 128], ident_b),
                         reads=[bwv[pa], bconst], writes=[bps[2]], sig=(c == 1))
                P.op("act", lambda e: e.copy(out=wT[pa], in_=pvw[:, 0:256].rearrange("p (c t) -> p c t", c=2)),
                     reads=[bps[2]], writes=[bwT[pa]])

            def stage2(eb, ti, pa):
                slot = eb % 2
                for cb in range(8):
                    po = 3 + n2[1] % 5
                    n2[1] += 1
                    for c in range(2):
                        P.op("pe", lambda e, c=c, cb=cb, po=po: e.matmul(ps[po][:], wT[pa][:, c, :], vt[slot][:, c, cb * 512:(cb + 1) * 512],
                                                                        start=(c == 0), stop=(c == 1)),
                             reads=[bwT[pa], bvt[slot]], writes=[bps[po]], sig=(c == 1))
                    av = acc[:, ti, cb * 512:(cb + 1) * 512]
                    if eb == 0:
                        eng0 = "act" if cb % 3 == 1 else "dve"
                        if eng0 == "act":
                            P.op("act", lambda e, av=av, po=po: e.copy(out=av, in_=ps[po][:]), reads=[bps[po]], writes=[bacc[ti][cb]])
                        else:
                            P.op("dve", lambda e, av=av, po=po: e.tensor_copy(out=av, in_=ps[po][:]), reads=[bps[po]], writes=[bacc[ti][cb]])
                    elif cb % 3 == 1:
                        ts_ = n2[2] % 2
                        n2[2] += 1
                        P.op("act", lambda e, po=po, ts_=ts_: e.copy(out=tmpa[ts_], in_=ps[po][:]), reads=[bps[po]], writes=[btmpa[ts_]])
                        P.op("pool", lambda e, av=av, ts_=ts_: e.tensor_tensor(out=av, in0=av, in1=tmpa[ts_], op=ALU.add),
                             reads=[btmpa[ts_]], writes=[bacc[ti][cb]])
                    else:
                        P.op("dve", lambda e, av=av, po=po: e.tensor_tensor(out=av, in0=av, in1=ps[po][:], op=ALU.add), reads=[bps[po]], writes=[bacc[ti][cb]])

            nti = 1 if (tb >= NBO and not dbg) else 4
            pairs = [(eb, ti) for eb in range(64) for ti in range(nti)]
            ld(0)(0)
            stage1(0, 0, 0)
            for idx, (eb, ti) in enumerate(pairs):
                if ti == 0 and eb + 1 < 64:
                    ld(eb + 1)((eb + 1) % 2)
                stage2a(eb, ti, idx % 2)
                if idx + 1 < len(pairs):
                    stage1(pairs[idx + 1][0], pairs[idx + 1][1], (idx + 1) % 2)
                stage2(eb, ti, idx % 2)
            for ti in range(nti):
                r0 = tb * TB + ti * 128
                P.dma("sp", lambda e, r0=r0: e.dma_start(out=hres, in_=hin[r0:r0 + 128, :]), writes=[bhres])
                P.op("dve", lambda e, ti=ti: e.tensor_tensor(out=hres, in0=hres, in1=acc[:, ti, :], op=ALU.add), reads=bacc[ti], writes=[bhres])
                P.dma("sp", lambda e, r0=r0: e.dma_start(out=hout[r0:r0 + 128, :], in_=hres), reads=[bhres], writes=[Buf()])

    cur = x
    if "attn" in stages:
        cur = h1
    if "peer0" in stages:
        peer_layer(0, cur, h2, NBQ)
        cur = h2
    if "conv" in stages:
        conv_layer(cur, h3)
        cur = h3
    if "peer1" in stages:
        peer_layer(1, cur, h4, NBO)
        cur = h4

    P.barrier()
    P.emit(nc, st)
    st.close()
    return nc


def host_consts():
    ident = np.eye(128, dtype=np.float32)
    ones = np.ones((128, 128), np.float32)
    rot = np.zeros((128, 128), np.float32)
    for m in range(128):
        half = (m % 64) // 32
        if half == 0:
            rot[m + 32, m] = -1.0
        else:
            rot[m - 32, m] = 1.0
    consts = np.concatenate([ident, ones, rot, np.zeros((128, 256), np.float32)], axis=1)
    pos = np.arange(T)
    row = (pos // 64).astype(np.float32)
    col = (pos % 64).astype(np.float32)
    inv_freq = (10000.0 ** (-np.arange(0, 64, 2, dtype=np.float32) / 64)).astype(np.float32)
    ang = np.zeros((128, T), np.float32)
    for d in range(128):
        axis = d // 64
        f = d % 32
        ang[d] = (row if axis == 0 else col) * inv_freq[f]
    rope = np.stack([np.cos(ang), np.sin(ang)]).astype(np.float32)
    return consts, rope


def make_in_maps(inputs, ncores, stages):
    consts, rope = host_consts()
    maps = []
    gains = np.stack([inputs["attn_q_gain"][0], inputs["attn_k_gain"][0]], axis=1).astype(np.float32)
    for c in range(ncores):
        b, half = (c // 2, c % 2) if ncores == 8 else (c, 0)
        xs = inputs["x"][b]
        rp = rope
        if half == 1:
            xs = xs[::-1]
            rp = rope[:, :, ::-1]
        m = {"x": np.ascontiguousarray(xs), "mixer_norm_g": inputs["mixer_norm_g"],
             "ffn_norm_g": inputs["ffn_norm_g"], "consts": consts, "rope_cs": np.ascontiguousarray(rp), "qk_gain": gains}
        if "attn" in stages:
            m["attn_w_qkv"] = inputs["attn_w_qkv"][0]
            m["attn_w_o"] = inputs["attn_w_o"][0]
        if "conv" in stages:
            m["conv_w_in"] = inputs["conv_w_in"][0]
            m["conv_w_out"] = inputs["conv_w_out"][0]
            cw = inputs["conv_w"][0]
            if half == 1:
                cw = cw[::-1]
            wb = np.concatenate([cw, inputs["conv_b"]], axis=0)
            m["conv_wb"] = np.ascontiguousarray(wb.T.reshape(32, 128, 4))
        if "peer0" in stages or "peer1" in stages:
            m["peer_w_query"] = inputs["peer_w_query"]
            m["peer_sub_keys"] = np.ascontiguousarray(inputs["peer_sub_keys"].reshape(2, 16, 128, 128))
            m["peer_u"] = inputs["peer_u"]
            m["peer_v"] = inputs["peer_v"]
        maps.append(m)
    return maps


def kernel(**inputs):
    inputs = {k: np.asarray(v) for k, v in inputs.items()}
    stages = ("attn", "peer0", "conv", "peer1")
    nc = build(stages)
    maps = make_in_maps(inputs, NCORES, stages)
    res = run_bass_kernel_spmd(nc, maps, core_ids=list(range(NCORES)))
    out = np.empty((4, T, D), np.float32)
    for c in range(NCORES):
        b, half = c // 2, c % 2
        y = res.results[c]["y"]
        if half == 0:
            out[b, :TOWN] = y
        else:
            out[b, TOWN:] = y[::-1]
    return out
```
